# Optimizing a Trainium2 kernel written in Bass

```python
import jax
import jax.numpy as jnp
from jax import lax
import numpy as np

D_MODEL = 1024
BATCH = 8
SEQ = 4096
DEPTH = 4

MEM_LEN = 256
D_FF = 2816
FFN_RES = 0.5
EPS = 1e-6
MIX_W = D_MODEL
CONV_W = D_MODEL // 4
CONV_LEN = 3
HG_HEAD_DIM = 64
HG_W = 3 * D_MODEL // 8
HG_HEADS = HG_W // HG_HEAD_DIM
HG_EXP_CLIP = 80.0
RW_HEAD_DIM = 64
RW_W = MIX_W - CONV_W - HG_W
RW_HEADS = RW_W // RW_HEAD_DIM
RW_DECAY_RANK = 64
RW_A_RANK = 64
RW_G_RANK = 128
RW_PROJ = 3 * RW_W + RW_DECAY_RANK + RW_A_RANK + RW_G_RANK
RW_SPLITS = (RW_W, 2 * RW_W, 3 * RW_W, 3 * RW_W + RW_DECAY_RANK, 3 * RW_W + RW_DECAY_RANK + RW_A_RANK)
RW_DECAY_SCALE = 0.606531
RW_GN_EPS = 64e-5
CONV_PROJ = 3 * CONV_W
HG_PROJ = 4 * HG_W
IN_W = CONV_PROJ + HG_PROJ + RW_PROJ
CHUNK = 64
XA_HEADS = 4
XA_HEAD_DIM = D_MODEL // XA_HEADS

kernel_name = 'hybrid_conv_hgrn2_rwkv7_macaron_memxattn'


def rmsnorm(x, g):
    xf = x.astype(jnp.float32)
    xf = xf * lax.rsqrt(jnp.mean(xf * xf, axis=-1, keepdims=True) + EPS)
    return (xf * g.astype(jnp.float32)).astype(x.dtype)


def swiglu_ffn(x, w_in, w_out):
    gate, up = jnp.split(x @ w_in, 2, axis=-1)
    return (jax.nn.silu(gate) * up) @ w_out


def short_conv_mixer(p, conv_w, conv_b):
    b_gate, c_gate, x_in = jnp.split(p, 3, axis=-1)
    z = c_gate * x_in
    zc = lax.conv_general_dilated(
        z, conv_w[:, None, :], window_strides=(1,), padding=[(CONV_LEN - 1, 0)],
        dimension_numbers=('NWC', 'WIO', 'NWC'), feature_group_count=CONV_W)
    return b_gate * (zc + conv_b)


def hgrn2_mixer(p, lb, norm_g):
    bsz, seqlen, _ = p.shape
    n_chunks = seqlen // CHUNK
    f32 = jnp.float32
    q, f_logit, i_in, g = jnp.split(p.astype(f32), 4, axis=-1)
    log_f = jax.nn.log_sigmoid(f_logit) + jnp.log1p(lb * jnp.exp(jnp.minimum(-f_logit, HG_EXP_CLIP)))
    k = (1.0 - lb) * jax.nn.sigmoid(-f_logit)
    q = jax.nn.silu(q)

    def to_chunks(t):
        return t.reshape(bsz, n_chunks, CHUNK, HG_HEADS, -1).transpose(1, 0, 3, 2, 4)

    causal = jnp.tril(jnp.ones((CHUNK, CHUNK), dtype=bool))[:, :, None]

    def chunk_step(state, inp):
        qc, kc, vc, lfc = inp
        b = jnp.cumsum(lfc, axis=2)
        o_inter = jnp.einsum('bhtd,bhde->bhte', qc * jnp.exp(b), state)
        diff = b[:, :, :, None, :] - b[:, :, None, :, :]
        dec = jnp.where(causal, jnp.exp(jnp.minimum(diff, 0.0)), 0.0)
        att = jnp.einsum('bhtd,bhsd,bhtsd->bhts', qc, kc, dec)
        o = o_inter + jnp.einsum('bhts,bhse->bhte', att, vc)
        b_last = b[:, :, -1:, :]
        state = (jnp.exp(b_last[:, :, 0, :])[..., None] * state
                 + jnp.einsum('bhsd,bhse->bhde', kc * jnp.exp(b_last - b), vc))
        return state, o

    state0 = jnp.zeros((bsz, HG_HEADS, HG_HEAD_DIM, HG_HEAD_DIM), f32)
    _, o = lax.scan(chunk_step, state0,
                    (to_chunks(q), to_chunks(k), to_chunks(i_in), to_chunks(log_f)))
    o = o.transpose(1, 0, 3, 2, 4).reshape(bsz, seqlen, HG_HEADS, HG_HEAD_DIM)
    o = o * lax.rsqrt(jnp.mean(o * o, axis=-1, keepdims=True) + EPS)
    o = o.reshape(bsz, seqlen, HG_W) * norm_g.astype(f32) * jax.nn.silu(g)
    return o.astype(p.dtype)


def rwkv7_mixer(p, mu, w0, w2, a0, a2, g2, k_k, k_a, r_k, ln_w, ln_b):
    bsz, seqlen, _ = p.shape
    f32 = jnp.float32
    cast = lambda t: t.astype(f32)
    pf = cast(p)
    p_prev = jnp.pad(pf, ((0, 0), (1, 0), (0, 0)))[:, :-1]
    pf = pf + (p_prev - pf) * cast(mu)
    r, k, v, wd, ad, gd = jnp.split(pf, RW_SPLITS, axis=-1)
    log_w = -RW_DECAY_SCALE * jax.nn.sigmoid(cast(w0) + jnp.tanh(wd) @ cast(w2))
    a = jax.nn.sigmoid(cast(a0) + ad @ cast(a2))
    g = jax.nn.sigmoid(gd) @ cast(g2)
    heads = lambda t: t.reshape(bsz, seqlen, RW_HEADS, RW_HEAD_DIM)
    kk = heads(k * cast(k_k))
    kk = kk / jnp.maximum(jnp.sqrt(jnp.sum(kk * kk, axis=-1, keepdims=True)), 1e-12)
    k = k * (1.0 + (a - 1.0) * cast(k_a))
    rh, kh, vh = heads(r), heads(k), heads(v)
    a_vec = -kk
    b_vec = kk * heads(a)
    tm = lambda t: jnp.swapaxes(t, 0, 1)

    def step(state, inp):
        r_t, w_t, k_t, v_t, a_t, b_t = inp
        sa = jnp.einsum('bhij,bhj->bhi', state, a_t)
        state = (state * w_t[:, :, None, :] + sa[..., None] * b_t[:, :, None, :]
                 + v_t[..., None] * k_t[:, :, None, :])
        return state, jnp.einsum('bhij,bhj->bhi', state, r_t)

    state0 = jnp.zeros((bsz, RW_HEADS, RW_HEAD_DIM, RW_HEAD_DIM), f32)
    _, y = lax.scan(step, state0, (tm(rh), tm(heads(jnp.exp(log_w))), tm(kh), tm(vh), tm(a_vec), tm(b_vec)))
    y = jnp.swapaxes(y, 0, 1)
    mean = jnp.mean(y, axis=-1, keepdims=True)
    var = jnp.mean(jnp.square(y - mean), axis=-1, keepdims=True)
    y = ((y - mean) * lax.rsqrt(var + RW_GN_EPS)).reshape(bsz, seqlen, RW_W) * cast(ln_w) + cast(ln_b)
    bonus = jnp.sum(rh * kh * cast(r_k), axis=-1, keepdims=True) * vh
    out = (y + bonus.reshape(bsz, seqlen, RW_W)) * g
    return out.astype(p.dtype)


def memory_cross_attention(h_n, mem_n, wq, wkv, wo):
    bsz, seqlen, _ = h_n.shape
    q = (h_n @ wq).reshape(bsz, seqlen, XA_HEADS, XA_HEAD_DIM)
    mk, mv = jnp.split(mem_n @ wkv, 2, axis=-1)
    mk = mk.reshape(bsz, -1, XA_HEADS, XA_HEAD_DIM)
    mv = mv.reshape(bsz, -1, XA_HEADS, XA_HEAD_DIM)
    s = jnp.einsum('bshd,bmhd->bhsm', q, mk).astype(jnp.float32) * (XA_HEAD_DIM ** -0.5)
    pr = jax.nn.softmax(s, axis=-1).astype(mv.dtype)
    o = jnp.einsum('bhsm,bmhd->bshd', pr, mv).reshape(bsz, seqlen, D_MODEL)
    return o @ wo


def setup_inputs(seed: int = 0) -> dict:
    key = jax.random.key(seed)
    ks = iter(jax.random.split(key, 40))
    nrm = lambda shape, scale: jax.random.normal(next(ks), shape, jnp.float32) * scale
    gain = lambda shape: 1.0 + nrm(shape, 0.02)
    L, D = DEPTH, D_MODEL
    return {
        'x': nrm((BATCH, SEQ, D), 1.0),
        'mem': nrm((BATCH, MEM_LEN, D), 1.0),
        'ffn1_norm': gain((L, D)),
        'ffn1_w_in': nrm((L, D, 2 * D_FF), D ** -0.5),
        'ffn1_w_out': nrm((L, D_FF, D), D_FF ** -0.5),
        'mix_norm': gain((L, D)),
        'w_mix_in': nrm((L, D, IN_W), D ** -0.5),
        'w_mix_out': nrm((L, MIX_W, D), MIX_W ** -0.5),
        'conv_w': nrm((L, CONV_LEN, CONV_W), CONV_LEN ** -0.5),
        'conv_b': nrm((L, CONV_W), 0.02),
        'hgrn_lb_logits': nrm((L, HG_W), 0.1),
        'hgrn_norm': gain((L, HG_W)),
        'rwkv_mu': jax.random.uniform(next(ks), (L, RW_PROJ), jnp.float32),
        'rwkv_w0': nrm((L, RW_W), 0.5),
        'rwkv_w2': nrm((L, RW_DECAY_RANK, RW_W), RW_DECAY_RANK ** -0.5),
        'rwkv_a0': nrm((L, RW_W), 0.1),
        'rwkv_a2': nrm((L, RW_A_RANK, RW_W), RW_A_RANK ** -0.5),
        'rwkv_g2': nrm((L, RW_G_RANK, RW_W), RW_G_RANK ** -0.5),
        'rwkv_k_k': 0.85 + nrm((L, RW_W), 0.05),
        'rwkv_k_a': 1.0 + nrm((L, RW_W), 0.05),
        'rwkv_r_k': nrm((L, RW_HEADS, RW_HEAD_DIM), 0.1),
        'rwkv_ln_w': gain((L, RW_W)),
        'rwkv_ln_b': nrm((L, RW_W), 0.02),
        'xattn_norm': gain((L, D)),
        'mem_norm': gain((L, D)),
        'xattn_wq': nrm((L, D, D), D ** -0.5),
        'xattn_wkv': nrm((L, D, 2 * D), D ** -0.5),
        'xattn_wo': nrm((L, D, D), D ** -0.5),
        'ffn2_norm': gain((L, D)),
        'ffn2_w_in': nrm((L, D, 2 * D_FF), D ** -0.5),
        'ffn2_w_out': nrm((L, D_FF, D), D_FF ** -0.5),
        'final_norm': gain((D,)),
    }


def reference(x, mem, ffn1_norm, ffn1_w_in, ffn1_w_out, mix_norm, w_mix_in, w_mix_out,
              conv_w, conv_b, hgrn_lb_logits, hgrn_norm, rwkv_mu, rwkv_w0, rwkv_w2, rwkv_a0,
              rwkv_a2, rwkv_g2, rwkv_k_k, rwkv_k_a, rwkv_r_k, rwkv_ln_w, rwkv_ln_b,
              xattn_norm, mem_norm, xattn_wq, xattn_wkv, xattn_wo,
              ffn2_norm, ffn2_w_in, ffn2_w_out, final_norm):
    lb_sm = jax.nn.softmax(hgrn_lb_logits.astype(jnp.float32), axis=0)
    lb_all = jnp.maximum(jnp.cumsum(lb_sm, axis=0) - lb_sm[0], 0.0)
    h = x
    for l in range(DEPTH):
        h = h + FFN_RES * swiglu_ffn(rmsnorm(h, ffn1_norm[l]), ffn1_w_in[l], ffn1_w_out[l])
        u = rmsnorm(h, mix_norm[l])
        p = u @ w_mix_in[l]
        p_conv, p_hg, p_rw = jnp.split(p, (CONV_PROJ, CONV_PROJ + HG_PROJ), axis=-1)
        y = jnp.concatenate([
            short_conv_mixer(p_conv, conv_w[l], conv_b[l]),
            hgrn2_mixer(p_hg, lb_all[l], hgrn_norm[l]),
            rwkv7_mixer(p_rw, rwkv_mu[l], rwkv_w0[l], rwkv_w2[l], rwkv_a0[l], rwkv_a2[l],
                        rwkv_g2[l], rwkv_k_k[l], rwkv_k_a[l], rwkv_r_k[l], rwkv_ln_w[l], rwkv_ln_b[l]),
        ], axis=-1)
        h = h + y @ w_mix_out[l]
        h = h + memory_cross_attention(rmsnorm(h, xattn_norm[l]), rmsnorm(mem, mem_norm[l]),
                                       xattn_wq[l], xattn_wkv[l], xattn_wo[l])
        h = h + FFN_RES * swiglu_ffn(rmsnorm(h, ffn2_norm[l]), ffn2_w_in[l], ffn2_w_out[l])
    return rmsnorm(h, final_norm)
```

```python
from concourse.bass_utils import run_bass_kernel_spmd
import numpy as np
from contextlib import ExitStack
import concourse.bass as bass
import concourse.mybir as mybir

F32 = mybir.dt.float32
BF16 = mybir.dt.bfloat16
AF = mybir.ActivationFunctionType
ALU = mybir.AluOpType
AX = mybir.AxisListType

COMPUTE = ("pe", "act", "dve", "pool")
NPH = 4
STRICT = True


class Buf:
    __slots__ = ("name", "writer", "readers", "sem", "cnt", "excl")

    def __init__(self, name, excl=False):
        self.name = name
        self.excl = excl
        self.writer = None
        self.readers = []
        self.sem = None
        self.cnt = 0


class V:
    __slots__ = ("ap", "bufs")

    def __init__(self, ap, bufs):
        self.ap = ap
        self.bufs = bufs

    def __getitem__(self, key):
        return V(self.ap[key], self.bufs)

    def re(self, s, **kw):
        return V(self.ap.rearrange(s, **kw), self.bufs)

    def bc(self, shape):
        return V(self.ap.to_broadcast(shape), self.bufs)


class Op:
    __slots__ = ("eng", "fn", "deps", "dwaits", "signal", "signum", "dma", "tok", "idx", "ph")

    def __init__(self, eng, fn, dma=False):
        self.eng = eng
        self.fn = fn
        self.deps = []
        self.dwaits = {}
        self.signal = False
        self.signum = 0
        self.dma = dma
        self.tok = None
        self.idx = 0
        self.ph = 0


class Prog:
    def __init__(self, nc):
        self.nc = nc
        self.es = ExitStack()
        self.ops = {e: [] for e in ("pe", "act", "dve", "pool", "sp")}
        self.sems = {}
        self.nbuf = 0
        self.final_waits = []
        self.phase = 0
        self.section = ""
        self.seclog = None

    def sbuf(self, name, shape, dtype, nsub=1):
        t = self.es.enter_context(self.nc.sbuf_tensor(name, list(shape), dtype))
        bufs = [Buf(f"{name}.{i}") for i in range(nsub)]
        return t, bufs

    def tile(self, name, shape, dtype):
        t, bufs = self.sbuf(name, shape, dtype)
        return V(t[tuple(slice(None) for _ in shape)], bufs)

    def psum(self, name, shape, dtype=F32):
        t = self.es.enter_context(self.nc.psum_tensor(name, list(shape), dtype))
        return V(t[tuple(slice(None) for _ in shape)], [Buf(name, excl=True)])

    def dram(self, name, shape, dtype, kind="Internal"):
        t = self.nc.dram_tensor(name, list(shape), dtype, kind=kind)
        return V(t.ap(), [Buf(name)])

    def newsem(self, name):
        s = self.es.enter_context(self.nc.semaphore(name))
        return s

    def _dep(self, B, A, kind):
        if A is None or A is B:
            return
        if A.dma:
            sem, val = A.tok
            if B.dwaits.get(sem, (None, 0))[1] < val:
                B.dwaits[sem] = (sem, val)
            return
        if A.eng == B.eng:
            if B.eng == "pe" or (kind != "RAW" and not STRICT):
                return
        A.signal = True
        B.deps.append(A)

    def _track(self, op, reads, writes):
        for b in reads:
            self._dep(op, b.writer, "RAW")
            if b.excl:
                for r in b.readers:
                    if r.eng != op.eng:
                        self._dep(op, r, "RAR")
        for b in writes:
            self._dep(op, b.writer, "WAW")
            for r in b.readers:
                self._dep(op, r, "WAR")
        for b in reads:
            b.readers.append(op)
        for b in writes:
            b.writer = op
            b.readers = []

    @staticmethod
    def _bufs(vs):
        out = []
        for v in vs:
            if isinstance(v, V):
                for b in v.bufs:
                    if b not in out:
                        out.append(b)
        return out

    def emit(self, eng, fn, reads, writes):
        op = Op(eng, fn)
        op.ph = self.phase % NPH
        if self.seclog is not None:
            self.seclog[eng].append(self.section)
        self._track(op, self._bufs(reads), self._bufs(writes))
        op.idx = len(self.ops[eng])
        self.ops[eng].append(op)
        return op

    def dma(self, eng, out, in_, **kw):
        op = Op(eng, None, dma=True)
        rb = self._bufs([in_])
        wb = self._bufs([out])
        self._track(op, rb, wb)
        owner = wb[0] if wb else rb[0]
        if owner.sem is None:
            owner.sem = self.newsem("d_" + owner.name.replace(".", "_"))
        owner.cnt += 16
        op.tok = (owner.sem, owner.cnt)
        oap = out.ap if isinstance(out, V) else out
        iap = in_.ap if isinstance(in_, V) else in_
        sem = owner.sem
        op.fn = lambda e: e.dma_start(out=oap, in_=iap, **kw).then_inc(sem, 16)
        op.idx = len(self.ops[eng])
        self.ops[eng].append(op)
        return op

    def wait_dma_final(self, eng, v):
        for b in v.bufs:
            if b.writer is not None and b.writer.dma:
                self.final_waits.append((eng, b.writer.tok))

    @staticmethod
    def _a(x):
        return x.ap if isinstance(x, V) else x

    def matmul(self, out, lhsT, rhs, start=True, stop=True, **kw):
        o, l, r = self._a(out), self._a(lhsT), self._a(rhs)
        op = self.emit("pe", lambda e: e.matmul(o, l, r, start=start, stop=stop, **kw),
                       [lhsT, rhs], [out])
        self._pe_rowgroup(op, l)
        if self.seclog is not None and l.dtype == F32:
            self.seclog["pe"].append(self.section)
        return op

    def _pe_rowgroup(self, op, lhs_ap):
        rg = (lhs_ap.base_partition(), min(128, ((lhs_ap.shape[0] + 31) // 32) * 32))
        prev = getattr(self, "_last_pe", None)
        if prev is not None and prev[1] != rg:
            prev[0].signal = True
            op.deps.append(prev[0])
        self._last_pe = (op, rg)

    def transpose(self, out, in_, ident):
        o, i, d = self._a(out), self._a(in_), self._a(ident)
        op = self.emit("pe", lambda e: e.transpose(o, i, d), [in_, ident], [out])
        self._pe_rowgroup(op, i)
        return op

    def act(self, out, in_, func, bias=None, scale=1.0, accum_out=None, eng="act"):
        o, i = self._a(out), self._a(in_)
        b = self._a(bias) if bias is not None else None
        s = self._a(scale)
        acc = self._a(accum_out) if accum_out is not None else None
        kw = {}
        if b is not None:
            kw["bias"] = b
        if acc is not None:
            kw["accum_out"] = acc
        return self.emit("act", lambda e: e.activation(o, i, func, scale=s, **kw),
                         [in_, bias, scale], [out, accum_out])

    def tt(self, eng, out, in0, in1, op):
        o, a, b = self._a(out), self._a(in0), self._a(in1)
        return self.emit(eng, lambda e: e.tensor_tensor(o, a, b, op), [in0, in1], [out])

    def ts(self, eng, out, in0, s1, op0, s2=None, op1=None, accum_out=None):
        o, a = self._a(out), self._a(in0)
        x1, x2 = self._a(s1), self._a(s2)
        acc = self._a(accum_out) if accum_out is not None else None
        kw = {}
        if op1 is not None:
            kw["op1"] = op1
        if acc is not None:
            kw["accum_out"] = acc
        return self.emit(eng, lambda e: e.tensor_scalar(o, a, x1, x2, op0, **kw),
                         [in0, s1, s2], [out, accum_out])

    def stt(self, eng, out, in0, scalar, in1, op0, op1):
        o, a, s, b = self._a(out), self._a(in0), self._a(scalar), self._a(in1)
        return self.emit(eng, lambda e: e.scalar_tensor_tensor(o, a, s, b, op0, op1),
                         [in0, scalar, in1], [out])

    def copy(self, eng, out, in_):
        o, i = self._a(out), self._a(in_)
        if eng == "act":
            return self.emit("act", lambda e: e.copy(o, i), [in_], [out])
        return self.emit(eng, lambda e: e.tensor_copy(o, i), [in_], [out])

    def memset(self, eng, out, val):
        o = self._a(out)
        return self.emit(eng, lambda e: e.memset(o, val), [], [out])

    def scan(self, out, d0, d1, initial, op0, op1):
        o, a, b, i = self._a(out), self._a(d0), self._a(d1), self._a(initial)
        return self.emit("dve", lambda e: e.tensor_tensor_scan(o, a, b, i, op0, op1),
                         [d0, d1, initial], [out])

    def recip(self, out, in_):
        o, i = self._a(out), self._a(in_)
        return self.emit("dve", lambda e: e.reciprocal(o, i), [in_], [out])

    def generic(self, eng, fn, reads, writes):
        return self.emit(eng, fn, reads, writes)

    def finish(self):
        nc = self.nc
        esem = {(e, k): self.newsem(f"s_{e}{k}") for e in COMPUTE for k in range(NPH)}
        for e in COMPUTE:
            n = [0] * NPH
            for op in self.ops[e]:
                if op.signal and not op.dma:
                    n[op.ph] += 1
                    op.signum = n[op.ph]
        engobj = {"pe": "tensor", "act": "scalar", "dve": "vector", "pool": "gpsimd", "sp": "sync"}
        stats = {}
        with nc.Block() as block:
            for ename in ("sp", "pool", "act", "dve", "pe"):
                ops = self.ops[ename]
                finals = [t for (e, t) in self.final_waits if e == ename]
                if not ops and not finals:
                    continue

                def body(eng, ops=ops, ename=ename, finals=finals):
                    seen = {}
                    nw = 0
                    for op in ops:
                        need = {}
                        for A in op.deps:
                            k = esem[(A.eng, A.ph)]
                            if need.get(k, 0) < A.signum:
                                need[k] = A.signum
                        for sem, val in op.dwaits.values():
                            if need.get(sem, 0) < val:
                                need[sem] = val
                        for k, val in need.items():
                            if seen.get(k, 0) < val:
                                eng.wait_ge(k, val)
                                seen[k] = val
                                nw += 1
                        ins = op.fn(eng)
                        if op.signal and not op.dma:
                            ins.then_inc(esem[(ename, op.ph)], 1)
                    for sem, val in finals:
                        eng.wait_ge(sem, val)
                    stats[ename] = (len(ops), nw)

                getattr(block, engobj[ename])(body)
        self.stats = stats
        return stats

    def close(self):
        self.es.close()


D = 1024; KC = 8; DFF = 2816; FC = 22; T = 512; MEM = 256
DDT = BF16
EPS = 1e-6
L_CONV0, L_HG0, L_RW0 = 0, 6, 18
NHG = 6; NRW = 6

PL = {}
_o = 0
for _n, _w in [("ffn1_norm", 8), ("mix_norm", 8), ("xattn_norm", 8), ("ffn2_norm", 8), ("mem_norm", 8),
               ("cw0", 2), ("cw1", 2), ("cw2", 2), ("cb", 2), ("lbl", 3), ("hgn", 3), ("mu", 11),
               ("w0", 3), ("a0", 3), ("k_k", 3), ("k_a", 3), ("r_k", 3), ("ln_w", 3), ("ln_b", 3),
               ("omu", 11), ("oka", 3), ("lb", 3), ("olb", 3)]:
    PL[_n] = (_o, _w); _o += _w
NPL = _o
CL = {}
_o = 0
for _n, _w in [("ident", 128), ("blk64", 128), ("ones", 128), ("m32i", 32), ("m64is", 128), ("m64s", 64),
               ("m64l", 64), ("r32", 512), ("r64", 512), ("id64", 64)]:
    CL[_n] = (_o, _w); _o += _w
NCL = _o


def make_consts():
    c = np.zeros((128, NCL), np.float32)
    def put(n, a):
        o, w = CL[n]; c[:a.shape[0], o:o + w] = a
    put("ident", np.eye(128))
    b = np.zeros((128, 128)); b[:64, :64] = 1; b[64:, 64:] = 1
    put("blk64", b)
    put("ones", np.ones((128, 128)))
    s32 = np.arange(32)
    put("m32i", (s32[:, None] <= s32[None, :]).astype(np.float32))
    s64 = np.arange(64)
    strict = (s64[:, None] < s64[None, :]).astype(np.float32)
    incl = (s64[:, None] <= s64[None, :]).astype(np.float32)
    put("m64is", np.concatenate([strict, incl], 1))
    put("m64s", strict)
    put("m64l", strict.T.copy())
    r = np.ones((128, 512)); r[:, ::32] = 0; put("r32", r)
    r = np.ones((128, 512)); r[:, ::64] = 0; put("r64", r)
    put("id64", np.eye(64))
    return c


def fm(v):
    v = np.asarray(v, np.float32).reshape(-1)
    return np.ascontiguousarray(v.reshape(-1, 128).T)


def build(S, L):
    NT = S // T
    nc = bass.Bass("TRN2", target_bir_lowering=False)
    P = Prog(nc)
    if SECLOG is not None:
        P.seclog = {e: [] for e in ("pe", "act", "dve", "pool", "sp")}
    ein = lambda n, sh: nc.dram_tensor(n, list(sh), F32, kind="ExternalInput").ap()
    x_d = ein("x", [S, D]); mem_d = ein("mem", [MEM, D])
    par_d = ein("params", [128, L * NPL + 8]); con_d = ein("consts", [128, NCL])
    wnames = [("ffn1_w_in", D, 2 * DFF), ("ffn1_w_out", DFF, D), ("w_mix_in", D, 3712), ("w_mix_out", D, D),
              ("xattn_wq", D, D), ("xattn_wkv", D, 2 * D), ("xattn_wo", D, D),
              ("ffn2_w_in", D, 2 * DFF), ("ffn2_w_out", DFF, D)]
    wf = {n: ein(n, [L, a, b]) for n, a, b in wnames}
    w2_d = ein("rwkv_w2", [L, 64, 384]); a2_d = ein("rwkv_a2", [L, 64, 384]); g2_d = ein("rwkv_g2", [L, 128, 384])
    out_d = P.dram("out", [S, D], F32, kind="ExternalOutput")
    wb = {}
    for n, a, b in wnames:
        t = nc.dram_tensor(n + "_b", [L, a, b], BF16, kind="Internal").ap()
        wb[n] = [V(t[l], [Buf(f"w_{n}{l}")]) for l in range(L)]
    mk_d = [P.dram(f"mk_d{l}", [128, 8 * 256], BF16) for l in range(L)]
    mv_d = [P.dram(f"mv_d{l}", [128, 2 * 1024], BF16) for l in range(L)]

    par = P.tile("par", [128, L * NPL + 8], F32)
    con = P.tile("con", [128, NCL], F32)
    cb = P.tile("cb", [128, 128 * 3], BF16)
    ident_b = cb[:, 0:128]; blk_b = cb[:, 128:256]; ones_b = cb[:, 256:384]
    ident_f = con[:, CL["ident"][0]:CL["ident"][0] + 128]
    def cst(n, rows=128):
        o, w = CL[n]; return con[0:rows, o:o + w]
    def pc(l, n, i=None):
        o, w = PL[n]; o += l * NPL
        return par[:, o:o + w] if i is None else par[:, o + i:o + i + 1]
    wa2 = P.tile("wa2", [128, L, 384], BF16)
    g2b = P.tile("g2b", [128, L, 384], BF16)
    def mtile(name, shape, dt, n):
        t, bufs = P.sbuf(name, shape, dt, nsub=n)
        return V(t[tuple(slice(None) for _ in shape)], bufs)
    def chv(v, c):
        return V(v.ap[:, c, :], [v.bufs[c]])
    hT = mtile("hT", [128, KC, T], F32, KC)
    xn = P.tile("xn", [128, KC, T], BF16)
    ARENA = 97 * 1024
    arena_t, _ = P.sbuf("arena", [128, ARENA // 2], BF16)
    gens = {}
    def carve(gen, name, shape, dt):
        g = gens.setdefault(gen, {"off": 0, "vs": []})
        esz = 4 if dt == F32 else 2
        n = 1
        for d in shape[1:]:
            n *= d
        nbytes = (n * esz + 63) // 64 * 64
        o = g["off"]; g["off"] += nbytes
        assert g["off"] <= ARENA, (gen, name, g["off"])
        ap = arena_t[0:shape[0], o // 2:(o + n * esz) // 2]
        if dt == F32:
            ap = ap.bitcast(F32)
        if len(shape) > 2:
            names = " ".join(f"d{i}" for i in range(1, len(shape)))
            ap = ap.rearrange(f"p ({names}) -> p {names}", **{f"d{i}": shape[i] for i in range(1, len(shape) - 1)})
        v = V(ap, [Buf(name)])
        g["vs"].append(v)
        return v
    def handoff(old, new):
        ops = []
        for v in gens[old]["vs"]:
            for b in v.bufs:
                if b.writer is not None:
                    ops.append(b.writer)
                ops.extend(b.readers)
        for v in gens[new]["vs"]:
            for b in v.bufs:
                b.readers.extend(ops)
    def enter(gen):
        for g_ in list(gens):
            if g_ != gen:
                handoff(g_, gen)
    hid = carve("ffn", "hid", [128, FC, T], BF16)
    mkT = carve("xa", "mkT", [128, 8, 256], BF16)
    mvt = carve("xa", "mvt", [128, 2, 1024], BF16)
    exs = [carve("xa", f"exs{i}", [128, 2, T], BF16) for i in range(4)]
    rden = [carve("xa", f"rden{i}", [128, T], F32) for i in range(4)]
    RING = 5
    ring = [P.tile(f"ring{i}", [128, 2048], BF16) for i in range(RING)]
    rstd = P.tile("rstd", [128, T], F32)
    xs = P.tile("xs", [128, 4, D], F32)
    sg = [P.tile(f"sg{i}", [128, T], F32) for i in range(2)]
    ymix = P.tile("ymix", [128, KC, T], BF16)
    qT = mtile("qT", [128, KC, T], BF16, KC)
    Sh = [P.tile(f"Sh{l}", [128, 3, 64], F32) for l in range(L)]
    Sr = [P.tile(f"Sr{l}", [128, 3, 64], F32) for l in range(L)]
    Shb = P.tile("Shb", [128, 3, 64], BF16)
    Srb = P.tile("Srb", [128, 3, 64], BF16)
    cz = [P.tile(f"cz{l}", [128, 2, 2], F32) for l in range(L)]
    crw = [P.tile(f"crw{l}", [128, 11], F32) for l in range(L)]
    PB = [P.psum(f"pb{i}", [128, 512]) for i in range(8)]
    st = {"ring": 0, "pj": 0, "ms": 0}
    def pj():
        st["pj"] = (st["pj"] + 1) % 3
        return PB[st["pj"]]
    def ms():
        st["ms"] = (st["ms"] + 1) % 4
        return PB[3 + st["ms"]]
    def bfview(bank):
        return V(bank.ap.bitcast(BF16), bank.bufs)

    P.dma("sp", par, par_d)
    P.dma("sp", con, con_d)
    for l in range(L):
        P.dma("pool", wb["xattn_wkv"][l], wf["xattn_wkv"][l])
    for l in range(L):
        for n in ("ffn1_w_in", "ffn1_w_out", "w_mix_in", "w_mix_out", "xattn_wq", "xattn_wo", "ffn2_w_in", "ffn2_w_out"):
            P.dma("pool", wb[n][l], wf[n][l])
    P.dma("pool", wa2[0:64], w2_d.rearrange("l k c -> k l c"))
    P.dma("pool", wa2[64:128], a2_d.rearrange("l k c -> k l c"))
    P.dma("pool", g2b, g2_d.rearrange("l k c -> k l c"))
    P.copy("act", cb[:, 0:128], ident_f)
    P.copy("act", cb[:, 128:256], cst("blk64"))
    P.copy("act", cb[:, 256:384], cst("ones"))
    for l in range(L):
        P.memset("pool", Sh[l], 0.0); P.memset("pool", Sr[l], 0.0)
        P.memset("pool", cz[l], 0.0); P.memset("pool", crw[l], 0.0)
    for l in range(L):
        P.ts("dve", pc(l, "omu"), pc(l, "mu"), -1.0, ALU.mult, 1.0, ALU.add)
        P.ts("dve", pc(l, "oka"), pc(l, "k_a"), -1.0, ALU.mult, 1.0, ALU.add)
    ex_l = P.tile("ex_l", [128, L, 3], F32)
    sm = P.tile("sm", [128, 3], F32)
    for l in range(L):
        P.act(ex_l[:, l, :], pc(l, "lbl"), AF.Exp)
    P.copy("dve", sm, ex_l[:, 0, :])
    for l in range(1, L):
        P.tt("dve", sm, sm, ex_l[:, l, :], ALU.add)
    P.recip(sm, sm)
    for l in range(L):
        P.tt("dve", ex_l[:, l, :], ex_l[:, l, :], sm, ALU.mult)
    P.memset("dve", pc(0, "lb"), 0.0)
    for l in range(1, L):
        P.tt("dve", pc(l, "lb"), pc(l - 1, "lb"), ex_l[:, l, :], ALU.add)
    for l in range(L):
        P.ts("dve", pc(l, "lb"), pc(l, "lb"), 0.0, ALU.max)
        P.ts("dve", pc(l, "olb"), pc(l, "lb"), -1.0, ALU.mult, 1.0, ALU.add)

    def ring_slot():
        st["ring"] = (st["ring"] + 1) % RING
        return ring[st["ring"]]

    def proj_cols(xin, W, groups, handler, ntok=T):
        for (c0, ncols) in groups:
            slot = ring_slot()
            sv = slot[:, 0:KC * ncols].re("p (k n) -> p k n", k=KC)
            P.dma("sp", sv, W[:, c0:c0 + ncols].re("(k p) n -> p k n", p=128))
            for j in range(ncols // 128):
                ps = pj()
                for k in range(KC):
                    P.matmul(ps[:, 0:ntok], sv[:, k, j * 128:(j + 1) * 128], xin[:, k, 0:ntok],
                             start=(k == 0), stop=(k == KC - 1))
                handler(c0 + j * 128, ps[:, 0:ntok])

    def proj_rows(rhs_list, W, handler):
        nK = len(rhs_list)
        for g in range(0, nK, 2):
            n = min(2, nK - g)
            slot = ring_slot()
            sv = slot[:, 0:n * 1024].re("p (k n) -> p k n", k=n)
            P.dma("sp", sv, W[g * 128:(g + n) * 128, :].re("(k p) n -> p k n", p=128))
            for kk in range(n):
                k = g + kk
                for dc in range(KC):
                    P.matmul(PB[dc], sv[:, kk, dc * 128:(dc + 1) * 128], rhs_list[k],
                             start=(k == 0), stop=(k == nK - 1))
        for dc in [KC - 1] + list(range(KC - 1)):
            handler(dc, PB[dc])

    def norm_partial(c):
        n = st.get("ncnt", 0)
        st["nps"] = PB[7]
        P.act(chv(qT, c), chv(hT, c), AF.Square)
        P.matmul(st["nps"], ones_b, chv(qT, c), start=(n == 0), stop=(n == KC - 1))
        st["ncnt"] = (n + 1) % KC
        if n == KC - 1:
            st["nready"] = True

    def rmsnorm(gcols, out):
        if not st.get("nready"):
            for c in range(KC):
                norm_partial(c)
        st["nready"] = False
        P.act(rstd, st["nps"], AF.Sqrt, scale=1.0 / D, bias=EPS)
        P.recip(rstd, rstd)
        for c in range(KC):
            P.stt("dve", out[:, c, :], chv(hT, c), gcols[:, c:c + 1], rstd, ALU.mult, ALU.mult)

    def groups(c0, n):
        g = []
        while n > 0:
            w = min(256, n); g.append((c0, w)); c0 += w; n -= w
        return g

    def ffn(l, which):
        P.section = which
        enter("ffn")
        rmsnorm(pc(l, which + "_norm"), xn)
        W = wb[which + "_w_in"][l]
        for g in range(FC // 2):
            def hg(c0, ps, g=g):
                j = (c0 - g * 256) // 128
                P.act(sg[j], ps, AF.Silu)
            proj_cols(xn, W, [(g * 256, 256)], hg)
            def hu(c0, ps, g=g):
                j = (c0 - DFF - g * 256) // 128
                P.tt("dve", hid[:, 2 * g + j, :], sg[j], ps, ALU.mult)
            proj_cols(xn, W, [(DFF + g * 256, 256)], hu)
        def ho(dc, ps):
            P.stt("dve", chv(hT, dc), ps, 0.5, chv(hT, dc), ALU.mult, ALU.add)
            norm_partial(dc)
        P.section = which + "_out"
        proj_rows([hid[:, k, :] for k in range(FC)], wb[which + "_w_out"][l], ho)

    def xattn(l):
        P.section = "xa"
        enter("xa")
        rmsnorm(pc(l, "xattn_norm"), xn)
        P.dma("sp", mkT, mk_d[l].re("p (k m) -> p k m", k=8))
        P.dma("sp", mvt, mv_d[l].re("p (k m) -> p k m", k=2))
        def hq(c0, ps):
            P.copy("act", qT[:, c0 // 128, :], ps)
        proj_cols(xn, wb["xattn_wq"][l], groups(0, D), hq)
        oT = xn
        for hh in range(4):
            for mb in range(2):
                ps = ms()
                for kk in range(2):
                    P.matmul(ps, mkT[:, 2 * hh + kk, mb * 128:(mb + 1) * 128], qT[:, 2 * hh + kk, :],
                             start=(kk == 0), stop=(kk == 1))
                P.act(exs[hh][:, mb, :], ps, AF.Exp, scale=1.0 / 16.0)
        for hh in range(4):
            ps = ms()
            for mb in range(2):
                P.matmul(ps, ones_b, exs[hh][:, mb, :], start=(mb == 0), stop=(mb == 1))
            P.recip(rden[hh], ps)
        for hh in range(4):
            for kk in range(2):
                ps = ms()
                for mb in range(2):
                    P.matmul(ps, mvt[:, mb, (2 * hh + kk) * 128:(2 * hh + kk + 1) * 128], exs[hh][:, mb, :],
                             start=(mb == 0), stop=(mb == 1))
                P.tt("dve", oT[:, 2 * hh + kk, :], ps, rden[hh], ALU.mult)
        def ho(dc, ps):
            P.tt("dve", chv(hT, dc), ps, chv(hT, dc), ALU.add)
            norm_partial(dc)
        proj_rows([oT[:, k, :] for k in range(KC)], wb["xattn_wo"][l], ho)

    cg = [carve("conv", f"cg{i}", [128, T], F32) for i in range(2)]
    zb = [carve("conv", f"zb{i}", [128, T + 2], F32) for i in range(2)]
    zc = [carve("conv", f"zc{i}", [128, T], F32) for i in range(2)]
    hq_ = carve("hg", "hq", [128, 3, T], BF16)
    hlf = carve("hg", "hlf", [128, 3, T], F32)
    hsg = carve("hg", "hsg", [128, 3, T], BF16)
    hgs = carve("hg", "hgs", [128, 3, T], BF16)
    hvT = carve("hg", "hvT", [128, 3, T], BF16)
    hqt = carve("hg", "hqt", [128, 3, T], BF16)
    hkt = carve("hg", "hkt", [128, 3, T], BF16)
    hgam = carve("hg", "hgam", [128, 3, 8], F32)
    hvtm = [carve("hg", f"hvtm{i}", [64, 384], BF16) for i in range(2)]
    hktm = [carve("hg", f"hktm{i}", [64, 384], BF16) for i in range(2)]
    hsc = carve("hg", "hsc", [64, 6, T], BF16)
    hA = carve("hg", "hA", [128, T], F32); hB = carve("hg", "hB", [128, T], F32)
    hC = carve("hg", "hC", [128, T], F32); hD = carve("hg", "hD", [128, T], F32)
    htS = carve("hg", "htS", [128, 3, 64], F32)
    of = carve("hg", "of", [128, T], F32)
    osq = carve("hg", "osq", [128, T], BF16)

    def conv_handlers(l):
        def h(ci, ps):
            i = ci % 2
            if ci in (2, 3):
                P.copy("act", cg[i], ps)
            elif ci in (4, 5):
                P.copy("pool", zb[i][:, 0:2], cz[l][:, i, :])
                P.tt("dve", zb[i][:, 2:T + 2], cg[i], ps, ALU.mult)
                P.copy("pool", cz[l][:, i, :], zb[i][:, T:T + 2])
                P.ts("dve", zc[i], zb[i][:, 2:T + 2], pc(l, "cw2", i), ALU.mult, pc(l, "cb", i), ALU.add)
                P.stt("dve", zc[i], zb[i][:, 1:T + 1], pc(l, "cw1", i), zc[i], ALU.mult, ALU.add)
                P.stt("dve", zc[i], zb[i][:, 0:T], pc(l, "cw0", i), zc[i], ALU.mult, ALU.add)
            else:
                P.tt("dve", ymix[:, i, :], ps, zc[i], ALU.mult)
        return h

    def hgrn_handler(l):
        def h(ci, ps):
            k = ci - L_HG0; i = k % 3; kind = k // 3
            if kind == 0:
                P.act(hq_[:, i, :], ps, AF.Silu)
            elif kind == 1:
                P.ts("dve", hA, ps, -80.0, ALU.max)
                P.act(hB, hA, AF.Exp, scale=-1.0)
                P.act(hC, hB, AF.Ln, bias=1.0)
                P.act(hD, hB, AF.Ln, scale=pc(l, "lb", i), bias=1.0)
                P.tt("pool", hlf[:, i, :], hD, hC, ALU.subtract)
                P.act(hsg[:, i, :], hA, AF.Sigmoid, scale=-1.0)
            elif kind == 2:
                P.copy("act", hvT[:, i, :], ps)
            else:
                P.act(hgs[:, i, :], ps, AF.Silu)
        return h

    def tm_transposes(srcs, dsts, c, C):
        for (src, dst) in zip(srcs, dsts):
            bank = ms(); bv = bfview(bank)
            for i in range(3):
                P.transpose(bv[0:C, i * 128:(i + 1) * 128], src[:, i, c * C:(c + 1) * C], ident_b)
            P.copy("act", dst, bv[0:C, 0:384])

    def hgrn_core(l):
        P.section = "hg_core"
        C = 64; NCH = T // C
        for i in range(3):
            P.scan(hlf[:, i, :], cst("r64"), hlf[:, i, :], 0.0, ALU.mult, ALU.add)
            P.act(hA, hlf[:, i, :], AF.Exp)
            P.tt("dve", hqt[:, i, :], hq_[:, i, :], hA, ALU.mult)
            P.copy("pool", hgam[:, i, :], hA[:, C - 1::C])
            P.act(hB, hlf[:, i, :], AF.Exp, scale=-1.0)
            P.stt("dve", hkt[:, i, :], hsg[:, i, :], pc(l, "olb", i), hB, ALU.mult, ALU.mult)
        if FLAGS.get("hg_stage", 9) < 2:
            P.memset("pool", ymix[:, 2:5, :], 0.0); return
        m64 = cst("m64is", 64)[:, 64:128]
        mb = V(m64.ap.unsqueeze(1).to_broadcast([C, NCH, C]), m64.bufs)
        for h in range(NHG):
            i = h // 2; r0 = (h % 2) * 64
            ps = ms()
            for c in range(NCH):
                P.matmul(ps[0:C, c * C:(c + 1) * C], hkt[r0:r0 + 64, i, c * C:(c + 1) * C],
                         hqt[r0:r0 + 64, i, c * C:(c + 1) * C])
            P.tt("dve", hsc[:, h, :].re("p (c t) -> p c t", t=C), ps[0:C, :].re("p (c t) -> p c t", t=C), mb, ALU.mult)
        if FLAGS.get("hg_stage", 9) < 3:
            P.memset("pool", ymix[:, 2:5, :], 0.0); return
        P.copy("act", Shb, Sh[l])
        psO = [PB[0], PB[1], PB[2]]

        def hg_pre(c):
            tm_transposes((hvT, hkt), (hvtm[c % 2], hktm[c % 2]), c, C)
            yield

        def hg_chain(c):
            vt = hvtm[c % 2]; kt = hktm[c % 2]
            for h in range(NHG):
                i = h // 2; r0 = (h % 2) * 64
                o = psO[i][r0:r0 + 64, c * C:(c + 1) * C]
                P.matmul(o, Shb[r0:r0 + 64, i, :], hqt[r0:r0 + 64, i, c * C:(c + 1) * C], start=True, stop=False)
                P.matmul(o, vt[:, h * 64:(h + 1) * 64], hsc[:, h, c * C:(c + 1) * C], start=False, stop=True)
            psD = ms()
            for h in range(NHG):
                i = h // 2; r0 = (h % 2) * 64
                P.matmul(psD[r0:r0 + 64, i * 64:(i + 1) * 64], kt[:, h * 64:(h + 1) * 64], vt[:, h * 64:(h + 1) * 64])
            P.tt("dve", htS, Sh[l], psD[:, 0:192].re("p (i e) -> p i e", i=3), ALU.add)
            g = hgam[:, :, c:c + 1]
            gb = V(g.ap.to_broadcast([128, 3, 64]), g.bufs)
            P.tt("dve", Shb, htS, gb, ALU.mult)
            P.tt("dve", Sh[l], htS, gb, ALU.mult)
            yield

        def interleave(gs):
            gs = list(gs)
            while gs:
                for g_ in list(gs):
                    try:
                        next(g_)
                    except StopIteration:
                        gs.remove(g_)

        interleave([hg_pre(0)])
        for c in range(NCH):
            interleave(([hg_pre(c + 1)] if c + 1 < NCH else []) + [hg_chain(c)])
        if FLAGS.get("hg_stage", 9) < 4:
            P.memset("pool", ymix[:, 2:5, :], 0.0); return
        for i in range(3):
            P.act(osq, psO[i], AF.Square)
            P.copy("dve", of, psO[i])
            ps = ms()
            P.matmul(ps, blk_b, osq)
            P.act(hA, ps, AF.Sqrt, scale=1.0 / 64.0, bias=EPS)
            P.recip(hA, hA)
            P.tt("dve", hB, of, hA, ALU.mult)
            P.stt("dve", ymix[:, 2 + i, :], hB, pc(l, "hgn", i), hgs[:, i, :], ALU.mult, ALU.mult)

    praw = [carve("rw", f"praw{i}", [128, T + 1], F32) for i in range(2)]
    rr = carve("rw", "rr", [128, 3, T], BF16); kr = carve("rw", "kr", [128, 3, T], F32); vr = carve("rw", "vr", [128, 3, T], BF16)
    wab = carve("rw", "wab", [128, T], BF16); gsb = carve("rw", "gsb", [128, T], BF16)
    lw = carve("rw", "lw", [128, T], F32); aicl = carve("rw", "aicl", [128, T], F32)
    gg = carve("rw", "gg", [128, 3, T], BF16); bon = carve("rw", "bon", [128, 3, T], BF16)
    kkn = carve("rw", "kkn", [128, T], F32); kmod = carve("rw", "kmod", [128, T], F32)
    bcs = carve("rw", "bcs", [128, T], F32)
    AR = carve("rw", "AR", [128, 3, 8, 2, 64], BF16)
    KT = carve("rw", "KT", [128, 3, T], BF16); BT = carve("rw", "BT", [128, 3, T], BF16); VT = carve("rw", "VT", [128, 3, T], BF16)
    rgam = carve("rw", "rgam", [128, 3, 8], F32)
    NS = 4
    ktm = [carve("rw", f"ktm{i}", [64, 384], BF16) for i in range(NS)]
    btm = [carve("rw", f"btm{i}", [64, 384], BF16) for i in range(NS)]
    vtm = [carve("rw", f"vtm{i}", [64, 384], BF16) for i in range(NS)]
    SKs = [carve("rw", f"SK{i}", [64, 6, 128], BF16) for i in range(NS)]
    SBs = [carve("rw", f"SBr{i}", [64, 6, 64], BF16) for i in range(NS)]
    Rs = [carve("rw", f"Rf{i}", [64, 6, 64], DDT) for i in range(NS)]
    Mfs = [[carve("rw", f"Mf{m}{i}", [64, 6, 64], DDT) for i in range(2)] for m in range(2)]
    Mtfs = [[carve("rw", f"Mtf{m}{i}", [64, 6, 64], DDT) for i in range(2)] for m in range(2)]
    P1f = carve("rw", "P1f", [64, 384], DDT); Ub = carve("rw", "Ub", [64, 384], BF16)
    sqb = carve("rw", "sqb", [128, T], BF16)
    tA = carve("rw", "tA", [128, T], F32); tB = carve("rw", "tB", [128, T], F32)
    tC = carve("rw", "tC", [128, T], F32); tD = carve("rw", "tD", [128, T], F32)
    tS = carve("rw", "tS", [128, 3, 64], F32)
    pT1 = carve("rw", "pT1", [128, T], F32); pT2 = carve("rw", "pT2", [128, T], F32)
    sqb2 = carve("rw", "sqb2", [128, T], BF16)

    def rwkv_handler(l):
        def h(ci, ps):
            j = ci - L_RW0
            pr = praw[j % 2]
            P.copy("act", pr[:, 1:T + 1], ps)
            P.copy("pool", pr[:, 0:1], crw[l][:, j:j + 1])
            P.copy("pool", crw[l][:, j:j + 1], pr[:, T:T + 1])
            P.ts("dve", tA, pr[:, 1:T + 1], pc(l, "omu", j), ALU.mult)
            if j < 9:
                dst = (rr, kr, vr)[j // 3][:, j % 3, :]
            else:
                dst = tB
            P.stt("dve", dst, pr[:, 0:T], pc(l, "mu", j), tA, ALU.mult, ALU.add)
            if j == 9:
                P.act(wab[0:64, :], tB[0:64, :], AF.Tanh)
                P.copy("act", wab[64:128, :], tB[64:128, :])
            elif j == 10:
                P.act(gsb, tB, AF.Sigmoid)
        return h

    def rwkv_core(l):
        P.section = "rw_pre"
        C = 64; NCH = T // C
        for i in range(3):
            ps = ms(); P.matmul(ps, wa2[0:64, l, i * 128:(i + 1) * 128], wab[0:64, :])
            P.act(tA, ps, AF.Sigmoid, bias=pc(l, "w0", i))
            P.ts("dve", lw, tA, -0.606531, ALU.mult)
            ps = ms(); P.matmul(ps, wa2[64:128, l, i * 128:(i + 1) * 128], wab[64:128, :])
            P.act(aicl, ps, AF.Sigmoid, bias=pc(l, "a0", i))
            ps = ms(); P.matmul(ps, g2b[:, l, i * 128:(i + 1) * 128], gsb)
            P.copy("act", gg[:, i, :], ps)
            P.ts("dve", tB, kr[:, i, :], pc(l, "k_k", i), ALU.mult)
            P.act(sqb, tB, AF.Square)
            ps = ms(); P.matmul(ps, blk_b, sqb)
            P.act(tC, ps, AF.Sqrt)
            P.ts("dve", tC, tC, 1e-12, ALU.max)
            P.recip(tC, tC)
            P.tt("dve", kkn, tB, tC, ALU.mult)
            P.ts("dve", pT1, aicl, pc(l, "k_a", i), ALU.mult, pc(l, "oka", i), ALU.add)
            P.tt("dve", kmod, kr[:, i, :], pT1, ALU.mult)
            P.tt("dve", pT1, rr[:, i, :], kmod, ALU.mult)
            P.ts("dve", sqb2, pT1, pc(l, "r_k", i), ALU.mult)
            ps = ms(); P.matmul(ps, blk_b, sqb2)
            P.tt("dve", bon[:, i, :], ps, vr[:, i, :], ALU.mult)
            P.scan(bcs, cst("r64"), lw, 0.0, ALU.mult, ALU.add)
            P.act(tA, bcs, AF.Exp)
            P.tt("dve", AR[:, i, :, 1, :], rr[:, i, :].re("p (c t) -> p c t", t=C), tA.re("p (c t) -> p c t", t=C), ALU.mult)
            P.copy("pool", rgam[:, i, :], tA[:, C - 1::C])
            P.tt("dve", tD, bcs, lw, ALU.subtract)
            P.act(tD, tD, AF.Exp)
            P.stt("dve", AR[:, i, :, 0, :], kkn.re("p (c t) -> p c t", t=C), -1.0, tD.re("p (c t) -> p c t", t=C), ALU.mult, ALU.mult)
            P.act(tC, bcs, AF.Exp, scale=-1.0)
            P.tt("dve", KT[:, i, :], kmod, tC, ALU.mult)
            P.tt("dve", pT2, kkn, aicl, ALU.mult)
            P.tt("dve", BT[:, i, :], pT2, tC, ALU.mult)
            P.copy("act", VT[:, i, :], vr[:, i, :])
        P.copy("act", Srb, Sr[l])
        psY = [PB[0], PB[1], PB[2]]
        mis = cst("m64is", 64); msk_s = cst("m64s", 64); msk_l = cst("m64l", 64); id64 = cst("id64", 64)
        bc3 = lambda m, n: V(m.ap.unsqueeze(1).to_broadcast([64, n, m.ap.shape[1]]), m.bufs)
        def rw_pre(c):
            P.section = "rw_chain"
            sx = c % NS; m = c % 2
            kt_ = ktm[sx]; bt_ = btm[sx]; vt_ = vtm[sx]
            SK = SKs[sx]; SBr = SBs[sx]; Rf = Rs[sx]; Mf = Mfs[m]; Mtf = Mtfs[m]
            tm_transposes((KT, BT, VT), (kt_, bt_, vt_), c, C)
            yield
            X1 = [ms(), ms()]
            for h in range(NRW):
                i = h // 2; r0 = (h % 2) * 64
                P.matmul(X1[h // 4][0:64, (h % 4) * 128:(h % 4 + 1) * 128], KT[r0:r0 + 64, i, c * C:(c + 1) * C],
                         AR[r0:r0 + 64, i, c, :, :].re("p a t -> p (a t)"))
            P.tt("dve", SK[:, 0:4, :], X1[0][0:64, :].re("p (h n) -> p h n", h=4), bc3(mis, 4), ALU.mult)
            P.tt("dve", SK[:, 4:6, :], X1[1][0:64, 0:256].re("p (h n) -> p h n", h=2), bc3(mis, 2), ALU.mult)
            yield
            X2 = [ms(), ms()]
            for h in range(NRW):
                i = h // 2; r0 = (h % 2) * 64
                P.matmul(X2[h // 4][0:64, (h % 4) * 128:(h % 4 + 1) * 128], BT[r0:r0 + 64, i, c * C:(c + 1) * C],
                         AR[r0:r0 + 64, i, c, :, :].re("p a t -> p (a t)"))
            for (bk, h0, nh) in ((X2[0], 0, 4), (X2[1], 4, 2)):
                v4 = bk[0:64, 0:nh * 128].re("p (h n) -> p h n", h=nh)
                P.tt("dve", Mf[0][:, h0:h0 + nh, :], v4[:, :, 0:64], bc3(msk_s, nh), ALU.mult)
                P.tt("dve", SBr[:, h0:h0 + nh, :], v4[:, :, 64:128], bc3(mis[:, 64:128], nh), ALU.mult)
            X3 = ms()
            for h in range(NRW):
                i = h // 2; r0 = (h % 2) * 64
                P.matmul(X3[0:64, h * 64:(h + 1) * 64], AR[r0:r0 + 64, i, c, 0, :], BT[r0:r0 + 64, i, c * C:(c + 1) * C])
            P.tt("dve", Mtf[0], X3[0:64, 0:384].re("p (h n) -> p h n", h=6), bc3(msk_l, 6), ALU.mult)
            P.tt("dve", Rf, Mf[0], bc3(id64, 6), ALU.add)
            yield
            idb64 = ident_b[0:64, 0:64]
            def sq_mm(cur, want_m):
                pMt = ms()
                for h in range(NRW):
                    P.matmul(pMt[0:64, h * 64:(h + 1) * 64], Mf[cur][:, h, :], Mtf[cur][:, h, :])
                pM = None
                if want_m:
                    pM = ms()
                    for h in range(NRW):
                        P.matmul(pM[0:64, h * 64:(h + 1) * 64], Mtf[cur][:, h, :], Mf[cur][:, h, :])
                return pMt, pM
            def sq_ev(pMt, pM, nxt):
                P.copy("dve", Mtf[nxt], pMt[0:64, 0:384].re("p (h n) -> p h n", h=6))
                if pM is not None:
                    P.copy("act", Mf[nxt], pM[0:64, 0:384].re("p (h n) -> p h n", h=6))
            def r_mm(nxt):
                pR = ms()
                for h in range(NRW):
                    o = pR[0:64, h * 64:(h + 1) * 64]
                    P.matmul(o, idb64, Rf[:, h, :], start=True, stop=False)
                    P.matmul(o, Mtf[nxt][:, h, :], Rf[:, h, :], start=False, stop=True)
                return pR
            def r_ev(pR):
                P.copy("act", Rf, pR[0:64, 0:384].re("p (h n) -> p h n", h=6))
            cur = 0
            pMt, pM = sq_mm(cur, True)
            sq_ev(pMt, pM, 1 - cur)
            yield
            for lev in range(5):
                nxt = 1 - cur
                pR = r_mm(nxt)
                if lev < 4:
                    pMt, pM = sq_mm(nxt, lev < 3)
                r_ev(pR)
                if lev < 4:
                    sq_ev(pMt, pM, cur)
                yield
                cur = nxt

        def rw_chain(c):
            P.section = "rw_chain2"
            sx = c % NS
            kt_ = ktm[sx]; bt_ = btm[sx]; vt_ = vtm[sx]
            SK = SKs[sx]; SBr = SBs[sx]; Rf = Rs[sx]
            pP = ms()
            for h in range(NRW):
                i = h // 2; r0 = (h % 2) * 64
                o = pP[0:64, h * 64:(h + 1) * 64]
                P.matmul(o, AR[r0:r0 + 64, i, c, 0, :], Srb[r0:r0 + 64, i, :], start=True, stop=False)
                P.matmul(o, SK[:, h, 0:64], vt_[:, h * 64:(h + 1) * 64], start=False, stop=True)
            P.copy("act", P1f, pP[0:64, 0:384])
            yield
            pU = ms()
            for h in range(NRW):
                P.matmul(pU[0:64, h * 64:(h + 1) * 64], Rf[:, h, :], P1f[:, h * 64:(h + 1) * 64])
            P.copy("dve", Ub, pU[0:64, 0:384])
            yield
            for h in range(NRW):
                i = h // 2; r0 = (h % 2) * 64
                o = psY[i][r0:r0 + 64, c * C:(c + 1) * C]
                P.matmul(o, Srb[r0:r0 + 64, i, :], AR[r0:r0 + 64, i, c, 1, :], start=True, stop=False)
                P.matmul(o, Ub[:, h * 64:(h + 1) * 64], SBr[:, h, :], start=False, stop=False)
                P.matmul(o, vt_[:, h * 64:(h + 1) * 64], SK[:, h, 64:128], start=False, stop=True)
            pD = ms()
            for h in range(NRW):
                i = h // 2; r0 = (h % 2) * 64
                o = pD[r0:r0 + 64, i * 64:(i + 1) * 64]
                P.matmul(o, bt_[:, h * 64:(h + 1) * 64], Ub[:, h * 64:(h + 1) * 64], start=True, stop=False)
                P.matmul(o, kt_[:, h * 64:(h + 1) * 64], vt_[:, h * 64:(h + 1) * 64], start=False, stop=True)
            P.tt("dve", tS, Sr[l], pD[:, 0:192].re("p (i e) -> p i e", i=3), ALU.add)
            g = rgam[:, :, c:c + 1]
            gb = V(g.ap.to_broadcast([128, 3, 64]), g.bufs)
            P.tt("dve", Srb, tS, gb, ALU.mult)
            P.tt("dve", Sr[l], tS, gb, ALU.mult)
            yield

        def seq(*gs):
            for g_ in gs:
                yield from g_

        def interleave(gs):
            gs = list(gs)
            while gs:
                for g_ in list(gs):
                    try:
                        next(g_)
                    except StopIteration:
                        gs.remove(g_)

        if FLAGS.get("rw_pipe", True):
            interleave([rw_pre(0), rw_pre(1)])
            for k in range(1, NCH // 2):
                interleave([rw_pre(2 * k), rw_pre(2 * k + 1), seq(rw_chain(2 * k - 2), rw_chain(2 * k - 1))])
            interleave([seq(rw_chain(NCH - 2), rw_chain(NCH - 1))])
        else:
            for c in range(NCH):
                interleave([seq(rw_pre(c), rw_chain(c))])
        P.section = "rw_post"
        for i in range(3):
            P.copy("dve", tA, psY[i])
            P.copy("act", sqb, tA)
            ps = ms(); P.matmul(ps, blk_b, sqb)
            P.stt("dve", tB, ps, -1.0 / 64.0, tA, ALU.mult, ALU.add)
            P.act(sqb, tB, AF.Square)
            ps = ms(); P.matmul(ps, blk_b, sqb)
            P.act(tC, ps, AF.Sqrt, scale=1.0 / 64.0, bias=64e-5)
            P.recip(tC, tC)
            P.tt("dve", tB, tB, tC, ALU.mult)
            P.ts("pool", pT1, tB, pc(l, "ln_w", i), ALU.mult, pc(l, "ln_b", i), ALU.add)
            P.tt("pool", pT1, pT1, bon[:, i, :], ALU.add)
            P.tt("pool", ymix[:, 5 + i, :], pT1, gg[:, i, :], ALU.mult)

    def mixer(l):
        P.section = "mix_in"
        rmsnorm(pc(l, "mix_norm"), xn)
        W = wb["w_mix_in"][l]
        enter("conv")
        ch = conv_handlers(l)
        proj_cols(xn, W, [(256, 256), (512, 256), (0, 256)], lambda c0, ps: ch(c0 // 128, ps))
        if FLAGS["hgrn"]:
            enter("hg")
            P.section = "hg_in"
            hh = hgrn_handler(l)
            proj_cols(xn, W, groups(L_HG0 * 128, 12 * 128), lambda c0, ps: hh(c0 // 128, ps))
            hgrn_core(l)
        else:
            P.memset("pool", ymix[:, 2:5, :], 0.0)
        if FLAGS["rwkv"]:
            enter("rw")
            P.section = "rw_in"
            rh = rwkv_handler(l)
            proj_cols(xn, W, groups(L_RW0 * 128, 11 * 128), lambda c0, ps: rh(c0 // 128, ps))
            rwkv_core(l)
        else:
            P.memset("pool", ymix[:, 5:8, :], 0.0)
        def ho(dc, ps):
            P.tt("dve", chv(hT, dc), ps, chv(hT, dc), ALU.add)
            norm_partial(dc)
        P.section = "mix_out"
        proj_rows([ymix[:, k, :] for k in range(KC)], wb["w_mix_out"][l], ho)

    memt = xs[:, 0:2, :]
    P.dma("sp", memt, mem_d.rearrange("(b p) d -> p b d", p=128))
    mss = P.tile("mss", [128, 2], F32)
    gens["pro"] = {"off": gens["xa"]["off"], "vs": []}
    msq = carve("pro", "msq", [128, D], BF16)
    for b in range(2):
        P.act(msq, memt[:, b, :], AF.Square, accum_out=mss[:, b:b + 1])
    P.act(mss, mss, AF.Sqrt, scale=1.0 / D, bias=EPS)
    P.recip(mss, mss)
    memn = carve("pro", "memn", [128, 2, D], F32)
    for b in range(2):
        P.ts("dve", memn[:, b, :], memt[:, b, :], mss[:, b:b + 1], ALU.mult)
    memT = carve("pro", "memT", [128, KC, MEM], F32)
    for k in range(KC):
        ps = ms()
        for b in range(2):
            P.transpose(ps[:, b * 128:(b + 1) * 128], memn[:, b, k * 128:(k + 1) * 128], ident_f)
        P.copy("act", memT[:, k, :], ps[:, 0:256])
    memg = qT[:, :, 0:MEM]
    mks = ymix[:, :, 0:MEM]
    for l in range(L):
        for k in range(KC):
            P.ts("dve", memg[:, k, :], memT[:, k, :], pc(l, "mem_norm", k), ALU.mult)
        def hk(c0, ps):
            P.copy("act", mks[:, c0 // 128, :], ps)
        proj_cols(memg, wb["xattn_wkv"][l], groups(0, D), hk, ntok=MEM)
        P.dma("sp", mk_d[l].re("p (k m) -> p k m", k=8), mks)
        for (c0, ncols) in groups(D, D):
            slot = ring_slot()
            sv = slot[:, 0:KC * ncols].re("p (k n) -> p k n", k=KC)
            P.dma("sp", sv, wb["xattn_wkv"][l][:, c0:c0 + ncols].re("(k p) n -> p k n", p=128))
            for b in range(2):
                ps = pj()
                for k in range(KC):
                    P.matmul(ps[:, 0:ncols], memg[:, k, b * 128:(b + 1) * 128], sv[:, k, :], start=(k == 0), stop=(k == KC - 1))
                P.copy("act", mvt[:, b, c0 - D:c0 - D + ncols], ps[:, 0:ncols])
        P.dma("sp", mv_d[l].re("p (k m) -> p k m", k=2), mvt)

    for t in range(NT):
        P.section = "io"
        P.dma("sp", xs, x_d[t * T:(t + 1) * T, :].rearrange("(s p) d -> p s d", p=128))
        for c in range(KC):
            ps = pj()
            for s in range(4):
                P.transpose(ps[:, s * 128:(s + 1) * 128], xs[:, s, c * 128:(c + 1) * 128], ident_f)
            P.copy("act" if c % 2 else "dve", chv(hT, c), ps)
            norm_partial(c)
        for l in range(L):
            P.phase += 1
            if FLAGS["ffn"]:
                ffn(l, "ffn1")
            if FLAGS["mix"]:
                mixer(l)
            if FLAGS["xa"]:
                xattn(l)
            if FLAGS["ffn"]:
                ffn(l, "ffn2")
        P.phase += 1
        P.section = "io"
        fo = L * NPL
        if not st.get("nready"):
            for c in range(KC):
                norm_partial(c)
        st["nready"] = False
        P.act(rstd, st["nps"], AF.Sqrt, scale=1.0 / D, bias=EPS)
        P.recip(rstd, rstd)
        for c in range(KC):
            P.stt("dve", hT[:, c, :], hT[:, c, :], par[:, fo + c:fo + c + 1], rstd, ALU.mult, ALU.mult)
        for s in range(4):
            for half in range(2):
                ps = pj()
                for cc in range(4):
                    c = half * 4 + cc
                    P.transpose(ps[:, cc * 128:(cc + 1) * 128], hT[:, c, s * 128:(s + 1) * 128], ident_f)
                P.copy("act" if half else "dve", xs[:, s, half * 512:(half + 1) * 512], ps)
        P.dma("sp", out_d[t * T:(t + 1) * T, :].re("(s p) d -> p s d", p=128), xs)
    P.wait_dma_final("sp", out_d)
    stats = P.finish()
    P.close()
    if SECLOG is not None:
        SECLOG.update(P.seclog)
    return nc, stats


SECLOG = None
FLAGS = {"ffn": True, "mix": True, "xa": True, "hgrn": True, "rwkv": True}


def kernel(**inp):
    x = np.asarray(inp["x"], np.float32)
    B, S, _ = x.shape
    L = inp["ffn1_norm"].shape[0]
    nc, stats = build(S, L)
    par = np.zeros((128, L * NPL + 8), np.float32)
    for l in range(L):
        def put(n, a):
            o, w = PL[n]; par[:, l * NPL + o:l * NPL + o + w] = a
        for n in ("ffn1_norm", "mix_norm", "xattn_norm", "ffn2_norm", "mem_norm"):
            put(n, fm(inp[n][l]))
        for k in range(3):
            put(f"cw{k}", fm(inp["conv_w"][l][k]))
        put("cb", fm(inp["conv_b"][l])); put("lbl", fm(inp["hgrn_lb_logits"][l])); put("hgn", fm(inp["hgrn_norm"][l]))
        put("mu", fm(inp["rwkv_mu"][l])); put("w0", fm(inp["rwkv_w0"][l])); put("a0", fm(inp["rwkv_a0"][l]))
        put("k_k", fm(inp["rwkv_k_k"][l])); put("k_a", fm(inp["rwkv_k_a"][l])); put("r_k", fm(inp["rwkv_r_k"][l]))
        put("ln_w", fm(inp["rwkv_ln_w"][l])); put("ln_b", fm(inp["rwkv_ln_b"][l]))
    par[:, L * NPL:L * NPL + 8] = fm(inp["final_norm"])
    con = make_consts()
    shared = {"params": par, "consts": con}
    for n in ("ffn1_w_in", "ffn1_w_out", "w_mix_in", "w_mix_out", "xattn_wq", "xattn_wkv", "xattn_wo",
              "ffn2_w_in", "ffn2_w_out", "rwkv_w2", "rwkv_a2", "rwkv_g2"):
        shared[n] = np.ascontiguousarray(np.asarray(inp[n], np.float32))
    mem = np.asarray(inp["mem"], np.float32)
    in_maps = []
    for b in range(B):
        m = dict(shared)
        m["x"] = np.ascontiguousarray(x[b]); m["mem"] = np.ascontiguousarray(mem[b])
        in_maps.append(m)
    res = run_bass_kernel_spmd(nc, in_maps, core_ids=list(range(B)))
    return np.stack([np.asarray(r["out"], np.float32) for r in res.results], 0)
```

```python
from concourse.bass_utils import run_bass_kernel_spmd
import numpy as np
from contextlib import ExitStack
import concourse.bass as bass
import concourse.mybir as mybir

F32 = mybir.dt.float32
BF16 = mybir.dt.bfloat16
AF = mybir.ActivationFunctionType
ALU = mybir.AluOpType
AX = mybir.AxisListType

COMPUTE = ("pe", "act", "dve", "pool")
NPH = 4
STRICT = True


class Buf:
    __slots__ = ("name", "writer", "readers", "sem", "cnt", "excl")

    def __init__(self, name, excl=False):
        self.name = name
        self.excl = excl
        self.writer = None
        self.readers = []
        self.sem = None
        self.cnt = 0


class V:
    __slots__ = ("ap", "bufs")

    def __init__(self, ap, bufs):
        self.ap = ap
        self.bufs = bufs

    def __getitem__(self, key):
        return V(self.ap[key], self.bufs)

    def re(self, s, **kw):
        return V(self.ap.rearrange(s, **kw), self.bufs)

    def bc(self, shape):
        return V(self.ap.to_broadcast(shape), self.bufs)


class Op:
    __slots__ = ("eng", "fn", "deps", "dwaits", "signal", "signum", "dma", "tok", "idx", "ph")

    def __init__(self, eng, fn, dma=False):
        self.eng = eng
        self.fn = fn
        self.deps = []
        self.dwaits = {}
        self.signal = False
        self.signum = 0
        self.dma = dma
        self.tok = None
        self.idx = 0
        self.ph = 0


class Prog:
    def __init__(self, nc):
        self.nc = nc
        self.es = ExitStack()
        self.ops = {e: [] for e in ("pe", "act", "dve", "pool", "sp")}
        self.sems = {}
        self.nbuf = 0
        self.final_waits = []
        self.phase = 0
        self.section = ""
        self.seclog = None

    def sbuf(self, name, shape, dtype, nsub=1):
        t = self.es.enter_context(self.nc.sbuf_tensor(name, list(shape), dtype))
        bufs = [Buf(f"{name}.{i}") for i in range(nsub)]
        return t, bufs

    def tile(self, name, shape, dtype):
        t, bufs = self.sbuf(name, shape, dtype)
        return V(t[tuple(slice(None) for _ in shape)], bufs)

    def psum(self, name, shape, dtype=F32):
        t = self.es.enter_context(self.nc.psum_tensor(name, list(shape), dtype))
        return V(t[tuple(slice(None) for _ in shape)], [Buf(name, excl=True)])

    def dram(self, name, shape, dtype, kind="Internal"):
        t = self.nc.dram_tensor(name, list(shape), dtype, kind=kind)
        return V(t.ap(), [Buf(name)])

    def newsem(self, name):
        s = self.es.enter_context(self.nc.semaphore(name))
        return s

    def _dep(self, B, A, kind):
        if A is None or A is B:
            return
        if A.dma:
            sem, val = A.tok
            if B.dwaits.get(sem, (None, 0))[1] < val:
                B.dwaits[sem] = (sem, val)
            return
        if A.eng == B.eng:
            if B.eng == "pe" or (kind != "RAW" and not STRICT):
                return
        A.signal = True
        B.deps.append(A)

    def _track(self, op, reads, writes):
        for b in reads:
            self._dep(op, b.writer, "RAW")
            if b.excl:
                for r in b.readers:
                    if r.eng != op.eng:
                        self._dep(op, r, "RAR")
        for b in writes:
            self._dep(op, b.writer, "WAW")
            for r in b.readers:
                self._dep(op, r, "WAR")
        for b in reads:
            b.readers.append(op)
        for b in writes:
            b.writer = op
            b.readers = []

    @staticmethod
    def _bufs(vs):
        out = []
        for v in vs:
            if isinstance(v, V):
                for b in v.bufs:
                    if b not in out:
                        out.append(b)
        return out

    def emit(self, eng, fn, reads, writes):
        op = Op(eng, fn)
        op.ph = self.phase % NPH
        if self.seclog is not None:
            self.seclog[eng].append(self.section)
        self._track(op, self._bufs(reads), self._bufs(writes))
        op.idx = len(self.ops[eng])
        self.ops[eng].append(op)
        return op

    def dma(self, eng, out, in_, **kw):
        op = Op(eng, None, dma=True)
        rb = self._bufs([in_])
        wb = self._bufs([out])
        self._track(op, rb, wb)
        owner = wb[0] if wb else rb[0]
        if owner.sem is None:
            owner.sem = self.newsem("d_" + owner.name.replace(".", "_"))
        owner.cnt += 16
        op.tok = (owner.sem, owner.cnt)
        oap = out.ap if isinstance(out, V) else out
        iap = in_.ap if isinstance(in_, V) else in_
        sem = owner.sem
        op.fn = lambda e: e.dma_start(out=oap, in_=iap, **kw).then_inc(sem, 16)
        op.idx = len(self.ops[eng])
        self.ops[eng].append(op)
        return op

    def wait_dma_final(self, eng, v):
        for b in v.bufs:
            if b.writer is not None and b.writer.dma:
                self.final_waits.append((eng, b.writer.tok))

    @staticmethod
    def _a(x):
        return x.ap if isinstance(x, V) else x

    def matmul(self, out, lhsT, rhs, start=True, stop=True, **kw):
        o, l, r = self._a(out), self._a(lhsT), self._a(rhs)
        op = self.emit("pe", lambda e: e.matmul(o, l, r, start=start, stop=stop, **kw),
                       [lhsT, rhs], [out])
        self._pe_rowgroup(op, l)
        if self.seclog is not None and l.dtype == F32:
            self.seclog["pe"].append(self.section)
        return op

    def _pe_rowgroup(self, op, lhs_ap):
        rg = (lhs_ap.base_partition(), min(128, ((lhs_ap.shape[0] + 31) // 32) * 32))
        prev = getattr(self, "_last_pe", None)
        if prev is not None and prev[1] != rg:
            prev[0].signal = True
            op.deps.append(prev[0])
        self._last_pe = (op, rg)

    def transpose(self, out, in_, ident):
        o, i, d = self._a(out), self._a(in_), self._a(ident)
        op = self.emit("pe", lambda e: e.transpose(o, i, d), [in_, ident], [out])
        self._pe_rowgroup(op, i)
        return op

    def act(self, out, in_, func, bias=None, scale=1.0, accum_out=None, eng="act"):
        o, i = self._a(out), self._a(in_)
        b = self._a(bias) if bias is not None else None
        s = self._a(scale)
        acc = self._a(accum_out) if accum_out is not None else None
        kw = {}
        if b is not None:
            kw["bias"] = b
        if acc is not None:
            kw["accum_out"] = acc
        return self.emit("act", lambda e: e.activation(o, i, func, scale=s, **kw),
                         [in_, bias, scale], [out, accum_out])

    def tt(self, eng, out, in0, in1, op):
        o, a, b = self._a(out), self._a(in0), self._a(in1)
        return self.emit(eng, lambda e: e.tensor_tensor(o, a, b, op), [in0, in1], [out])

    def ts(self, eng, out, in0, s1, op0, s2=None, op1=None, accum_out=None):
        o, a = self._a(out), self._a(in0)
        x1, x2 = self._a(s1), self._a(s2)
        acc = self._a(accum_out) if accum_out is not None else None
        kw = {}
        if op1 is not None:
            kw["op1"] = op1
        if acc is not None:
            kw["accum_out"] = acc
        return self.emit(eng, lambda e: e.tensor_scalar(o, a, x1, x2, op0, **kw),
                         [in0, s1, s2], [out, accum_out])

    def stt(self, eng, out, in0, scalar, in1, op0, op1):
        o, a, s, b = self._a(out), self._a(in0), self._a(scalar), self._a(in1)
        return self.emit(eng, lambda e: e.scalar_tensor_tensor(o, a, s, b, op0, op1),
                         [in0, scalar, in1], [out])

    def copy(self, eng, out, in_):
        o, i = self._a(out), self._a(in_)
        if eng == "act":
            return self.emit("act", lambda e: e.copy(o, i), [in_], [out])
        return self.emit(eng, lambda e: e.tensor_copy(o, i), [in_], [out])

    def memset(self, eng, out, val):
        o = self._a(out)
        return self.emit(eng, lambda e: e.memset(o, val), [], [out])

    def scan(self, out, d0, d1, initial, op0, op1):
        o, a, b, i = self._a(out), self._a(d0), self._a(d1), self._a(initial)
        return self.emit("dve", lambda e: e.tensor_tensor_scan(o, a, b, i, op0, op1),
                         [d0, d1, initial], [out])

    def recip(self, out, in_):
        o, i = self._a(out), self._a(in_)
        return self.emit("dve", lambda e: e.reciprocal(o, i), [in_], [out])

    def generic(self, eng, fn, reads, writes):
        return self.emit(eng, fn, reads, writes)

    def finish(self):
        nc = self.nc
        esem = {(e, k): self.newsem(f"s_{e}{k}") for e in COMPUTE for k in range(NPH)}
        for e in COMPUTE:
            n = [0] * NPH
            for op in self.ops[e]:
                if op.signal and not op.dma:
                    n[op.ph] += 1
                    op.signum = n[op.ph]
        engobj = {"pe": "tensor", "act": "scalar", "dve": "vector", "pool": "gpsimd", "sp": "sync"}
        stats = {}
        with nc.Block() as block:
            for ename in ("sp", "pool", "act", "dve", "pe"):
                ops = self.ops[ename]
                finals = [t for (e, t) in self.final_waits if e == ename]
                if not ops and not finals:
                    continue

                def body(eng, ops=ops, ename=ename, finals=finals):
                    seen = {}
                    nw = 0
                    for op in ops:
                        need = {}
                        for A in op.deps:
                            k = esem[(A.eng, A.ph)]
                            if need.get(k, 0) < A.signum:
                                need[k] = A.signum
                        for sem, val in op.dwaits.values():
                            if need.get(sem, 0) < val:
                                need[sem] = val
                        for k, val in need.items():
                            if seen.get(k, 0) < val:
                                eng.wait_ge(k, val)
                                seen[k] = val
                                nw += 1
                        ins = op.fn(eng)
                        if op.signal and not op.dma:
                            ins.then_inc(esem[(ename, op.ph)], 1)
                    for sem, val in finals:
                        eng.wait_ge(sem, val)
                    stats[ename] = (len(ops), nw)

                getattr(block, engobj[ename])(body)
        self.stats = stats
        return stats

    def close(self):
        self.es.close()


D = 1024; KC = 8; DFF = 2816; FC = 22; T = 512; MEM = 256
DDT = BF16
EPS = 1e-6
L_CONV0, L_HG0, L_RW0 = 0, 6, 18
NHG = 6; NRW = 6
HORD = [0, 2, 4, 1, 3, 5]

PL = {}
_o = 0
for _n, _w in [("ffn1_norm", 8), ("mix_norm", 8), ("xattn_norm", 8), ("ffn2_norm", 8), ("mem_norm", 8),
               ("cw0", 2), ("cw1", 2), ("cw2", 2), ("cb", 2), ("lbl", 3), ("hgn", 3), ("mu", 11),
               ("w0", 3), ("a0", 3), ("k_k", 3), ("k_a", 3), ("r_k", 3), ("ln_w", 3), ("ln_b", 3),
               ("omu", 11), ("oka", 3), ("lb", 3), ("olb", 3)]:
    PL[_n] = (_o, _w); _o += _w
NPL = _o
CL = {}
_o = 0
for _n, _w in [("ident", 128), ("blk64", 128), ("ones", 128), ("m32i", 32), ("m64is", 128), ("m64s", 64),
               ("m64l", 64), ("r32", 512), ("r64", 512), ("id64", 64)]:
    CL[_n] = (_o, _w); _o += _w
NCL = _o


def make_consts():
    c = np.zeros((128, NCL), np.float32)
    def put(n, a):
        o, w = CL[n]; c[:a.shape[0], o:o + w] = a
    put("ident", np.eye(128))
    b = np.zeros((128, 128)); b[:64, :64] = 1; b[64:, 64:] = 1
    put("blk64", b)
    put("ones", np.ones((128, 128)))
    s32 = np.arange(32)
    put("m32i", (s32[:, None] <= s32[None, :]).astype(np.float32))
    s64 = np.arange(64)
    strict = (s64[:, None] < s64[None, :]).astype(np.float32)
    incl = (s64[:, None] <= s64[None, :]).astype(np.float32)
    put("m64is", np.concatenate([strict, incl], 1))
    put("m64s", strict)
    put("m64l", strict.T.copy())
    r = np.ones((128, 512)); r[:, ::32] = 0; put("r32", r)
    r = np.ones((128, 512)); r[:, ::64] = 0; put("r64", r)
    put("id64", np.eye(64))
    return c


def fm(v):
    v = np.asarray(v, np.float32).reshape(-1)
    return np.ascontiguousarray(v.reshape(-1, 128).T)


def build(S, L):
    NT = S // T
    nc = bass.Bass("TRN2", target_bir_lowering=False)
    P = Prog(nc)
    if SECLOG is not None:
        P.seclog = {e: [] for e in ("pe", "act", "dve", "pool", "sp")}
    ein = lambda n, sh: nc.dram_tensor(n, list(sh), F32, kind="ExternalInput").ap()
    x_d = ein("x", [S, D]); mem_d = ein("mem", [MEM, D])
    par_d = ein("params", [128, L * NPL + 8]); con_d = ein("consts", [128, NCL])
    wnames = [("ffn1_w_in", D, 2 * DFF), ("ffn1_w_out", DFF, D), ("w_mix_in", D, 3712), ("w_mix_out", D, D),
              ("xattn_wq", D, D), ("xattn_wkv", D, 2 * D), ("xattn_wo", D, D),
              ("ffn2_w_in", D, 2 * DFF), ("ffn2_w_out", DFF, D)]
    wf = {n: ein(n, [L, a, b]) for n, a, b in wnames}
    w2_d = ein("rwkv_w2", [L, 64, 384]); a2_d = ein("rwkv_a2", [L, 64, 384]); g2_d = ein("rwkv_g2", [L, 128, 384])
    out_d = P.dram("out", [S, D], F32, kind="ExternalOutput")
    wb = {}
    for n, a, b in wnames:
        t = nc.dram_tensor(n + "_b", [L, a, b], BF16, kind="Internal").ap()
        wb[n] = [V(t[l], [Buf(f"w_{n}{l}")]) for l in range(L)]
    mk_d = [P.dram(f"mk_d{l}", [128, 8 * 256], BF16) for l in range(L)]
    mv_d = [P.dram(f"mv_d{l}", [128, 2 * 1024], BF16) for l in range(L)]

    par = P.tile("par", [128, L * NPL + 8], F32)
    con = P.tile("con", [128, NCL], F32)
    cb = P.tile("cb", [128, 128 * 3], BF16)
    ident_b = cb[:, 0:128]; blk_b = cb[:, 128:256]; ones_b = cb[:, 256:384]
    ident_f = con[:, CL["ident"][0]:CL["ident"][0] + 128]
    def cst(n, rows=128):
        o, w = CL[n]; return con[0:rows, o:o + w]
    def pc(l, n, i=None):
        o, w = PL[n]; o += l * NPL
        return par[:, o:o + w] if i is None else par[:, o + i:o + i + 1]
    wa2 = P.tile("wa2", [128, L, 384], BF16)
    g2b = P.tile("g2b", [128, L, 384], BF16)
    def mtile(name, shape, dt, n):
        t, bufs = P.sbuf(name, shape, dt, nsub=n)
        return V(t[tuple(slice(None) for _ in shape)], bufs)
    def chv(v, c):
        return V(v.ap[:, c, :], [v.bufs[c]])
    hT = mtile("hT", [128, KC, T], F32, KC)
    xn = P.tile("xn", [128, KC, T], BF16)
    ARENA = 97 * 1024
    arena_t, _ = P.sbuf("arena", [128, ARENA // 2], BF16)
    gens = {}
    def carve(gen, name, shape, dt):
        g = gens.setdefault(gen, {"off": 0, "vs": []})
        esz = 4 if dt == F32 else 2
        n = 1
        for d in shape[1:]:
            n *= d
        nbytes = (n * esz + 63) // 64 * 64
        o = g["off"]; g["off"] += nbytes
        assert g["off"] <= ARENA, (gen, name, g["off"])
        ap = arena_t[0:shape[0], o // 2:(o + n * esz) // 2]
        if dt == F32:
            ap = ap.bitcast(F32)
        if len(shape) > 2:
            names = " ".join(f"d{i}" for i in range(1, len(shape)))
            ap = ap.rearrange(f"p ({names}) -> p {names}", **{f"d{i}": shape[i] for i in range(1, len(shape) - 1)})
        v = V(ap, [Buf(name)])
        g["vs"].append(v)
        return v
    def handoff(old, new):
        ops = []
        for v in gens[old]["vs"]:
            for b in v.bufs:
                if b.writer is not None:
                    ops.append(b.writer)
                ops.extend(b.readers)
        for v in gens[new]["vs"]:
            for b in v.bufs:
                b.readers.extend(ops)
    def enter(gen):
        for g_ in list(gens):
            if g_ != gen:
                handoff(g_, gen)
    hid = carve("ffn", "hid", [128, FC, T], BF16)
    mkT = carve("xa", "mkT", [128, 8, 256], BF16)
    mvt = carve("xa", "mvt", [128, 2, 1024], BF16)
    exs = [carve("xa", f"exs{i}", [128, 2, T], BF16) for i in range(4)]
    rden = [carve("xa", f"rden{i}", [128, T], F32) for i in range(4)]
    RING = 5
    ring = [P.tile(f"ring{i}", [128, 2048], BF16) for i in range(RING)]
    rstd = P.tile("rstd", [128, T], F32)
    xs = P.tile("xs", [128, 4, D], F32)
    sg = [P.tile(f"sg{i}", [128, T], F32) for i in range(2)]
    ymix = P.tile("ymix", [128, KC, T], BF16)
    qT = mtile("qT", [128, KC, T], BF16, KC)
    Sh = [P.tile(f"Sh{l}", [128, 3, 64], F32) for l in range(L)]
    Sr = [P.tile(f"Sr{l}", [128, 3, 64], F32) for l in range(L)]
    Shb = P.tile("Shb", [128, 3, 64], BF16)
    Srb = P.tile("Srb", [128, 3, 64], BF16)
    cz = [P.tile(f"cz{l}", [128, 2, 2], F32) for l in range(L)]
    crw = [P.tile(f"crw{l}", [128, 11], F32) for l in range(L)]
    PB = [P.psum(f"pb{i}", [128, 512]) for i in range(8)]
    st = {"ring": 0, "pj": 0, "ms": 0}
    def pj():
        st["pj"] = (st["pj"] + 1) % 3
        return PB[st["pj"]]
    def ms():
        st["ms"] = (st["ms"] + 1) % 4
        return PB[3 + st["ms"]]
    def bfview(bank):
        return V(bank.ap.bitcast(BF16), bank.bufs)

    P.dma("sp", par, par_d)
    P.dma("sp", con, con_d)
    for l in range(L):
        P.dma("pool", wb["xattn_wkv"][l], wf["xattn_wkv"][l])
    for l in range(L):
        for n in ("ffn1_w_in", "ffn1_w_out", "w_mix_in", "w_mix_out", "xattn_wq", "xattn_wo", "ffn2_w_in", "ffn2_w_out"):
            P.dma("pool", wb[n][l], wf[n][l])
    P.dma("pool", wa2[0:64], w2_d.rearrange("l k c -> k l c"))
    P.dma("pool", wa2[64:128], a2_d.rearrange("l k c -> k l c"))
    P.dma("pool", g2b, g2_d.rearrange("l k c -> k l c"))
    P.copy("act", cb[:, 0:128], ident_f)
    P.copy("act", cb[:, 128:256], cst("blk64"))
    P.copy("act", cb[:, 256:384], cst("ones"))
    for l in range(L):
        P.memset("pool", Sh[l], 0.0); P.memset("pool", Sr[l], 0.0)
        P.memset("pool", cz[l], 0.0); P.memset("pool", crw[l], 0.0)
    for l in range(L):
        P.ts("dve", pc(l, "omu"), pc(l, "mu"), -1.0, ALU.mult, 1.0, ALU.add)
        P.ts("dve", pc(l, "oka"), pc(l, "k_a"), -1.0, ALU.mult, 1.0, ALU.add)
    ex_l = P.tile("ex_l", [128, L, 3], F32)
    sm = P.tile("sm", [128, 3], F32)
    for l in range(L):
        P.act(ex_l[:, l, :], pc(l, "lbl"), AF.Exp)
    P.copy("dve", sm, ex_l[:, 0, :])
    for l in range(1, L):
        P.tt("dve", sm, sm, ex_l[:, l, :], ALU.add)
    P.recip(sm, sm)
    for l in range(L):
        P.tt("dve", ex_l[:, l, :], ex_l[:, l, :], sm, ALU.mult)
    P.memset("dve", pc(0, "lb"), 0.0)
    for l in range(1, L):
        P.tt("dve", pc(l, "lb"), pc(l - 1, "lb"), ex_l[:, l, :], ALU.add)
    for l in range(L):
        P.ts("dve", pc(l, "lb"), pc(l, "lb"), 0.0, ALU.max)
        P.ts("dve", pc(l, "olb"), pc(l, "lb"), -1.0, ALU.mult, 1.0, ALU.add)

    def ring_slot():
        st["ring"] = (st["ring"] + 1) % RING
        return ring[st["ring"]]

    def proj_cols(xin, W, groups, handler, ntok=T):
        for (c0, ncols) in groups:
            slot = ring_slot()
            sv = slot[:, 0:KC * ncols].re("p (k n) -> p k n", k=KC)
            P.dma("sp", sv, W[:, c0:c0 + ncols].re("(k p) n -> p k n", p=128))
            for j in range(ncols // 128):
                ps = pj()
                for k in range(KC):
                    P.matmul(ps[:, 0:ntok], sv[:, k, j * 128:(j + 1) * 128], xin[:, k, 0:ntok],
                             start=(k == 0), stop=(k == KC - 1))
                handler(c0 + j * 128, ps[:, 0:ntok])

    def proj_rows(rhs_list, W, handler):
        nK = len(rhs_list)
        for g in range(0, nK, 2):
            n = min(2, nK - g)
            slot = ring_slot()
            sv = slot[:, 0:n * 1024].re("p (k n) -> p k n", k=n)
            P.dma("sp", sv, W[g * 128:(g + n) * 128, :].re("(k p) n -> p k n", p=128))
            for kk in range(n):
                k = g + kk
                for dc in range(KC):
                    P.matmul(PB[dc], sv[:, kk, dc * 128:(dc + 1) * 128], rhs_list[k],
                             start=(k == 0), stop=(k == nK - 1))
        for dc in [KC - 1] + list(range(KC - 1)):
            handler(dc, PB[dc])

    def norm_partial(c):
        n = st.get("ncnt", 0)
        st["nps"] = PB[7]
        P.act(chv(qT, c), chv(hT, c), AF.Square)
        P.matmul(st["nps"], ones_b, chv(qT, c), start=(n == 0), stop=(n == KC - 1))
        st["ncnt"] = (n + 1) % KC
        if n == KC - 1:
            st["nready"] = True

    def rmsnorm(gcols, out):
        if not st.get("nready"):
            for c in range(KC):
                norm_partial(c)
        st["nready"] = False
        P.act(rstd, st["nps"], AF.Sqrt, scale=1.0 / D, bias=EPS)
        P.recip(rstd, rstd)
        for c in range(KC):
            P.stt("dve", out[:, c, :], chv(hT, c), gcols[:, c:c + 1], rstd, ALU.mult, ALU.mult)

    def groups(c0, n):
        g = []
        while n > 0:
            w = min(256, n); g.append((c0, w)); c0 += w; n -= w
        return g

    def ffn(l, which):
        P.section = which
        enter("ffn")
        rmsnorm(pc(l, which + "_norm"), xn)
        W = wb[which + "_w_in"][l]
        for g in range(FC // 2):
            def hg(c0, ps, g=g):
                j = (c0 - g * 256) // 128
                P.act(sg[j], ps, AF.Silu)
            proj_cols(xn, W, [(g * 256, 256)], hg)
            def hu(c0, ps, g=g):
                j = (c0 - DFF - g * 256) // 128
                P.tt("dve", hid[:, 2 * g + j, :], sg[j], ps, ALU.mult)
            proj_cols(xn, W, [(DFF + g * 256, 256)], hu)
        def ho(dc, ps):
            P.stt("dve", chv(hT, dc), ps, 0.5, chv(hT, dc), ALU.mult, ALU.add)
            norm_partial(dc)
        P.section = which + "_out"
        proj_rows([hid[:, k, :] for k in range(FC)], wb[which + "_w_out"][l], ho)

    def xattn(l):
        P.section = "xa"
        enter("xa")
        rmsnorm(pc(l, "xattn_norm"), xn)
        P.dma("sp", mkT, mk_d[l].re("p (k m) -> p k m", k=8))
        P.dma("sp", mvt, mv_d[l].re("p (k m) -> p k m", k=2))
        def hq(c0, ps):
            P.copy("act", qT[:, c0 // 128, :], ps)
        proj_cols(xn, wb["xattn_wq"][l], groups(0, D), hq)
        oT = xn
        for hh in range(4):
            for mb in range(2):
                ps = ms()
                for kk in range(2):
                    P.matmul(ps, mkT[:, 2 * hh + kk, mb * 128:(mb + 1) * 128], qT[:, 2 * hh + kk, :],
                             start=(kk == 0), stop=(kk == 1))
                P.act(exs[hh][:, mb, :], ps, AF.Exp, scale=1.0 / 16.0)
        for hh in range(4):
            ps = ms()
            for mb in range(2):
                P.matmul(ps, ones_b, exs[hh][:, mb, :], start=(mb == 0), stop=(mb == 1))
            P.recip(rden[hh], ps)
        for hh in range(4):
            for kk in range(2):
                ps = ms()
                for mb in range(2):
                    P.matmul(ps, mvt[:, mb, (2 * hh + kk) * 128:(2 * hh + kk + 1) * 128], exs[hh][:, mb, :],
                             start=(mb == 0), stop=(mb == 1))
                P.tt("dve", oT[:, 2 * hh + kk, :], ps, rden[hh], ALU.mult)
        def ho(dc, ps):
            P.tt("dve", chv(hT, dc), ps, chv(hT, dc), ALU.add)
            norm_partial(dc)
        proj_rows([oT[:, k, :] for k in range(KC)], wb["xattn_wo"][l], ho)

    cg = [carve("conv", f"cg{i}", [128, T], F32) for i in range(2)]
    zb = [carve("conv", f"zb{i}", [128, T + 2], F32) for i in range(2)]
    zc = [carve("conv", f"zc{i}", [128, T], F32) for i in range(2)]
    hq_ = carve("hg", "hq", [128, 3, T], BF16)
    hlf = carve("hg", "hlf", [128, 3, T], F32)
    hsg = carve("hg", "hsg", [128, 3, T], BF16)
    hgs = carve("hg", "hgs", [128, 3, T], BF16)
    hvT = carve("hg", "hvT", [128, 3, T], BF16)
    hqt = carve("hg", "hqt", [128, 3, T], BF16)
    hkt = carve("hg", "hkt", [128, 3, T], BF16)
    hgam = carve("hg", "hgam", [128, 3, 8], F32)
    hvtm = [carve("hg", f"hvtm{i}", [64, 384], BF16) for i in range(2)]
    hktm = [carve("hg", f"hktm{i}", [64, 384], BF16) for i in range(2)]
    hsc = carve("hg", "hsc", [64, 6, T], BF16)
    hA = carve("hg", "hA", [128, T], F32); hB = carve("hg", "hB", [128, T], F32)
    hC = carve("hg", "hC", [128, T], F32); hD = carve("hg", "hD", [128, T], F32)
    htS = carve("hg", "htS", [128, 3, 64], F32)
    of = carve("hg", "of", [128, T], F32)
    osq = carve("hg", "osq", [128, T], BF16)

    def conv_handlers(l):
        def h(ci, ps):
            i = ci % 2
            if ci in (2, 3):
                P.copy("act", cg[i], ps)
            elif ci in (4, 5):
                P.copy("pool", zb[i][:, 0:2], cz[l][:, i, :])
                P.tt("dve", zb[i][:, 2:T + 2], cg[i], ps, ALU.mult)
                P.copy("pool", cz[l][:, i, :], zb[i][:, T:T + 2])
                P.ts("dve", zc[i], zb[i][:, 2:T + 2], pc(l, "cw2", i), ALU.mult, pc(l, "cb", i), ALU.add)
                P.stt("dve", zc[i], zb[i][:, 1:T + 1], pc(l, "cw1", i), zc[i], ALU.mult, ALU.add)
                P.stt("dve", zc[i], zb[i][:, 0:T], pc(l, "cw0", i), zc[i], ALU.mult, ALU.add)
            else:
                P.tt("dve", ymix[:, i, :], ps, zc[i], ALU.mult)
        return h

    def hgrn_handler(l):
        def h(ci, ps):
            k = ci - L_HG0; i = k % 3; kind = k // 3
            if kind == 0:
                P.act(hq_[:, i, :], ps, AF.Silu)
            elif kind == 1:
                P.ts("dve", hA, ps, -80.0, ALU.max)
                P.act(hB, hA, AF.Exp, scale=-1.0)
                P.act(hC, hB, AF.Ln, bias=1.0)
                P.act(hD, hB, AF.Ln, scale=pc(l, "lb", i), bias=1.0)
                P.tt("pool", hlf[:, i, :], hD, hC, ALU.subtract)
                P.act(hsg[:, i, :], hA, AF.Sigmoid, scale=-1.0)
            elif kind == 2:
                P.copy("act", hvT[:, i, :], ps)
            else:
                P.act(hgs[:, i, :], ps, AF.Silu)
        return h

    def tm_transposes(srcs, dsts, c, C):
        for (src, dst) in zip(srcs, dsts):
            bank = ms(); bv = bfview(bank)
            for i in range(3):
                P.transpose(bv[0:C, i * 128:(i + 1) * 128], src[:, i, c * C:(c + 1) * C], ident_b)
            P.copy("act", dst, bv[0:C, 0:384])

    def hgrn_core(l):
        P.section = "hg_core"
        C = 64; NCH = T // C
        for i in range(3):
            P.scan(hlf[:, i, :], cst("r64"), hlf[:, i, :], 0.0, ALU.mult, ALU.add)
            P.act(hA, hlf[:, i, :], AF.Exp)
            P.tt("dve", hqt[:, i, :], hq_[:, i, :], hA, ALU.mult)
            P.copy("pool", hgam[:, i, :], hA[:, C - 1::C])
            P.act(hB, hlf[:, i, :], AF.Exp, scale=-1.0)
            P.stt("dve", hkt[:, i, :], hsg[:, i, :], pc(l, "olb", i), hB, ALU.mult, ALU.mult)
        if FLAGS.get("hg_stage", 9) < 2:
            P.memset("pool", ymix[:, 2:5, :], 0.0); return
        m64 = cst("m64is", 64)[:, 64:128]
        mb = V(m64.ap.unsqueeze(1).to_broadcast([C, NCH, C]), m64.bufs)
        for h in HORD:
            i = h // 2; r0 = (h % 2) * 64
            ps = ms()
            for c in range(NCH):
                P.matmul(ps[0:C, c * C:(c + 1) * C], hkt[r0:r0 + 64, i, c * C:(c + 1) * C],
                         hqt[r0:r0 + 64, i, c * C:(c + 1) * C])
            P.tt("dve", hsc[:, h, :].re("p (c t) -> p c t", t=C), ps[0:C, :].re("p (c t) -> p c t", t=C), mb, ALU.mult)
        if FLAGS.get("hg_stage", 9) < 3:
            P.memset("pool", ymix[:, 2:5, :], 0.0); return
        P.copy("act", Shb, Sh[l])
        psO = [PB[0], PB[1], PB[2]]

        def hg_pre(c):
            tm_transposes((hvT, hkt), (hvtm[c % 2], hktm[c % 2]), c, C)
            yield

        def hg_chain(c):
            vt = hvtm[c % 2]; kt = hktm[c % 2]
            for part in ("even", "odd1", "odd2"):
                for h in ((0, 2, 4) if part == "even" else (1, 3, 5)):
                    i = h // 2; r0 = (h % 2) * 64
                    o = psO[i][r0:r0 + 64, c * C:(c + 1) * C]
                    if part != "odd2":
                        P.matmul(o, Shb[r0:r0 + 64, i, :], hqt[r0:r0 + 64, i, c * C:(c + 1) * C], start=True, stop=False)
                    if part != "odd1":
                        P.matmul(o, vt[:, h * 64:(h + 1) * 64], hsc[:, h, c * C:(c + 1) * C], start=False, stop=True)
            psD = ms()
            for h in range(NHG):
                i = h // 2; r0 = (h % 2) * 64
                P.matmul(psD[r0:r0 + 64, i * 64:(i + 1) * 64], kt[:, h * 64:(h + 1) * 64], vt[:, h * 64:(h + 1) * 64])
            P.tt("dve", htS, Sh[l], psD[:, 0:192].re("p (i e) -> p i e", i=3), ALU.add)
            g = hgam[:, :, c:c + 1]
            gb = V(g.ap.to_broadcast([128, 3, 64]), g.bufs)
            P.tt("dve", Shb, htS, gb, ALU.mult)
            P.tt("dve", Sh[l], htS, gb, ALU.mult)
            yield

        def interleave(gs):
            gs = list(gs)
            while gs:
                for g_ in list(gs):
                    try:
                        next(g_)
                    except StopIteration:
                        gs.remove(g_)

        interleave([hg_pre(0)])
        for c in range(NCH):
            interleave(([hg_pre(c + 1)] if c + 1 < NCH else []) + [hg_chain(c)])
        if FLAGS.get("hg_stage", 9) < 4:
            P.memset("pool", ymix[:, 2:5, :], 0.0); return
        for i in range(3):
            P.act(osq, psO[i], AF.Square)
            P.copy("dve", of, psO[i])
            ps = ms()
            P.matmul(ps, blk_b, osq)
            P.act(hA, ps, AF.Sqrt, scale=1.0 / 64.0, bias=EPS)
            P.recip(hA, hA)
            P.tt("dve", hB, of, hA, ALU.mult)
            P.stt("dve", ymix[:, 2 + i, :], hB, pc(l, "hgn", i), hgs[:, i, :], ALU.mult, ALU.mult)

    praw = [carve("rw", f"praw{i}", [128, T + 1], F32) for i in range(2)]
    rr = carve("rw", "rr", [128, 3, T], BF16); kr = carve("rw", "kr", [128, 3, T], F32); vr = carve("rw", "vr", [128, 3, T], BF16)
    wab = carve("rw", "wab", [128, T], BF16); gsb = carve("rw", "gsb", [128, T], BF16)
    lw = carve("rw", "lw", [128, T], F32); aicl = carve("rw", "aicl", [128, T], F32)
    gg = carve("rw", "gg", [128, 3, T], BF16); bon = carve("rw", "bon", [128, 3, T], BF16)
    kkn = carve("rw", "kkn", [128, T], F32); kmod = carve("rw", "kmod", [128, T], F32)
    bcs = carve("rw", "bcs", [128, T], F32)
    AR = carve("rw", "AR", [128, 3, 8, 2, 64], BF16)
    KT = carve("rw", "KT", [128, 3, T], BF16); BT = carve("rw", "BT", [128, 3, T], BF16); VT = carve("rw", "VT", [128, 3, T], BF16)
    rgam = carve("rw", "rgam", [128, 3, 8], F32)
    NS = 4
    ktm = [carve("rw", f"ktm{i}", [64, 384], BF16) for i in range(NS)]
    btm = [carve("rw", f"btm{i}", [64, 384], BF16) for i in range(NS)]
    vtm = [carve("rw", f"vtm{i}", [64, 384], BF16) for i in range(NS)]
    SKs = [carve("rw", f"SK{i}", [64, 6, 128], BF16) for i in range(NS)]
    SBs = [carve("rw", f"SBr{i}", [64, 6, 64], BF16) for i in range(NS)]
    Rs = [carve("rw", f"Rf{i}", [64, 6, 64], DDT) for i in range(NS)]
    Mfs = [[carve("rw", f"Mf{m}{i}", [64, 6, 64], DDT) for i in range(2)] for m in range(2)]
    Mtfs = [[carve("rw", f"Mtf{m}{i}", [64, 6, 64], DDT) for i in range(2)] for m in range(2)]
    P1f = carve("rw", "P1f", [64, 384], DDT); Ub = carve("rw", "Ub", [64, 384], BF16)
    sqb = carve("rw", "sqb", [128, T], BF16)
    tA = carve("rw", "tA", [128, T], F32); tB = carve("rw", "tB", [128, T], F32)
    tC = carve("rw", "tC", [128, T], F32); tD = carve("rw", "tD", [128, T], F32)
    tS = carve("rw", "tS", [128, 3, 64], F32)
    pT1 = carve("rw", "pT1", [128, T], F32); pT2 = carve("rw", "pT2", [128, T], F32)
    sqb2 = carve("rw", "sqb2", [128, T], BF16)

    def rwkv_handler(l):
        def h(ci, ps):
            j = ci - L_RW0
            pr = praw[j % 2]
            P.copy("act", pr[:, 1:T + 1], ps)
            P.copy("pool", pr[:, 0:1], crw[l][:, j:j + 1])
            P.copy("pool", crw[l][:, j:j + 1], pr[:, T:T + 1])
            P.ts("dve", tA, pr[:, 1:T + 1], pc(l, "omu", j), ALU.mult)
            if j < 9:
                dst = (rr, kr, vr)[j // 3][:, j % 3, :]
            else:
                dst = tB
            P.stt("dve", dst, pr[:, 0:T], pc(l, "mu", j), tA, ALU.mult, ALU.add)
            if j == 9:
                P.act(wab[0:64, :], tB[0:64, :], AF.Tanh)
                P.copy("act", wab[64:128, :], tB[64:128, :])
            elif j == 10:
                P.act(gsb, tB, AF.Sigmoid)
        return h

    def rwkv_core(l):
        P.section = "rw_pre"
        C = 64; NCH = T // C
        for i in range(3):
            ps = ms(); P.matmul(ps, wa2[0:64, l, i * 128:(i + 1) * 128], wab[0:64, :])
            P.act(tA, ps, AF.Sigmoid, bias=pc(l, "w0", i))
            P.ts("dve", lw, tA, -0.606531, ALU.mult)
            ps = ms(); P.matmul(ps, wa2[64:128, l, i * 128:(i + 1) * 128], wab[64:128, :])
            P.act(aicl, ps, AF.Sigmoid, bias=pc(l, "a0", i))
            ps = ms(); P.matmul(ps, g2b[:, l, i * 128:(i + 1) * 128], gsb)
            P.copy("act", gg[:, i, :], ps)
            P.ts("dve", tB, kr[:, i, :], pc(l, "k_k", i), ALU.mult)
            P.act(sqb, tB, AF.Square)
            ps = ms(); P.matmul(ps, blk_b, sqb)
            P.act(tC, ps, AF.Sqrt)
            P.ts("dve", tC, tC, 1e-12, ALU.max)
            P.recip(tC, tC)
            P.tt("dve", kkn, tB, tC, ALU.mult)
            P.ts("dve", pT1, aicl, pc(l, "k_a", i), ALU.mult, pc(l, "oka", i), ALU.add)
            P.tt("dve", kmod, kr[:, i, :], pT1, ALU.mult)
            P.tt("dve", pT1, rr[:, i, :], kmod, ALU.mult)
            P.ts("dve", sqb2, pT1, pc(l, "r_k", i), ALU.mult)
            ps = ms(); P.matmul(ps, blk_b, sqb2)
            P.tt("dve", bon[:, i, :], ps, vr[:, i, :], ALU.mult)
            P.scan(bcs, cst("r64"), lw, 0.0, ALU.mult, ALU.add)
            P.act(tA, bcs, AF.Exp)
            P.tt("dve", AR[:, i, :, 1, :], rr[:, i, :].re("p (c t) -> p c t", t=C), tA.re("p (c t) -> p c t", t=C), ALU.mult)
            P.copy("pool", rgam[:, i, :], tA[:, C - 1::C])
            P.tt("dve", tD, bcs, lw, ALU.subtract)
            P.act(tD, tD, AF.Exp)
            P.stt("dve", AR[:, i, :, 0, :], kkn.re("p (c t) -> p c t", t=C), -1.0, tD.re("p (c t) -> p c t", t=C), ALU.mult, ALU.mult)
            P.act(tC, bcs, AF.Exp, scale=-1.0)
            P.tt("dve", KT[:, i, :], kmod, tC, ALU.mult)
            P.tt("dve", pT2, kkn, aicl, ALU.mult)
            P.tt("dve", BT[:, i, :], pT2, tC, ALU.mult)
            P.copy("act", VT[:, i, :], vr[:, i, :])
        P.copy("act", Srb, Sr[l])
        psY = [PB[0], PB[1], PB[2]]
        mis = cst("m64is", 64); msk_s = cst("m64s", 64); msk_l = cst("m64l", 64); id64 = cst("id64", 64)
        bc3 = lambda m, n: V(m.ap.unsqueeze(1).to_broadcast([64, n, m.ap.shape[1]]), m.bufs)
        def rw_pre(c):
            P.section = "rw_chain"
            sx = c % NS; m = c % 2
            kt_ = ktm[sx]; bt_ = btm[sx]; vt_ = vtm[sx]
            SK = SKs[sx]; SBr = SBs[sx]; Rf = Rs[sx]; Mf = Mfs[m]; Mtf = Mtfs[m]
            tm_transposes((KT, BT, VT), (kt_, bt_, vt_), c, C)
            yield
            X1 = [ms(), ms()]
            for h in HORD:
                i = h // 2; r0 = (h % 2) * 64
                P.matmul(X1[h // 4][0:64, (h % 4) * 128:(h % 4 + 1) * 128], KT[r0:r0 + 64, i, c * C:(c + 1) * C],
                         AR[r0:r0 + 64, i, c, :, :].re("p a t -> p (a t)"))
            P.tt("dve", SK[:, 0:4, :], X1[0][0:64, :].re("p (h n) -> p h n", h=4), bc3(mis, 4), ALU.mult)
            P.tt("dve", SK[:, 4:6, :], X1[1][0:64, 0:256].re("p (h n) -> p h n", h=2), bc3(mis, 2), ALU.mult)
            yield
            X2 = [ms(), ms()]
            for h in HORD:
                i = h // 2; r0 = (h % 2) * 64
                P.matmul(X2[h // 4][0:64, (h % 4) * 128:(h % 4 + 1) * 128], BT[r0:r0 + 64, i, c * C:(c + 1) * C],
                         AR[r0:r0 + 64, i, c, :, :].re("p a t -> p (a t)"))
            for (bk, h0, nh) in ((X2[0], 0, 4), (X2[1], 4, 2)):
                v4 = bk[0:64, 0:nh * 128].re("p (h n) -> p h n", h=nh)
                P.tt("dve", Mf[0][:, h0:h0 + nh, :], v4[:, :, 0:64], bc3(msk_s, nh), ALU.mult)
                P.tt("dve", SBr[:, h0:h0 + nh, :], v4[:, :, 64:128], bc3(mis[:, 64:128], nh), ALU.mult)
            X3 = ms()
            for h in HORD:
                i = h // 2; r0 = (h % 2) * 64
                P.matmul(X3[0:64, h * 64:(h + 1) * 64], AR[r0:r0 + 64, i, c, 0, :], BT[r0:r0 + 64, i, c * C:(c + 1) * C])
            P.tt("dve", Mtf[0], X3[0:64, 0:384].re("p (h n) -> p h n", h=6), bc3(msk_l, 6), ALU.mult)
            P.tt("dve", Rf, Mf[0], bc3(id64, 6), ALU.add)
            yield
            idb64 = ident_b[0:64, 0:64]
            def sq_mm(cur, want_m):
                pMt = ms()
                for h in range(NRW):
                    P.matmul(pMt[0:64, h * 64:(h + 1) * 64], Mf[cur][:, h, :], Mtf[cur][:, h, :])
                pM = None
                if want_m:
                    pM = ms()
                    for h in range(NRW):
                        P.matmul(pM[0:64, h * 64:(h + 1) * 64], Mtf[cur][:, h, :], Mf[cur][:, h, :])
                return pMt, pM
            def sq_ev(pMt, pM, nxt):
                P.copy("dve", Mtf[nxt], pMt[0:64, 0:384].re("p (h n) -> p h n", h=6))
                if pM is not None:
                    P.copy("act", Mf[nxt], pM[0:64, 0:384].re("p (h n) -> p h n", h=6))
            def r_mm(nxt):
                pR = ms()
                for h in range(NRW):
                    P.matmul(pR[0:64, h * 64:(h + 1) * 64], Mtf[nxt][:, h, :], Rf[:, h, :])
                return pR
            def r_ev(pR):
                P.tt("dve", Rf, Rf, pR[0:64, 0:384].re("p (h n) -> p h n", h=6), ALU.add)
            cur = 0
            pMt, pM = sq_mm(cur, True)
            sq_ev(pMt, pM, 1 - cur)
            yield
            for lev in range(5):
                nxt = 1 - cur
                pR = r_mm(nxt)
                if lev < 4:
                    pMt, pM = sq_mm(nxt, lev < 3)
                r_ev(pR)
                if lev < 4:
                    sq_ev(pMt, pM, cur)
                yield
                cur = nxt

        def rw_chain(c):
            P.section = "rw_chain2"
            sx = c % NS
            kt_ = ktm[sx]; bt_ = btm[sx]; vt_ = vtm[sx]
            SK = SKs[sx]; SBr = SBs[sx]; Rf = Rs[sx]
            pP = ms()
            for h in HORD:
                i = h // 2; r0 = (h % 2) * 64
                o = pP[0:64, h * 64:(h + 1) * 64]
                P.matmul(o, AR[r0:r0 + 64, i, c, 0, :], Srb[r0:r0 + 64, i, :], start=True, stop=False)
                P.matmul(o, SK[:, h, 0:64], vt_[:, h * 64:(h + 1) * 64], start=False, stop=True)
            P.copy("act", P1f, pP[0:64, 0:384])
            yield
            pU = ms()
            for h in range(NRW):
                P.matmul(pU[0:64, h * 64:(h + 1) * 64], Rf[:, h, :], P1f[:, h * 64:(h + 1) * 64])
            P.copy("dve", Ub, pU[0:64, 0:384])
            yield
            for part in ("even", "odd1", "odd2"):
                for h in ((0, 2, 4) if part == "even" else (1, 3, 5)):
                    i = h // 2; r0 = (h % 2) * 64
                    o = psY[i][r0:r0 + 64, c * C:(c + 1) * C]
                    if part != "odd2":
                        P.matmul(o, Srb[r0:r0 + 64, i, :], AR[r0:r0 + 64, i, c, 1, :], start=True, stop=False)
                    if part != "odd1":
                        P.matmul(o, Ub[:, h * 64:(h + 1) * 64], SBr[:, h, :], start=False, stop=False)
                        P.matmul(o, vt_[:, h * 64:(h + 1) * 64], SK[:, h, 64:128], start=False, stop=True)
            pD = ms()
            for h in range(NRW):
                i = h // 2; r0 = (h % 2) * 64
                o = pD[r0:r0 + 64, i * 64:(i + 1) * 64]
                P.matmul(o, bt_[:, h * 64:(h + 1) * 64], Ub[:, h * 64:(h + 1) * 64], start=True, stop=False)
                P.matmul(o, kt_[:, h * 64:(h + 1) * 64], vt_[:, h * 64:(h + 1) * 64], start=False, stop=True)
            P.tt("dve", tS, Sr[l], pD[:, 0:192].re("p (i e) -> p i e", i=3), ALU.add)
            g = rgam[:, :, c:c + 1]
            gb = V(g.ap.to_broadcast([128, 3, 64]), g.bufs)
            P.tt("dve", Srb, tS, gb, ALU.mult)
            P.tt("dve", Sr[l], tS, gb, ALU.mult)
            yield

        def seq(*gs):
            for g_ in gs:
                yield from g_

        def interleave(gs):
            gs = list(gs)
            while gs:
                for g_ in list(gs):
                    try:
                        next(g_)
                    except StopIteration:
                        gs.remove(g_)

        if FLAGS.get("rw_pipe", True):
            interleave([rw_pre(0), rw_pre(1)])
            for k in range(1, NCH // 2):
                interleave([rw_pre(2 * k), rw_pre(2 * k + 1), seq(rw_chain(2 * k - 2), rw_chain(2 * k - 1))])
            interleave([seq(rw_chain(NCH - 2), rw_chain(NCH - 1))])
        else:
            for c in range(NCH):
                interleave([seq(rw_pre(c), rw_chain(c))])
        P.section = "rw_post"
        for i in range(3):
            P.copy("dve", tA, psY[i])
            P.copy("act", sqb, tA)
            ps = ms(); P.matmul(ps, blk_b, sqb)
            P.stt("dve", tB, ps, -1.0 / 64.0, tA, ALU.mult, ALU.add)
            P.act(sqb, tB, AF.Square)
            ps = ms(); P.matmul(ps, blk_b, sqb)
            P.act(tC, ps, AF.Sqrt, scale=1.0 / 64.0, bias=64e-5)
            P.recip(tC, tC)
            P.tt("dve", tB, tB, tC, ALU.mult)
            P.ts("pool", pT1, tB, pc(l, "ln_w", i), ALU.mult, pc(l, "ln_b", i), ALU.add)
            P.tt("pool", pT1, pT1, bon[:, i, :], ALU.add)
            P.tt("pool", ymix[:, 5 + i, :], pT1, gg[:, i, :], ALU.mult)

    def mixer(l):
        P.section = "mix_in"
        rmsnorm(pc(l, "mix_norm"), xn)
        W = wb["w_mix_in"][l]
        enter("conv")
        ch = conv_handlers(l)
        proj_cols(xn, W, [(256, 256), (512, 256), (0, 256)], lambda c0, ps: ch(c0 // 128, ps))
        if FLAGS["hgrn"]:
            enter("hg")
            P.section = "hg_in"
            hh = hgrn_handler(l)
            proj_cols(xn, W, groups(L_HG0 * 128, 12 * 128), lambda c0, ps: hh(c0 // 128, ps))
            hgrn_core(l)
        else:
            P.memset("pool", ymix[:, 2:5, :], 0.0)
        if FLAGS["rwkv"]:
            enter("rw")
            P.section = "rw_in"
            rh = rwkv_handler(l)
            proj_cols(xn, W, groups(L_RW0 * 128, 11 * 128), lambda c0, ps: rh(c0 // 128, ps))
            rwkv_core(l)
        else:
            P.memset("pool", ymix[:, 5:8, :], 0.0)
        def ho(dc, ps):
            P.tt("dve", chv(hT, dc), ps, chv(hT, dc), ALU.add)
            norm_partial(dc)
        P.section = "mix_out"
        proj_rows([ymix[:, k, :] for k in range(KC)], wb["w_mix_out"][l], ho)

    memt = xs[:, 0:2, :]
    P.dma("sp", memt, mem_d.rearrange("(b p) d -> p b d", p=128))
    mss = P.tile("mss", [128, 2], F32)
    gens["pro"] = {"off": gens["xa"]["off"], "vs": []}
    msq = carve("pro", "msq", [128, D], BF16)
    for b in range(2):
        P.act(msq, memt[:, b, :], AF.Square, accum_out=mss[:, b:b + 1])
    P.act(mss, mss, AF.Sqrt, scale=1.0 / D, bias=EPS)
    P.recip(mss, mss)
    memn = carve("pro", "memn", [128, 2, D], F32)
    for b in range(2):
        P.ts("dve", memn[:, b, :], memt[:, b, :], mss[:, b:b + 1], ALU.mult)
    memT = carve("pro", "memT", [128, KC, MEM], F32)
    for k in range(KC):
        ps = ms()
        for b in range(2):
            P.transpose(ps[:, b * 128:(b + 1) * 128], memn[:, b, k * 128:(k + 1) * 128], ident_f)
        P.copy("act", memT[:, k, :], ps[:, 0:256])
    memg = qT[:, :, 0:MEM]
    mks = ymix[:, :, 0:MEM]
    for l in range(L):
        for k in range(KC):
            P.ts("dve", memg[:, k, :], memT[:, k, :], pc(l, "mem_norm", k), ALU.mult)
        def hk(c0, ps):
            P.copy("act", mks[:, c0 // 128, :], ps)
        proj_cols(memg, wb["xattn_wkv"][l], groups(0, D), hk, ntok=MEM)
        P.dma("sp", mk_d[l].re("p (k m) -> p k m", k=8), mks)
        for (c0, ncols) in groups(D, D):
            slot = ring_slot()
            sv = slot[:, 0:KC * ncols].re("p (k n) -> p k n", k=KC)
            P.dma("sp", sv, wb["xattn_wkv"][l][:, c0:c0 + ncols].re("(k p) n -> p k n", p=128))
            for b in range(2):
                ps = pj()
                for k in range(KC):
                    P.matmul(ps[:, 0:ncols], memg[:, k, b * 128:(b + 1) * 128], sv[:, k, :], start=(k == 0), stop=(k == KC - 1))
                P.copy("act", mvt[:, b, c0 - D:c0 - D + ncols], ps[:, 0:ncols])
        P.dma("sp", mv_d[l].re("p (k m) -> p k m", k=2), mvt)

    for t in range(NT):
        P.section = "io"
        P.dma("sp", xs, x_d[t * T:(t + 1) * T, :].rearrange("(s p) d -> p s d", p=128))
        for c in range(KC):
            ps = pj()
            for s in range(4):
                P.transpose(ps[:, s * 128:(s + 1) * 128], xs[:, s, c * 128:(c + 1) * 128], ident_f)
            P.copy("act" if c % 2 else "dve", chv(hT, c), ps)
            norm_partial(c)
        for l in range(L):
            P.phase += 1
            if FLAGS["ffn"]:
                ffn(l, "ffn1")
            if FLAGS["mix"]:
                mixer(l)
            if FLAGS["xa"]:
                xattn(l)
            if FLAGS["ffn"]:
                ffn(l, "ffn2")
        P.phase += 1
        P.section = "io"
        fo = L * NPL
        if not st.get("nready"):
            for c in range(KC):
                norm_partial(c)
        st["nready"] = False
        P.act(rstd, st["nps"], AF.Sqrt, scale=1.0 / D, bias=EPS)
        P.recip(rstd, rstd)
        for c in range(KC):
            P.stt("dve", hT[:, c, :], hT[:, c, :], par[:, fo + c:fo + c + 1], rstd, ALU.mult, ALU.mult)
        for s in range(4):
            for half in range(2):
                ps = pj()
                for cc in range(4):
                    c = half * 4 + cc
                    P.transpose(ps[:, cc * 128:(cc + 1) * 128], hT[:, c, s * 128:(s + 1) * 128], ident_f)
                P.copy("act" if half else "dve", xs[:, s, half * 512:(half + 1) * 512], ps)
        P.dma("sp", out_d[t * T:(t + 1) * T, :].re("(s p) d -> p s d", p=128), xs)
    P.wait_dma_final("sp", out_d)
    stats = P.finish()
    P.close()
    if SECLOG is not None:
        SECLOG.update(P.seclog)
    return nc, stats


SECLOG = None
FLAGS = {"ffn": True, "mix": True, "xa": True, "hgrn": True, "rwkv": True}


def kernel(**inp):
    x = np.asarray(inp["x"], np.float32)
    B, S, _ = x.shape
    L = inp["ffn1_norm"].shape[0]
    nc, stats = build(S, L)
    par = np.zeros((128, L * NPL + 8), np.float32)
    for l in range(L):
        def put(n, a):
            o, w = PL[n]; par[:, l * NPL + o:l * NPL + o + w] = a
        for n in ("ffn1_norm", "mix_norm", "xattn_norm", "ffn2_norm", "mem_norm"):
            put(n, fm(inp[n][l]))
        for k in range(3):
            put(f"cw{k}", fm(inp["conv_w"][l][k]))
        put("cb", fm(inp["conv_b"][l])); put("lbl", fm(inp["hgrn_lb_logits"][l])); put("hgn", fm(inp["hgrn_norm"][l]))
        put("mu", fm(inp["rwkv_mu"][l])); put("w0", fm(inp["rwkv_w0"][l])); put("a0", fm(inp["rwkv_a0"][l]))
        put("k_k", fm(inp["rwkv_k_k"][l])); put("k_a", fm(inp["rwkv_k_a"][l])); put("r_k", fm(inp["rwkv_r_k"][l]))
        put("ln_w", fm(inp["rwkv_ln_w"][l])); put("ln_b", fm(inp["rwkv_ln_b"][l]))
    par[:, L * NPL:L * NPL + 8] = fm(inp["final_norm"])
    con = make_consts()
    shared = {"params": par, "consts": con}
    for n in ("ffn1_w_in", "ffn1_w_out", "w_mix_in", "w_mix_out", "xattn_wq", "xattn_wkv", "xattn_wo",
              "ffn2_w_in", "ffn2_w_out", "rwkv_w2", "rwkv_a2", "rwkv_g2"):
        shared[n] = np.ascontiguousarray(np.asarray(inp[n], np.float32))
    mem = np.asarray(inp["mem"], np.float32)
    in_maps = []
    for b in range(B):
        m = dict(shared)
        m["x"] = np.ascontiguousarray(x[b]); m["mem"] = np.ascontiguousarray(mem[b])
        in_maps.append(m)
    res = run_bass_kernel_spmd(nc, in_maps, core_ids=list(range(B)))
    return np.stack([np.asarray(r["out"], np.float32) for r in res.results], 0)
```

```python
from concourse.bass_utils import run_bass_kernel_spmd
import numpy as np
from contextlib import ExitStack
import concourse.bass as bass
import concourse.mybir as mybir

F32 = mybir.dt.float32
BF16 = mybir.dt.bfloat16
AF = mybir.ActivationFunctionType
ALU = mybir.AluOpType
AX = mybir.AxisListType

COMPUTE = ("pe", "act", "dve", "pool")
NPH = 4
STRICT = True


class Buf:
    __slots__ = ("name", "writer", "readers", "sem", "cnt", "excl")

    def __init__(self, name, excl=False):
        self.name = name
        self.excl = excl
        self.writer = None
        self.readers = []
        self.sem = None
        self.cnt = 0


class V:
    __slots__ = ("ap", "bufs")

    def __init__(self, ap, bufs):
        self.ap = ap
        self.bufs = bufs

    def __getitem__(self, key):
        return V(self.ap[key], self.bufs)

    def re(self, s, **kw):
        return V(self.ap.rearrange(s, **kw), self.bufs)

    def bc(self, shape):
        return V(self.ap.to_broadcast(shape), self.bufs)


class Op:
    __slots__ = ("eng", "fn", "deps", "dwaits", "signal", "signum", "dma", "tok", "idx", "ph")

    def __init__(self, eng, fn, dma=False):
        self.eng = eng
        self.fn = fn
        self.deps = []
        self.dwaits = {}
        self.signal = False
        self.signum = 0
        self.dma = dma
        self.tok = None
        self.idx = 0
        self.ph = 0


class Prog:
    def __init__(self, nc):
        self.nc = nc
        self.es = ExitStack()
        self.ops = {e: [] for e in ("pe", "act", "dve", "pool", "sp")}
        self.sems = {}
        self.nbuf = 0
        self.final_waits = []
        self.phase = 0
        self.section = ""
        self.seclog = None

    def sbuf(self, name, shape, dtype, nsub=1):
        t = self.es.enter_context(self.nc.sbuf_tensor(name, list(shape), dtype))
        bufs = [Buf(f"{name}.{i}") for i in range(nsub)]
        return t, bufs

    def tile(self, name, shape, dtype):
        t, bufs = self.sbuf(name, shape, dtype)
        return V(t[tuple(slice(None) for _ in shape)], bufs)

    def psum(self, name, shape, dtype=F32):
        t = self.es.enter_context(self.nc.psum_tensor(name, list(shape), dtype))
        return V(t[tuple(slice(None) for _ in shape)], [Buf(name, excl=True)])

    def dram(self, name, shape, dtype, kind="Internal"):
        t = self.nc.dram_tensor(name, list(shape), dtype, kind=kind)
        return V(t.ap(), [Buf(name)])

    def newsem(self, name):
        s = self.es.enter_context(self.nc.semaphore(name))
        return s

    def _dep(self, B, A, kind):
        if A is None or A is B:
            return
        if A.dma:
            sem, val = A.tok
            if B.dwaits.get(sem, (None, 0))[1] < val:
                B.dwaits[sem] = (sem, val)
            return
        if A.eng == B.eng:
            if B.eng == "pe" or (kind != "RAW" and not STRICT):
                return
        A.signal = True
        B.deps.append(A)

    def _track(self, op, reads, writes):
        for b in reads:
            self._dep(op, b.writer, "RAW")
            if b.excl:
                for r in b.readers:
                    if r.eng != op.eng:
                        self._dep(op, r, "RAR")
        for b in writes:
            self._dep(op, b.writer, "WAW")
            for r in b.readers:
                self._dep(op, r, "WAR")
        for b in reads:
            b.readers.append(op)
        for b in writes:
            b.writer = op
            b.readers = []

    @staticmethod
    def _bufs(vs):
        out = []
        for v in vs:
            if isinstance(v, V):
                for b in v.bufs:
                    if b not in out:
                        out.append(b)
        return out

    def emit(self, eng, fn, reads, writes):
        op = Op(eng, fn)
        op.ph = self.phase % NPH
        if self.seclog is not None:
            self.seclog[eng].append(self.section)
        self._track(op, self._bufs(reads), self._bufs(writes))
        op.idx = len(self.ops[eng])
        self.ops[eng].append(op)
        return op

    def dma(self, eng, out, in_, **kw):
        op = Op(eng, None, dma=True)
        rb = self._bufs([in_])
        wb = self._bufs([out])
        self._track(op, rb, wb)
        owner = wb[0] if wb else rb[0]
        if owner.sem is None:
            owner.sem = self.newsem("d_" + owner.name.replace(".", "_"))
        owner.cnt += 16
        op.tok = (owner.sem, owner.cnt)
        oap = out.ap if isinstance(out, V) else out
        iap = in_.ap if isinstance(in_, V) else in_
        sem = owner.sem
        op.fn = lambda e: e.dma_start(out=oap, in_=iap, **kw).then_inc(sem, 16)
        op.idx = len(self.ops[eng])
        self.ops[eng].append(op)
        return op

    def wait_dma_final(self, eng, v):
        for b in v.bufs:
            if b.writer is not None and b.writer.dma:
                self.final_waits.append((eng, b.writer.tok))

    @staticmethod
    def _a(x):
        return x.ap if isinstance(x, V) else x

    def matmul(self, out, lhsT, rhs, start=True, stop=True, **kw):
        o, l, r = self._a(out), self._a(lhsT), self._a(rhs)
        op = self.emit("pe", lambda e: e.matmul(o, l, r, start=start, stop=stop, **kw),
                       [lhsT, rhs], [out])
        self._pe_rowgroup(op, l)
        if self.seclog is not None and l.dtype == F32:
            self.seclog["pe"].append(self.section)
        return op

    def _pe_rowgroup(self, op, lhs_ap):
        rg = (lhs_ap.base_partition(), min(128, ((lhs_ap.shape[0] + 31) // 32) * 32))
        prev = getattr(self, "_last_pe", None)
        if prev is not None and prev[1] != rg:
            prev[0].signal = True
            op.deps.append(prev[0])
        self._last_pe = (op, rg)

    def transpose(self, out, in_, ident):
        o, i, d = self._a(out), self._a(in_), self._a(ident)
        op = self.emit("pe", lambda e: e.transpose(o, i, d), [in_, ident], [out])
        self._pe_rowgroup(op, i)
        return op

    def act(self, out, in_, func, bias=None, scale=1.0, accum_out=None, eng="act"):
        o, i = self._a(out), self._a(in_)
        b = self._a(bias) if bias is not None else None
        s = self._a(scale)
        acc = self._a(accum_out) if accum_out is not None else None
        kw = {}
        if b is not None:
            kw["bias"] = b
        if acc is not None:
            kw["accum_out"] = acc
        return self.emit("act", lambda e: e.activation(o, i, func, scale=s, **kw),
                         [in_, bias, scale], [out, accum_out])

    def tt(self, eng, out, in0, in1, op):
        o, a, b = self._a(out), self._a(in0), self._a(in1)
        return self.emit(eng, lambda e: e.tensor_tensor(o, a, b, op), [in0, in1], [out])

    def ts(self, eng, out, in0, s1, op0, s2=None, op1=None, accum_out=None):
        o, a = self._a(out), self._a(in0)
        x1, x2 = self._a(s1), self._a(s2)
        acc = self._a(accum_out) if accum_out is not None else None
        kw = {}
        if op1 is not None:
            kw["op1"] = op1
        if acc is not None:
            kw["accum_out"] = acc
        return self.emit(eng, lambda e: e.tensor_scalar(o, a, x1, x2, op0, **kw),
                         [in0, s1, s2], [out, accum_out])

    def stt(self, eng, out, in0, scalar, in1, op0, op1):
        o, a, s, b = self._a(out), self._a(in0), self._a(scalar), self._a(in1)
        return self.emit(eng, lambda e: e.scalar_tensor_tensor(o, a, s, b, op0, op1),
                         [in0, scalar, in1], [out])

    def copy(self, eng, out, in_):
        o, i = self._a(out), self._a(in_)
        if eng == "act":
            return self.emit("act", lambda e: e.copy(o, i), [in_], [out])
        return self.emit(eng, lambda e: e.tensor_copy(o, i), [in_], [out])

    def memset(self, eng, out, val):
        o = self._a(out)
        return self.emit(eng, lambda e: e.memset(o, val), [], [out])

    def scan(self, out, d0, d1, initial, op0, op1):
        o, a, b, i = self._a(out), self._a(d0), self._a(d1), self._a(initial)
        return self.emit("dve", lambda e: e.tensor_tensor_scan(o, a, b, i, op0, op1),
                         [d0, d1, initial], [out])

    def recip(self, out, in_):
        o, i = self._a(out), self._a(in_)
        return self.emit("dve", lambda e: e.reciprocal(o, i), [in_], [out])

    def generic(self, eng, fn, reads, writes):
        return self.emit(eng, fn, reads, writes)

    def finish(self):
        nc = self.nc
        esem = {(e, k): self.newsem(f"s_{e}{k}") for e in COMPUTE for k in range(NPH)}
        for e in COMPUTE:
            n = [0] * NPH
            for op in self.ops[e]:
                if op.signal and not op.dma:
                    n[op.ph] += 1
                    op.signum = n[op.ph]
        engobj = {"pe": "tensor", "act": "scalar", "dve": "vector", "pool": "gpsimd", "sp": "sync"}
        stats = {}
        with nc.Block() as block:
            for ename in ("sp", "pool", "act", "dve", "pe"):
                ops = self.ops[ename]
                finals = [t for (e, t) in self.final_waits if e == ename]
                if not ops and not finals:
                    continue

                def body(eng, ops=ops, ename=ename, finals=finals):
                    seen = {}
                    nw = 0
                    for op in ops:
                        need = {}
                        for A in op.deps:
                            k = esem[(A.eng, A.ph)]
                            if need.get(k, 0) < A.signum:
                                need[k] = A.signum
                        for sem, val in op.dwaits.values():
                            if need.get(sem, 0) < val:
                                need[sem] = val
                        for k, val in need.items():
                            if seen.get(k, 0) < val:
                                eng.wait_ge(k, val)
                                seen[k] = val
                                nw += 1
                        ins = op.fn(eng)
                        if op.signal and not op.dma:
                            ins.then_inc(esem[(ename, op.ph)], 1)
                    for sem, val in finals:
                        eng.wait_ge(sem, val)
                    stats[ename] = (len(ops), nw)

                getattr(block, engobj[ename])(body)
        self.stats = stats
        return stats

    def close(self):
        self.es.close()


D = 1024; KC = 8; DFF = 2816; FC = 22; T = 512; MEM = 256
DDT = BF16
EPS = 1e-6
L_CONV0, L_HG0, L_RW0 = 0, 6, 18
NHG = 6; NRW = 6
HORD = [0, 2, 4, 1, 3, 5]

PL = {}
_o = 0
for _n, _w in [("ffn1_norm", 8), ("mix_norm", 8), ("xattn_norm", 8), ("ffn2_norm", 8), ("mem_norm", 8),
               ("cw0", 2), ("cw1", 2), ("cw2", 2), ("cb", 2), ("lbl", 3), ("hgn", 3), ("mu", 11),
               ("w0", 3), ("a0", 3), ("k_k", 3), ("k_a", 3), ("r_k", 3), ("ln_w", 3), ("ln_b", 3),
               ("omu", 11), ("oka", 3), ("lb", 3), ("olb", 3)]:
    PL[_n] = (_o, _w); _o += _w
NPL = _o
CL = {}
_o = 0
for _n, _w in [("ident", 128), ("blk64", 128), ("ones", 128), ("m32i", 32), ("m64is", 128), ("m64s", 64),
               ("m64l", 64), ("r32", 512), ("r64", 512), ("id64", 64)]:
    CL[_n] = (_o, _w); _o += _w
NCL = _o


def make_consts():
    c = np.zeros((128, NCL), np.float32)
    def put(n, a):
        o, w = CL[n]; c[:a.shape[0], o:o + w] = a
    put("ident", np.eye(128))
    b = np.zeros((128, 128)); b[:64, :64] = 1; b[64:, 64:] = 1
    put("blk64", b)
    put("ones", np.ones((128, 128)))
    s32 = np.arange(32)
    put("m32i", (s32[:, None] <= s32[None, :]).astype(np.float32))
    s64 = np.arange(64)
    strict = (s64[:, None] < s64[None, :]).astype(np.float32)
    incl = (s64[:, None] <= s64[None, :]).astype(np.float32)
    put("m64is", np.concatenate([strict, incl], 1))
    put("m64s", strict)
    put("m64l", strict.T.copy())
    r = np.ones((128, 512)); r[:, ::32] = 0; put("r32", r)
    r = np.ones((128, 512)); r[:, ::64] = 0; put("r64", r)
    put("id64", np.eye(64))
    return c


def fm(v):
    v = np.asarray(v, np.float32).reshape(-1)
    return np.ascontiguousarray(v.reshape(-1, 128).T)


def build(S, L):
    NT = S // T
    nc = bass.Bass("TRN2", target_bir_lowering=False)
    P = Prog(nc)
    if SECLOG is not None:
        P.seclog = {e: [] for e in ("pe", "act", "dve", "pool", "sp")}
    ein = lambda n, sh: nc.dram_tensor(n, list(sh), F32, kind="ExternalInput").ap()
    x_d = ein("x", [S, D]); mem_d = ein("mem", [MEM, D])
    par_d = ein("params", [128, L * NPL + 8]); con_d = ein("consts", [128, NCL])
    wnames = [("ffn1_w_in", D, 2 * DFF), ("ffn1_w_out", DFF, D), ("w_mix_in", D, 3712), ("w_mix_out", D, D),
              ("xattn_wq", D, D), ("xattn_wkv", D, 2 * D), ("xattn_wo", D, D),
              ("ffn2_w_in", D, 2 * DFF), ("ffn2_w_out", DFF, D)]
    wf = {n: ein(n, [L, a, b]) for n, a, b in wnames}
    w2_d = ein("rwkv_w2", [L, 64, 384]); a2_d = ein("rwkv_a2", [L, 64, 384]); g2_d = ein("rwkv_g2", [L, 128, 384])
    out_d = P.dram("out", [S, D], F32, kind="ExternalOutput")
    wb = {}
    for n, a, b in wnames:
        t = nc.dram_tensor(n + "_b", [L, a, b], BF16, kind="Internal").ap()
        wb[n] = [V(t[l], [Buf(f"w_{n}{l}")]) for l in range(L)]
    mk_d = [P.dram(f"mk_d{l}", [128, 8 * 256], BF16) for l in range(L)]
    mv_d = [P.dram(f"mv_d{l}", [128, 2 * 1024], BF16) for l in range(L)]

    par = P.tile("par", [128, L * NPL + 8], F32)
    con = P.tile("con", [128, NCL], F32)
    cb = P.tile("cb", [128, 128 * 3], BF16)
    ident_b = cb[:, 0:128]; blk_b = cb[:, 128:256]; ones_b = cb[:, 256:384]
    ident_f = con[:, CL["ident"][0]:CL["ident"][0] + 128]
    def cst(n, rows=128):
        o, w = CL[n]; return con[0:rows, o:o + w]
    def pc(l, n, i=None):
        o, w = PL[n]; o += l * NPL
        return par[:, o:o + w] if i is None else par[:, o + i:o + i + 1]
    wa2 = P.tile("wa2", [128, L, 384], BF16)
    g2b = P.tile("g2b", [128, L, 384], BF16)
    def mtile(name, shape, dt, n):
        t, bufs = P.sbuf(name, shape, dt, nsub=n)
        return V(t[tuple(slice(None) for _ in shape)], bufs)
    def chv(v, c):
        return V(v.ap[:, c, :], [v.bufs[c]])
    hT = mtile("hT", [128, KC, T], F32, KC)
    xn = P.tile("xn", [128, KC, T], BF16)
    ARENA = 100 * 1024
    arena_t, _ = P.sbuf("arena", [128, ARENA // 2], BF16)
    gens = {}
    def carve(gen, name, shape, dt):
        g = gens.setdefault(gen, {"off": 0, "vs": []})
        esz = 4 if dt == F32 else 2
        n = 1
        for d in shape[1:]:
            n *= d
        nbytes = (n * esz + 63) // 64 * 64
        o = g["off"]; g["off"] += nbytes
        assert g["off"] <= ARENA, (gen, name, g["off"])
        ap = arena_t[0:shape[0], o // 2:(o + n * esz) // 2]
        if dt == F32:
            ap = ap.bitcast(F32)
        if len(shape) > 2:
            names = " ".join(f"d{i}" for i in range(1, len(shape)))
            ap = ap.rearrange(f"p ({names}) -> p {names}", **{f"d{i}": shape[i] for i in range(1, len(shape) - 1)})
        v = V(ap, [Buf(name)])
        g["vs"].append(v)
        return v
    def handoff(old, new):
        ops = []
        for v in gens[old]["vs"]:
            for b in v.bufs:
                if b.writer is not None:
                    ops.append(b.writer)
                ops.extend(b.readers)
        for v in gens[new]["vs"]:
            for b in v.bufs:
                b.readers.extend(ops)
    def enter(gen):
        for g_ in list(gens):
            if g_ != gen:
                handoff(g_, gen)
    hid = carve("ffn", "hid", [128, FC, T], BF16)
    mkT = carve("xa", "mkT", [128, 8, 256], BF16)
    mvt = carve("xa", "mvt", [128, 2, 1024], BF16)
    exs = [carve("xa", f"exs{i}", [128, 2, T], BF16) for i in range(4)]
    rden = [carve("xa", f"rden{i}", [128, T], F32) for i in range(4)]
    RING = 5
    ring = [P.tile(f"ring{i}", [128, 2048], BF16) for i in range(RING)]
    rstd = P.tile("rstd", [128, T], F32)
    xs = P.tile("xs", [128, 4, D], F32)
    sg = [P.tile(f"sg{i}", [128, T], F32) for i in range(2)]
    ymix = P.tile("ymix", [128, KC, T], BF16)
    qT = mtile("qT", [128, KC, T], BF16, KC)
    Sh = [P.tile(f"Sh{l}", [128, 3, 64], F32) for l in range(L)]
    Sr = [P.tile(f"Sr{l}", [128, 3, 64], F32) for l in range(L)]
    Shb = P.tile("Shb", [128, 3, 64], BF16)
    Srb = P.tile("Srb", [128, 3, 64], BF16)
    cz = [P.tile(f"cz{l}", [128, 2, 2], F32) for l in range(L)]
    crw = [P.tile(f"crw{l}", [128, 11], F32) for l in range(L)]
    PB = [P.psum(f"pb{i}", [128, 512]) for i in range(8)]
    st = {"ring": 0, "pj": 0, "ms": 0}
    def pj():
        st["pj"] = (st["pj"] + 1) % 3
        return PB[st["pj"]]
    def ms():
        st["ms"] = (st["ms"] + 1) % 5
        return PB[3 + st["ms"]]
    def bfview(bank):
        return V(bank.ap.bitcast(BF16), bank.bufs)

    P.dma("sp", par, par_d)
    P.dma("sp", con, con_d)
    for l in range(L):
        P.dma("pool", wb["xattn_wkv"][l], wf["xattn_wkv"][l])
    for l in range(L):
        for n in ("ffn1_w_in", "ffn1_w_out", "w_mix_in", "w_mix_out", "xattn_wq", "xattn_wo", "ffn2_w_in", "ffn2_w_out"):
            P.dma("pool", wb[n][l], wf[n][l])
    P.dma("pool", wa2[0:64], w2_d.rearrange("l k c -> k l c"))
    P.dma("pool", wa2[64:128], a2_d.rearrange("l k c -> k l c"))
    P.dma("pool", g2b, g2_d.rearrange("l k c -> k l c"))
    P.copy("act", cb[:, 0:128], ident_f)
    P.copy("act", cb[:, 128:256], cst("blk64"))
    P.copy("act", cb[:, 256:384], cst("ones"))
    for l in range(L):
        P.memset("pool", Sh[l], 0.0); P.memset("pool", Sr[l], 0.0)
        P.memset("pool", cz[l], 0.0); P.memset("pool", crw[l], 0.0)
    for l in range(L):
        P.ts("dve", pc(l, "omu"), pc(l, "mu"), -1.0, ALU.mult, 1.0, ALU.add)
        P.ts("dve", pc(l, "oka"), pc(l, "k_a"), -1.0, ALU.mult, 1.0, ALU.add)
    ex_l = P.tile("ex_l", [128, L, 3], F32)
    sm = P.tile("sm", [128, 3], F32)
    for l in range(L):
        P.act(ex_l[:, l, :], pc(l, "lbl"), AF.Exp)
    P.copy("dve", sm, ex_l[:, 0, :])
    for l in range(1, L):
        P.tt("dve", sm, sm, ex_l[:, l, :], ALU.add)
    P.recip(sm, sm)
    for l in range(L):
        P.tt("dve", ex_l[:, l, :], ex_l[:, l, :], sm, ALU.mult)
    P.memset("dve", pc(0, "lb"), 0.0)
    for l in range(1, L):
        P.tt("dve", pc(l, "lb"), pc(l - 1, "lb"), ex_l[:, l, :], ALU.add)
    for l in range(L):
        P.ts("dve", pc(l, "lb"), pc(l, "lb"), 0.0, ALU.max)
        P.ts("dve", pc(l, "olb"), pc(l, "lb"), -1.0, ALU.mult, 1.0, ALU.add)

    def ring_slot():
        st["ring"] = (st["ring"] + 1) % RING
        return ring[st["ring"]]

    def proj_cols(xin, W, groups, handler, ntok=T):
        for (c0, ncols) in groups:
            slot = ring_slot()
            sv = slot[:, 0:KC * ncols].re("p (k n) -> p k n", k=KC)
            P.dma("sp", sv, W[:, c0:c0 + ncols].re("(k p) n -> p k n", p=128))
            for j in range(ncols // 128):
                ps = pj()
                for k in range(KC):
                    P.matmul(ps[:, 0:ntok], sv[:, k, j * 128:(j + 1) * 128], xin[:, k, 0:ntok],
                             start=(k == 0), stop=(k == KC - 1))
                handler(c0 + j * 128, ps[:, 0:ntok])

    def proj_rows(rhs_list, W, handler):
        nK = len(rhs_list)
        for g in range(0, nK, 2):
            n = min(2, nK - g)
            slot = ring_slot()
            sv = slot[:, 0:n * 1024].re("p (k n) -> p k n", k=n)
            P.dma("sp", sv, W[g * 128:(g + n) * 128, :].re("(k p) n -> p k n", p=128))
            for kk in range(n):
                k = g + kk
                for dc in range(KC):
                    P.matmul(PB[dc], sv[:, kk, dc * 128:(dc + 1) * 128], rhs_list[k],
                             start=(k == 0), stop=(k == nK - 1))
        for dc in [KC - 1] + list(range(KC - 1)):
            handler(dc, PB[dc])

    def norm_partial(c):
        n = st.get("ncnt", 0)
        st["nps"] = PB[7]
        P.act(chv(qT, c), chv(hT, c), AF.Square)
        P.matmul(st["nps"], ones_b, chv(qT, c), start=(n == 0), stop=(n == KC - 1))
        st["ncnt"] = (n + 1) % KC
        if n == KC - 1:
            st["nready"] = True

    def rmsnorm(gcols, out):
        if not st.get("nready"):
            for c in range(KC):
                norm_partial(c)
        st["nready"] = False
        P.act(rstd, st["nps"], AF.Sqrt, scale=1.0 / D, bias=EPS)
        P.recip(rstd, rstd)
        for c in range(KC):
            P.stt("dve", out[:, c, :], chv(hT, c), gcols[:, c:c + 1], rstd, ALU.mult, ALU.mult)

    def groups(c0, n):
        g = []
        while n > 0:
            w = min(256, n); g.append((c0, w)); c0 += w; n -= w
        return g

    def ffn(l, which):
        P.section = which
        enter("ffn")
        rmsnorm(pc(l, which + "_norm"), xn)
        W = wb[which + "_w_in"][l]
        for g in range(FC // 2):
            def hg(c0, ps, g=g):
                j = (c0 - g * 256) // 128
                P.act(sg[j], ps, AF.Silu)
            proj_cols(xn, W, [(g * 256, 256)], hg)
            def hu(c0, ps, g=g):
                j = (c0 - DFF - g * 256) // 128
                P.tt("dve", hid[:, 2 * g + j, :], sg[j], ps, ALU.mult)
            proj_cols(xn, W, [(DFF + g * 256, 256)], hu)
        def ho(dc, ps):
            P.stt("dve", chv(hT, dc), ps, 0.5, chv(hT, dc), ALU.mult, ALU.add)
            norm_partial(dc)
        P.section = which + "_out"
        proj_rows([hid[:, k, :] for k in range(FC)], wb[which + "_w_out"][l], ho)

    def xattn(l):
        P.section = "xa"
        enter("xa")
        rmsnorm(pc(l, "xattn_norm"), xn)
        P.dma("sp", mkT, mk_d[l].re("p (k m) -> p k m", k=8))
        P.dma("sp", mvt, mv_d[l].re("p (k m) -> p k m", k=2))
        def hq(c0, ps):
            P.copy("act", qT[:, c0 // 128, :], ps)
        proj_cols(xn, wb["xattn_wq"][l], groups(0, D), hq)
        oT = xn
        for hh in range(4):
            for mb in range(2):
                ps = ms()
                for kk in range(2):
                    P.matmul(ps, mkT[:, 2 * hh + kk, mb * 128:(mb + 1) * 128], qT[:, 2 * hh + kk, :],
                             start=(kk == 0), stop=(kk == 1))
                P.act(exs[hh][:, mb, :], ps, AF.Exp, scale=1.0 / 16.0)
        for hh in range(4):
            ps = ms()
            for mb in range(2):
                P.matmul(ps, ones_b, exs[hh][:, mb, :], start=(mb == 0), stop=(mb == 1))
            P.recip(rden[hh], ps)
        for hh in range(4):
            for kk in range(2):
                ps = ms()
                for mb in range(2):
                    P.matmul(ps, mvt[:, mb, (2 * hh + kk) * 128:(2 * hh + kk + 1) * 128], exs[hh][:, mb, :],
                             start=(mb == 0), stop=(mb == 1))
                P.tt("dve", oT[:, 2 * hh + kk, :], ps, rden[hh], ALU.mult)
        def ho(dc, ps):
            P.tt("dve", chv(hT, dc), ps, chv(hT, dc), ALU.add)
            norm_partial(dc)
        proj_rows([oT[:, k, :] for k in range(KC)], wb["xattn_wo"][l], ho)

    cg = [carve("conv", f"cg{i}", [128, T], F32) for i in range(2)]
    zb = [carve("conv", f"zb{i}", [128, T + 2], F32) for i in range(2)]
    zc = [carve("conv", f"zc{i}", [128, T], F32) for i in range(2)]
    hq_ = carve("hg", "hq", [128, 3, T], BF16)
    hlf = carve("hg", "hlf", [128, 3, T], F32)
    hsg = carve("hg", "hsg", [128, 3, T], BF16)
    hgs = carve("hg", "hgs", [128, 3, T], BF16)
    hvT = carve("hg", "hvT", [128, 3, T], BF16)
    hqt = carve("hg", "hqt", [128, 3, T], BF16)
    hkt = carve("hg", "hkt", [128, 3, T], BF16)
    hgam = carve("hg", "hgam", [128, 3, 8], F32)
    hvtm = [carve("hg", f"hvtm{i}", [64, 384], BF16) for i in range(2)]
    hktm = [carve("hg", f"hktm{i}", [64, 384], BF16) for i in range(2)]
    hsc = carve("hg", "hsc", [64, 6, T], BF16)
    hA = carve("hg", "hA", [128, T], F32); hB = carve("hg", "hB", [128, T], F32)
    hC = carve("hg", "hC", [128, T], F32); hD = carve("hg", "hD", [128, T], F32)
    htS = carve("hg", "htS", [128, 3, 64], F32)
    of = carve("hg", "of", [128, T], F32)
    osq = carve("hg", "osq", [128, T], BF16)

    def conv_handlers(l):
        def h(ci, ps):
            i = ci % 2
            if ci in (2, 3):
                P.copy("act", cg[i], ps)
            elif ci in (4, 5):
                P.copy("pool", zb[i][:, 0:2], cz[l][:, i, :])
                P.tt("dve", zb[i][:, 2:T + 2], cg[i], ps, ALU.mult)
                P.copy("pool", cz[l][:, i, :], zb[i][:, T:T + 2])
                P.ts("dve", zc[i], zb[i][:, 2:T + 2], pc(l, "cw2", i), ALU.mult, pc(l, "cb", i), ALU.add)
                P.stt("dve", zc[i], zb[i][:, 1:T + 1], pc(l, "cw1", i), zc[i], ALU.mult, ALU.add)
                P.stt("dve", zc[i], zb[i][:, 0:T], pc(l, "cw0", i), zc[i], ALU.mult, ALU.add)
            else:
                P.tt("dve", ymix[:, i, :], ps, zc[i], ALU.mult)
        return h

    def hgrn_handler(l):
        def h(ci, ps):
            k = ci - L_HG0; i = k % 3; kind = k // 3
            if kind == 0:
                P.act(hq_[:, i, :], ps, AF.Silu)
            elif kind == 1:
                P.ts("dve", hA, ps, -80.0, ALU.max)
                P.act(hB, hA, AF.Exp, scale=-1.0)
                P.act(hC, hB, AF.Ln, bias=1.0)
                P.act(hD, hB, AF.Ln, scale=pc(l, "lb", i), bias=1.0)
                P.tt("pool", hlf[:, i, :], hD, hC, ALU.subtract)
                P.act(hsg[:, i, :], hA, AF.Sigmoid, scale=-1.0)
            elif kind == 2:
                P.copy("act", hvT[:, i, :], ps)
            else:
                P.act(hgs[:, i, :], ps, AF.Silu)
        return h

    def tm_transposes(srcs, dsts, c, C):
        for (src, dst) in zip(srcs, dsts):
            bank = ms(); bv = bfview(bank)
            for i in range(3):
                P.transpose(bv[0:C, i * 128:(i + 1) * 128], src[:, i, c * C:(c + 1) * C], ident_b)
            P.copy("act", dst, bv[0:C, 0:384])

    def hgrn_core(l):
        P.section = "hg_core"
        C = 64; NCH = T // C
        for i in range(3):
            P.scan(hlf[:, i, :], cst("r64"), hlf[:, i, :], 0.0, ALU.mult, ALU.add)
            P.act(hA, hlf[:, i, :], AF.Exp)
            P.tt("dve", hqt[:, i, :], hq_[:, i, :], hA, ALU.mult)
            P.copy("pool", hgam[:, i, :], hA[:, C - 1::C])
            P.act(hB, hlf[:, i, :], AF.Exp, scale=-1.0)
            P.stt("dve", hkt[:, i, :], hsg[:, i, :], pc(l, "olb", i), hB, ALU.mult, ALU.mult)
        if FLAGS.get("hg_stage", 9) < 2:
            P.memset("pool", ymix[:, 2:5, :], 0.0); return
        m64 = cst("m64is", 64)[:, 64:128]
        mb = V(m64.ap.unsqueeze(1).to_broadcast([C, NCH, C]), m64.bufs)
        for h in HORD:
            i = h // 2; r0 = (h % 2) * 64
            ps = ms()
            for c in range(NCH):
                P.matmul(ps[0:C, c * C:(c + 1) * C], hkt[r0:r0 + 64, i, c * C:(c + 1) * C],
                         hqt[r0:r0 + 64, i, c * C:(c + 1) * C])
            P.tt("dve", hsc[:, h, :].re("p (c t) -> p c t", t=C), ps[0:C, :].re("p (c t) -> p c t", t=C), mb, ALU.mult)
        if FLAGS.get("hg_stage", 9) < 3:
            P.memset("pool", ymix[:, 2:5, :], 0.0); return
        P.copy("act", Shb, Sh[l])
        psO = [PB[0], PB[1], PB[2]]

        def hg_pre(c):
            tm_transposes((hvT, hkt), (hvtm[c % 2], hktm[c % 2]), c, C)
            yield

        def hg_chain(c):
            vt = hvtm[c % 2]; kt = hktm[c % 2]
            for part in ("even", "odd1", "odd2"):
                for h in ((0, 2, 4) if part == "even" else (1, 3, 5)):
                    i = h // 2; r0 = (h % 2) * 64
                    o = psO[i][r0:r0 + 64, c * C:(c + 1) * C]
                    if part != "odd2":
                        P.matmul(o, Shb[r0:r0 + 64, i, :], hqt[r0:r0 + 64, i, c * C:(c + 1) * C], start=True, stop=False)
                    if part != "odd1":
                        P.matmul(o, vt[:, h * 64:(h + 1) * 64], hsc[:, h, c * C:(c + 1) * C], start=False, stop=True)
            psD = ms()
            for h in range(NHG):
                i = h // 2; r0 = (h % 2) * 64
                P.matmul(psD[r0:r0 + 64, i * 64:(i + 1) * 64], kt[:, h * 64:(h + 1) * 64], vt[:, h * 64:(h + 1) * 64])
            P.tt("dve", htS, Sh[l], psD[:, 0:192].re("p (i e) -> p i e", i=3), ALU.add)
            g = hgam[:, :, c:c + 1]
            gb = V(g.ap.to_broadcast([128, 3, 64]), g.bufs)
            P.tt("dve", Shb, htS, gb, ALU.mult)
            P.tt("dve", Sh[l], htS, gb, ALU.mult)
            yield

        def interleave(gs):
            gs = list(gs)
            while gs:
                for g_ in list(gs):
                    try:
                        next(g_)
                    except StopIteration:
                        gs.remove(g_)

        interleave([hg_pre(0)])
        for c in range(NCH):
            interleave(([hg_pre(c + 1)] if c + 1 < NCH else []) + [hg_chain(c)])
        if FLAGS.get("hg_stage", 9) < 4:
            P.memset("pool", ymix[:, 2:5, :], 0.0); return
        for i in range(3):
            P.act(osq, psO[i], AF.Square)
            P.copy("dve", of, psO[i])
            ps = ms()
            P.matmul(ps, blk_b, osq)
            P.act(hA, ps, AF.Sqrt, scale=1.0 / 64.0, bias=EPS)
            P.recip(hA, hA)
            P.tt("dve", hB, of, hA, ALU.mult)
            P.stt("dve", ymix[:, 2 + i, :], hB, pc(l, "hgn", i), hgs[:, i, :], ALU.mult, ALU.mult)

    praw = [carve("rw", f"praw{i}", [128, T + 1], F32) for i in range(2)]
    rr = carve("rw", "rr", [128, 3, T], BF16); kr = carve("rw", "kr", [128, 3, T], F32); vr = carve("rw", "vr", [128, 3, T], BF16)
    wab = carve("rw", "wab", [128, T], BF16); gsb = carve("rw", "gsb", [128, T], BF16)
    lw = carve("rw", "lw", [128, T], F32); aicl = carve("rw", "aicl", [128, T], F32)
    gg = carve("rw", "gg", [128, 3, T], BF16); bon = carve("rw", "bon", [128, 3, T], BF16)
    kkn = carve("rw", "kkn", [128, T], F32); kmod = carve("rw", "kmod", [128, T], F32)
    bcs = carve("rw", "bcs", [128, T], F32)
    AR = carve("rw", "AR", [128, 3, 8, 2, 64], BF16)
    KT = carve("rw", "KT", [128, 3, T], BF16); BT = carve("rw", "BT", [128, 3, T], BF16); VT = carve("rw", "VT", [128, 3, T], BF16)
    rgam = carve("rw", "rgam", [128, 3, 8], F32)
    NS = 4
    ktm = [carve("rw", f"ktm{i}", [64, 384], BF16) for i in range(NS)]
    btm = [carve("rw", f"btm{i}", [64, 384], BF16) for i in range(NS)]
    vtm = [carve("rw", f"vtm{i}", [64, 384], BF16) for i in range(NS)]
    SKs = [carve("rw", f"SK{i}", [64, 6, 128], BF16) for i in range(NS)]
    SBs = [carve("rw", f"SBr{i}", [64, 6, 64], BF16) for i in range(NS)]
    Rs = [carve("rw", f"Rf{i}", [64, 6, 64], DDT) for i in range(NS)]
    AVs = [carve("rw", f"AV{i}", [64, 384], F32) for i in range(NS)]
    Mfs = [[carve("rw", f"Mf{m}{i}", [64, 6, 64], DDT) for i in range(2)] for m in range(2)]
    Mtfs = [[carve("rw", f"Mtf{m}{i}", [64, 6, 64], DDT) for i in range(2)] for m in range(2)]
    P1f = carve("rw", "P1f", [64, 384], DDT); Ub = carve("rw", "Ub", [64, 384], BF16)
    sqb = carve("rw", "sqb", [128, T], BF16)
    tA = carve("rw", "tA", [128, T], F32); tB = carve("rw", "tB", [128, T], F32)
    tC = carve("rw", "tC", [128, T], F32); tD = carve("rw", "tD", [128, T], F32)
    tS = carve("rw", "tS", [128, 3, 64], F32)
    pT1 = carve("rw", "pT1", [128, T], F32); pT2 = carve("rw", "pT2", [128, T], F32)
    sqb2 = carve("rw", "sqb2", [128, T], BF16)

    def rwkv_handler(l):
        def h(ci, ps):
            j = ci - L_RW0
            pr = praw[j % 2]
            P.copy("act", pr[:, 1:T + 1], ps)
            P.copy("pool", pr[:, 0:1], crw[l][:, j:j + 1])
            P.copy("pool", crw[l][:, j:j + 1], pr[:, T:T + 1])
            P.ts("dve", tA, pr[:, 1:T + 1], pc(l, "omu", j), ALU.mult)
            if j < 9:
                dst = (rr, kr, vr)[j // 3][:, j % 3, :]
            else:
                dst = tB
            P.stt("dve", dst, pr[:, 0:T], pc(l, "mu", j), tA, ALU.mult, ALU.add)
            if j == 9:
                P.act(wab[0:64, :], tB[0:64, :], AF.Tanh)
                P.copy("act", wab[64:128, :], tB[64:128, :])
            elif j == 10:
                P.act(gsb, tB, AF.Sigmoid)
        return h

    def rwkv_core(l):
        P.section = "rw_pre"
        C = 64; NCH = T // C
        for i in range(3):
            ps = ms(); P.matmul(ps, wa2[0:64, l, i * 128:(i + 1) * 128], wab[0:64, :])
            P.act(tA, ps, AF.Sigmoid, bias=pc(l, "w0", i))
            P.ts("dve", lw, tA, -0.606531, ALU.mult)
            ps = ms(); P.matmul(ps, wa2[64:128, l, i * 128:(i + 1) * 128], wab[64:128, :])
            P.act(aicl, ps, AF.Sigmoid, bias=pc(l, "a0", i))
            ps = ms(); P.matmul(ps, g2b[:, l, i * 128:(i + 1) * 128], gsb)
            P.copy("act", gg[:, i, :], ps)
            P.ts("dve", tB, kr[:, i, :], pc(l, "k_k", i), ALU.mult)
            P.act(sqb, tB, AF.Square)
            ps = ms(); P.matmul(ps, blk_b, sqb)
            P.act(tC, ps, AF.Sqrt)
            P.ts("dve", tC, tC, 1e-12, ALU.max)
            P.recip(tC, tC)
            P.tt("dve", kkn, tB, tC, ALU.mult)
            P.ts("dve", pT1, aicl, pc(l, "k_a", i), ALU.mult, pc(l, "oka", i), ALU.add)
            P.tt("dve", kmod, kr[:, i, :], pT1, ALU.mult)
            P.tt("dve", pT1, rr[:, i, :], kmod, ALU.mult)
            P.ts("dve", sqb2, pT1, pc(l, "r_k", i), ALU.mult)
            ps = ms(); P.matmul(ps, blk_b, sqb2)
            P.tt("dve", bon[:, i, :], ps, vr[:, i, :], ALU.mult)
            P.scan(bcs, cst("r64"), lw, 0.0, ALU.mult, ALU.add)
            P.act(tA, bcs, AF.Exp)
            P.tt("dve", AR[:, i, :, 1, :], rr[:, i, :].re("p (c t) -> p c t", t=C), tA.re("p (c t) -> p c t", t=C), ALU.mult)
            P.copy("pool", rgam[:, i, :], tA[:, C - 1::C])
            P.tt("dve", tD, bcs, lw, ALU.subtract)
            P.act(tD, tD, AF.Exp)
            P.stt("dve", AR[:, i, :, 0, :], kkn.re("p (c t) -> p c t", t=C), -1.0, tD.re("p (c t) -> p c t", t=C), ALU.mult, ALU.mult)
            P.act(tC, bcs, AF.Exp, scale=-1.0)
            P.tt("dve", KT[:, i, :], kmod, tC, ALU.mult)
            P.tt("dve", pT2, kkn, aicl, ALU.mult)
            P.tt("dve", BT[:, i, :], pT2, tC, ALU.mult)
            P.copy("act", VT[:, i, :], vr[:, i, :])
        P.copy("act", Srb, Sr[l])
        psY = [PB[0], PB[1], PB[2]]
        mis = cst("m64is", 64); msk_s = cst("m64s", 64); msk_l = cst("m64l", 64); id64 = cst("id64", 64)
        bc3 = lambda m, n: V(m.ap.unsqueeze(1).to_broadcast([64, n, m.ap.shape[1]]), m.bufs)
        def rw_pre(c):
            P.section = "rw_chain"
            sx = c % NS; m = c % 2
            kt_ = ktm[sx]; bt_ = btm[sx]; vt_ = vtm[sx]
            SK = SKs[sx]; SBr = SBs[sx]; Rf = Rs[sx]; Mf = Mfs[m]; Mtf = Mtfs[m]
            tm_transposes((KT, BT, VT), (kt_, bt_, vt_), c, C)
            yield
            X1 = [ms(), ms()]; X2 = [ms(), ms()]; X3 = ms()
            for hs in ((0, 2, 4), (1, 3, 5)):
                for h in hs:
                    i = h // 2; r0 = (h % 2) * 64
                    P.matmul(X1[h // 4][0:64, (h % 4) * 128:(h % 4 + 1) * 128], KT[r0:r0 + 64, i, c * C:(c + 1) * C],
                             AR[r0:r0 + 64, i, c, :, :].re("p a t -> p (a t)"))
                for h in hs:
                    i = h // 2; r0 = (h % 2) * 64
                    P.matmul(X2[h // 4][0:64, (h % 4) * 128:(h % 4 + 1) * 128], BT[r0:r0 + 64, i, c * C:(c + 1) * C],
                             AR[r0:r0 + 64, i, c, :, :].re("p a t -> p (a t)"))
                for h in hs:
                    i = h // 2; r0 = (h % 2) * 64
                    P.matmul(X3[0:64, h * 64:(h + 1) * 64], AR[r0:r0 + 64, i, c, 0, :], BT[r0:r0 + 64, i, c * C:(c + 1) * C])
            P.tt("dve", SK[:, 0:4, :], X1[0][0:64, :].re("p (h n) -> p h n", h=4), bc3(mis, 4), ALU.mult)
            P.tt("dve", SK[:, 4:6, :], X1[1][0:64, 0:256].re("p (h n) -> p h n", h=2), bc3(mis, 2), ALU.mult)
            for (bk, h0, nh) in ((X2[0], 0, 4), (X2[1], 4, 2)):
                v4 = bk[0:64, 0:nh * 128].re("p (h n) -> p h n", h=nh)
                P.tt("dve", Mf[0][:, h0:h0 + nh, :], v4[:, :, 0:64], bc3(msk_s, nh), ALU.mult)
                P.tt("dve", SBr[:, h0:h0 + nh, :], v4[:, :, 64:128], bc3(mis[:, 64:128], nh), ALU.mult)
            P.tt("dve", Mtf[0], X3[0:64, 0:384].re("p (h n) -> p h n", h=6), bc3(msk_l, 6), ALU.mult)
            P.tt("dve", Rf, Mf[0], bc3(id64, 6), ALU.add)
            yield
            pAV = ms()
            for h in range(NRW):
                P.matmul(pAV[0:64, h * 64:(h + 1) * 64], SK[:, h, 0:64], vt_[:, h * 64:(h + 1) * 64])
            P.copy("act", AVs[sx], pAV[0:64, 0:384])
            idb64 = ident_b[0:64, 0:64]
            def sq_mm(cur, want_m):
                pMt = ms()
                for h in range(NRW):
                    P.matmul(pMt[0:64, h * 64:(h + 1) * 64], Mf[cur][:, h, :], Mtf[cur][:, h, :])
                pM = None
                if want_m:
                    pM = ms()
                    for h in range(NRW):
                        P.matmul(pM[0:64, h * 64:(h + 1) * 64], Mtf[cur][:, h, :], Mf[cur][:, h, :])
                return pMt, pM
            def sq_ev(pMt, pM, nxt):
                P.copy("dve", Mtf[nxt], pMt[0:64, 0:384].re("p (h n) -> p h n", h=6))
                if pM is not None:
                    P.copy("act", Mf[nxt], pM[0:64, 0:384].re("p (h n) -> p h n", h=6))
            def r_mm(nxt):
                pR = ms()
                for h in range(NRW):
                    P.matmul(pR[0:64, h * 64:(h + 1) * 64], Mtf[nxt][:, h, :], Rf[:, h, :])
                return pR
            def r_ev(pR):
                P.tt("dve", Rf, Rf, pR[0:64, 0:384].re("p (h n) -> p h n", h=6), ALU.add)
            cur = 0
            pMt, pM = sq_mm(cur, True)
            sq_ev(pMt, pM, 1 - cur)
            yield
            for lev in range(5):
                nxt = 1 - cur
                pR = r_mm(nxt)
                if lev < 4:
                    pMt, pM = sq_mm(nxt, lev < 3)
                r_ev(pR)
                if lev < 4:
                    sq_ev(pMt, pM, cur)
                yield
                cur = nxt

        def rw_chain(c):
            P.section = "rw_chain2"
            sx = c % NS
            kt_ = ktm[sx]; bt_ = btm[sx]; vt_ = vtm[sx]
            SK = SKs[sx]; SBr = SBs[sx]; Rf = Rs[sx]
            pP = ms()
            for h in HORD:
                i = h // 2; r0 = (h % 2) * 64
                P.matmul(pP[0:64, h * 64:(h + 1) * 64], AR[r0:r0 + 64, i, c, 0, :], Srb[r0:r0 + 64, i, :])
            P.tt("dve", P1f, pP[0:64, 0:384], AVs[sx], ALU.add)
            yield
            pU = ms()
            for h in range(NRW):
                P.matmul(pU[0:64, h * 64:(h + 1) * 64], Rf[:, h, :], P1f[:, h * 64:(h + 1) * 64])
            P.copy("dve", Ub, pU[0:64, 0:384])
            yield
            for part in ("even", "odd1", "odd2"):
                for h in ((0, 2, 4) if part == "even" else (1, 3, 5)):
                    i = h // 2; r0 = (h % 2) * 64
                    o = psY[i][r0:r0 + 64, c * C:(c + 1) * C]
                    if part != "odd2":
                        P.matmul(o, Srb[r0:r0 + 64, i, :], AR[r0:r0 + 64, i, c, 1, :], start=True, stop=False)
                    if part != "odd1":
                        P.matmul(o, Ub[:, h * 64:(h + 1) * 64], SBr[:, h, :], start=False, stop=False)
                        P.matmul(o, vt_[:, h * 64:(h + 1) * 64], SK[:, h, 64:128], start=False, stop=True)
            pD = ms()
            for h in range(NRW):
                i = h // 2; r0 = (h % 2) * 64
                o = pD[r0:r0 + 64, i * 64:(i + 1) * 64]
                P.matmul(o, bt_[:, h * 64:(h + 1) * 64], Ub[:, h * 64:(h + 1) * 64], start=True, stop=False)
                P.matmul(o, kt_[:, h * 64:(h + 1) * 64], vt_[:, h * 64:(h + 1) * 64], start=False, stop=True)
            P.tt("dve", tS, Sr[l], pD[:, 0:192].re("p (i e) -> p i e", i=3), ALU.add)
            g = rgam[:, :, c:c + 1]
            gb = V(g.ap.to_broadcast([128, 3, 64]), g.bufs)
            P.tt("dve", Srb, tS, gb, ALU.mult)
            P.tt("dve", Sr[l], tS, gb, ALU.mult)
            yield

        def seq(*gs):
            for g_ in gs:
                yield from g_

        def interleave(gs):
            gs = list(gs)
            while gs:
                for g_ in list(gs):
                    try:
                        next(g_)
                    except StopIteration:
                        gs.remove(g_)

        if FLAGS.get("rw_pipe", True):
            interleave([rw_pre(0), rw_pre(1)])
            for k in range(1, NCH // 2):
                interleave([rw_pre(2 * k), rw_pre(2 * k + 1), seq(rw_chain(2 * k - 2), rw_chain(2 * k - 1))])
            interleave([seq(rw_chain(NCH - 2), rw_chain(NCH - 1))])
        else:
            for c in range(NCH):
                interleave([seq(rw_pre(c), rw_chain(c))])
        P.section = "rw_post"
        for i in range(3):
            P.copy("dve", tA, psY[i])
            P.copy("act", sqb, tA)
            ps = ms(); P.matmul(ps, blk_b, sqb)
            P.stt("dve", tB, ps, -1.0 / 64.0, tA, ALU.mult, ALU.add)
            P.act(sqb, tB, AF.Square)
            ps = ms(); P.matmul(ps, blk_b, sqb)
            P.act(tC, ps, AF.Sqrt, scale=1.0 / 64.0, bias=64e-5)
            P.recip(tC, tC)
            P.tt("dve", tB, tB, tC, ALU.mult)
            P.ts("pool", pT1, tB, pc(l, "ln_w", i), ALU.mult, pc(l, "ln_b", i), ALU.add)
            P.tt("pool", pT1, pT1, bon[:, i, :], ALU.add)
            P.tt("pool", ymix[:, 5 + i, :], pT1, gg[:, i, :], ALU.mult)

    def mixer(l):
        P.section = "mix_in"
        rmsnorm(pc(l, "mix_norm"), xn)
        W = wb["w_mix_in"][l]
        enter("conv")
        ch = conv_handlers(l)
        proj_cols(xn, W, [(256, 256), (512, 256), (0, 256)], lambda c0, ps: ch(c0 // 128, ps))
        if FLAGS["hgrn"]:
            enter("hg")
            P.section = "hg_in"
            hh = hgrn_handler(l)
            proj_cols(xn, W, groups(L_HG0 * 128, 12 * 128), lambda c0, ps: hh(c0 // 128, ps))
            hgrn_core(l)
        else:
            P.memset("pool", ymix[:, 2:5, :], 0.0)
        if FLAGS["rwkv"]:
            enter("rw")
            P.section = "rw_in"
            rh = rwkv_handler(l)
            proj_cols(xn, W, groups(L_RW0 * 128, 11 * 128), lambda c0, ps: rh(c0 // 128, ps))
            rwkv_core(l)
        else:
            P.memset("pool", ymix[:, 5:8, :], 0.0)
        def ho(dc, ps):
            P.tt("dve", chv(hT, dc), ps, chv(hT, dc), ALU.add)
            norm_partial(dc)
        P.section = "mix_out"
        proj_rows([ymix[:, k, :] for k in range(KC)], wb["w_mix_out"][l], ho)

    memt = xs[:, 0:2, :]
    P.dma("sp", memt, mem_d.rearrange("(b p) d -> p b d", p=128))
    mss = P.tile("mss", [128, 2], F32)
    gens["pro"] = {"off": gens["xa"]["off"], "vs": []}
    msq = carve("pro", "msq", [128, D], BF16)
    for b in range(2):
        P.act(msq, memt[:, b, :], AF.Square, accum_out=mss[:, b:b + 1])
    P.act(mss, mss, AF.Sqrt, scale=1.0 / D, bias=EPS)
    P.recip(mss, mss)
    memn = carve("pro", "memn", [128, 2, D], F32)
    for b in range(2):
        P.ts("dve", memn[:, b, :], memt[:, b, :], mss[:, b:b + 1], ALU.mult)
    memT = carve("pro", "memT", [128, KC, MEM], F32)
    for k in range(KC):
        ps = ms()
        for b in range(2):
            P.transpose(ps[:, b * 128:(b + 1) * 128], memn[:, b, k * 128:(k + 1) * 128], ident_f)
        P.copy("act", memT[:, k, :], ps[:, 0:256])
    memg = qT[:, :, 0:MEM]
    mks = ymix[:, :, 0:MEM]
    for l in range(L):
        for k in range(KC):
            P.ts("dve", memg[:, k, :], memT[:, k, :], pc(l, "mem_norm", k), ALU.mult)
        def hk(c0, ps):
            P.copy("act", mks[:, c0 // 128, :], ps)
        proj_cols(memg, wb["xattn_wkv"][l], groups(0, D), hk, ntok=MEM)
        P.dma("sp", mk_d[l].re("p (k m) -> p k m", k=8), mks)
        for (c0, ncols) in groups(D, D):
            slot = ring_slot()
            sv = slot[:, 0:KC * ncols].re("p (k n) -> p k n", k=KC)
            P.dma("sp", sv, wb["xattn_wkv"][l][:, c0:c0 + ncols].re("(k p) n -> p k n", p=128))
            for b in range(2):
                ps = pj()
                for k in range(KC):
                    P.matmul(ps[:, 0:ncols], memg[:, k, b * 128:(b + 1) * 128], sv[:, k, :], start=(k == 0), stop=(k == KC - 1))
                P.copy("act", mvt[:, b, c0 - D:c0 - D + ncols], ps[:, 0:ncols])
        P.dma("sp", mv_d[l].re("p (k m) -> p k m", k=2), mvt)

    for t in range(NT):
        P.section = "io"
        P.dma("sp", xs, x_d[t * T:(t + 1) * T, :].rearrange("(s p) d -> p s d", p=128))
        for c in range(KC):
            ps = pj()
            for s in range(4):
                P.transpose(ps[:, s * 128:(s + 1) * 128], xs[:, s, c * 128:(c + 1) * 128], ident_f)
            P.copy("act" if c % 2 else "dve", chv(hT, c), ps)
            norm_partial(c)
        for l in range(L):
            P.phase += 1
            if FLAGS["ffn"]:
                ffn(l, "ffn1")
            if FLAGS["mix"]:
                mixer(l)
            if FLAGS["xa"]:
                xattn(l)
            if FLAGS["ffn"]:
                ffn(l, "ffn2")
        P.phase += 1
        P.section = "io"
        fo = L * NPL
        if not st.get("nready"):
            for c in range(KC):
                norm_partial(c)
        st["nready"] = False
        P.act(rstd, st["nps"], AF.Sqrt, scale=1.0 / D, bias=EPS)
        P.recip(rstd, rstd)
        for c in range(KC):
            P.stt("dve", hT[:, c, :], hT[:, c, :], par[:, fo + c:fo + c + 1], rstd, ALU.mult, ALU.mult)
        for s in range(4):
            for half in range(2):
                ps = pj()
                for cc in range(4):
                    c = half * 4 + cc
                    P.transpose(ps[:, cc * 128:(cc + 1) * 128], hT[:, c, s * 128:(s + 1) * 128], ident_f)
                P.copy("act" if half else "dve", xs[:, s, half * 512:(half + 1) * 512], ps)
        P.dma("sp", out_d[t * T:(t + 1) * T, :].re("(s p) d -> p s d", p=128), xs)
    P.wait_dma_final("sp", out_d)
    stats = P.finish()
    stats["sbuf_free"] = nc.sbuf_bytes_remaining
    stats["gens"] = {k: v["off"] for k, v in gens.items()}
    P.close()
    if SECLOG is not None:
        SECLOG.update(P.seclog)
    return nc, stats


SECLOG = None
FLAGS = {"ffn": True, "mix": True, "xa": True, "hgrn": True, "rwkv": True}


def kernel(**inp):
    x = np.asarray(inp["x"], np.float32)
    B, S, _ = x.shape
    L = inp["ffn1_norm"].shape[0]
    nc, stats = build(S, L)
    par = np.zeros((128, L * NPL + 8), np.float32)
    for l in range(L):
        def put(n, a):
            o, w = PL[n]; par[:, l * NPL + o:l * NPL + o + w] = a
        for n in ("ffn1_norm", "mix_norm", "xattn_norm", "ffn2_norm", "mem_norm"):
            put(n, fm(inp[n][l]))
        for k in range(3):
            put(f"cw{k}", fm(inp["conv_w"][l][k]))
        put("cb", fm(inp["conv_b"][l])); put("lbl", fm(inp["hgrn_lb_logits"][l])); put("hgn", fm(inp["hgrn_norm"][l]))
        put("mu", fm(inp["rwkv_mu"][l])); put("w0", fm(inp["rwkv_w0"][l])); put("a0", fm(inp["rwkv_a0"][l]))
        put("k_k", fm(inp["rwkv_k_k"][l])); put("k_a", fm(inp["rwkv_k_a"][l])); put("r_k", fm(inp["rwkv_r_k"][l]))
        put("ln_w", fm(inp["rwkv_ln_w"][l])); put("ln_b", fm(inp["rwkv_ln_b"][l]))
    par[:, L * NPL:L * NPL + 8] = fm(inp["final_norm"])
    con = make_consts()
    shared = {"params": par, "consts": con}
    for n in ("ffn1_w_in", "ffn1_w_out", "w_mix_in", "w_mix_out", "xattn_wq", "xattn_wkv", "xattn_wo",
              "ffn2_w_in", "ffn2_w_out", "rwkv_w2", "rwkv_a2", "rwkv_g2"):
        shared[n] = np.ascontiguousarray(np.asarray(inp[n], np.float32))
    mem = np.asarray(inp["mem"], np.float32)
    in_maps = []
    for b in range(B):
        m = dict(shared)
        m["x"] = np.ascontiguousarray(x[b]); m["mem"] = np.ascontiguousarray(mem[b])
        in_maps.append(m)
    res = run_bass_kernel_spmd(nc, in_maps, core_ids=list(range(B)))
    return np.stack([np.asarray(r["out"], np.float32) for r in res.results], 0)
```

```python
from concourse.bass_utils import run_bass_kernel_spmd
import numpy as np
from contextlib import ExitStack
import concourse.bass as bass
import concourse.mybir as mybir

F32 = mybir.dt.float32
BF16 = mybir.dt.bfloat16
AF = mybir.ActivationFunctionType
ALU = mybir.AluOpType
AX = mybir.AxisListType

COMPUTE = ("pe", "act", "dve", "pool")
NPH = 4
STRICT = True


class Buf:
    __slots__ = ("name", "writer", "readers", "sem", "cnt", "excl")

    def __init__(self, name, excl=False):
        self.name = name
        self.excl = excl
        self.writer = None
        self.readers = []
        self.sem = None
        self.cnt = 0


class V:
    __slots__ = ("ap", "bufs")

    def __init__(self, ap, bufs):
        self.ap = ap
        self.bufs = bufs

    def __getitem__(self, key):
        return V(self.ap[key], self.bufs)

    def re(self, s, **kw):
        return V(self.ap.rearrange(s, **kw), self.bufs)

    def bc(self, shape):
        return V(self.ap.to_broadcast(shape), self.bufs)


class Op:
    __slots__ = ("eng", "fn", "deps", "dwaits", "signal", "signum", "dma", "tok", "idx", "ph")

    def __init__(self, eng, fn, dma=False):
        self.eng = eng
        self.fn = fn
        self.deps = []
        self.dwaits = {}
        self.signal = False
        self.signum = 0
        self.dma = dma
        self.tok = None
        self.idx = 0
        self.ph = 0


class Prog:
    def __init__(self, nc):
        self.nc = nc
        self.es = ExitStack()
        self.ops = {e: [] for e in ("pe", "act", "dve", "pool", "sp")}
        self.sems = {}
        self.nbuf = 0
        self.final_waits = []
        self.phase = 0
        self.section = ""
        self.seclog = None

    def sbuf(self, name, shape, dtype, nsub=1):
        t = self.es.enter_context(self.nc.sbuf_tensor(name, list(shape), dtype))
        bufs = [Buf(f"{name}.{i}") for i in range(nsub)]
        return t, bufs

    def tile(self, name, shape, dtype):
        t, bufs = self.sbuf(name, shape, dtype)
        return V(t[tuple(slice(None) for _ in shape)], bufs)

    def psum(self, name, shape, dtype=F32):
        t = self.es.enter_context(self.nc.psum_tensor(name, list(shape), dtype))
        return V(t[tuple(slice(None) for _ in shape)], [Buf(name, excl=True)])

    def dram(self, name, shape, dtype, kind="Internal"):
        t = self.nc.dram_tensor(name, list(shape), dtype, kind=kind)
        return V(t.ap(), [Buf(name)])

    def newsem(self, name):
        s = self.es.enter_context(self.nc.semaphore(name))
        return s

    def _dep(self, B, A, kind):
        if A is None or A is B:
            return
        if A.dma:
            sem, val = A.tok
            if B.dwaits.get(sem, (None, 0))[1] < val:
                B.dwaits[sem] = (sem, val)
            return
        if A.eng == B.eng:
            if B.eng == "pe" or (kind != "RAW" and not STRICT):
                return
        A.signal = True
        B.deps.append(A)

    def _track(self, op, reads, writes):
        for b in reads:
            self._dep(op, b.writer, "RAW")
            if b.excl:
                for r in b.readers:
                    if r.eng != op.eng:
                        self._dep(op, r, "RAR")
        for b in writes:
            self._dep(op, b.writer, "WAW")
            for r in b.readers:
                self._dep(op, r, "WAR")
        for b in reads:
            b.readers.append(op)
        for b in writes:
            b.writer = op
            b.readers = []

    @staticmethod
    def _bufs(vs):
        out = []
        for v in vs:
            if isinstance(v, V):
                for b in v.bufs:
                    if b not in out:
                        out.append(b)
        return out

    def emit(self, eng, fn, reads, writes):
        op = Op(eng, fn)
        op.ph = self.phase % NPH
        if self.seclog is not None:
            self.seclog[eng].append(self.section)
        self._track(op, self._bufs(reads), self._bufs(writes))
        op.idx = len(self.ops[eng])
        self.ops[eng].append(op)
        return op

    def dma(self, eng, out, in_, **kw):
        op = Op(eng, None, dma=True)
        rb = self._bufs([in_])
        wb = self._bufs([out])
        self._track(op, rb, wb)
        owner = wb[0] if wb else rb[0]
        if owner.sem is None:
            owner.sem = self.newsem("d_" + owner.name.replace(".", "_"))
        owner.cnt += 16
        op.tok = (owner.sem, owner.cnt)
        oap = out.ap if isinstance(out, V) else out
        iap = in_.ap if isinstance(in_, V) else in_
        sem = owner.sem
        op.fn = lambda e: e.dma_start(out=oap, in_=iap, **kw).then_inc(sem, 16)
        op.idx = len(self.ops[eng])
        self.ops[eng].append(op)
        return op

    def wait_dma_final(self, eng, v):
        for b in v.bufs:
            if b.writer is not None and b.writer.dma:
                self.final_waits.append((eng, b.writer.tok))

    @staticmethod
    def _a(x):
        return x.ap if isinstance(x, V) else x

    def matmul(self, out, lhsT, rhs, start=True, stop=True, **kw):
        o, l, r = self._a(out), self._a(lhsT), self._a(rhs)
        op = self.emit("pe", lambda e: e.matmul(o, l, r, start=start, stop=stop, **kw),
                       [lhsT, rhs], [out])
        self._pe_rowgroup(op, l)
        if self.seclog is not None and l.dtype == F32:
            self.seclog["pe"].append(self.section)
        return op

    def _pe_rowgroup(self, op, lhs_ap):
        rg = (lhs_ap.base_partition(), min(128, ((lhs_ap.shape[0] + 31) // 32) * 32))
        prev = getattr(self, "_last_pe", None)
        if prev is not None and prev[1] != rg:
            prev[0].signal = True
            op.deps.append(prev[0])
        self._last_pe = (op, rg)

    def transpose(self, out, in_, ident):
        o, i, d = self._a(out), self._a(in_), self._a(ident)
        op = self.emit("pe", lambda e: e.transpose(o, i, d), [in_, ident], [out])
        self._pe_rowgroup(op, i)
        return op

    def act(self, out, in_, func, bias=None, scale=1.0, accum_out=None, eng="act"):
        o, i = self._a(out), self._a(in_)
        b = self._a(bias) if bias is not None else None
        s = self._a(scale)
        acc = self._a(accum_out) if accum_out is not None else None
        kw = {}
        if b is not None:
            kw["bias"] = b
        if acc is not None:
            kw["accum_out"] = acc
        return self.emit("act", lambda e: e.activation(o, i, func, scale=s, **kw),
                         [in_, bias, scale], [out, accum_out])

    def tt(self, eng, out, in0, in1, op):
        o, a, b = self._a(out), self._a(in0), self._a(in1)
        return self.emit(eng, lambda e: e.tensor_tensor(o, a, b, op), [in0, in1], [out])

    def ts(self, eng, out, in0, s1, op0, s2=None, op1=None, accum_out=None):
        o, a = self._a(out), self._a(in0)
        x1, x2 = self._a(s1), self._a(s2)
        acc = self._a(accum_out) if accum_out is not None else None
        kw = {}
        if op1 is not None:
            kw["op1"] = op1
        if acc is not None:
            kw["accum_out"] = acc
        return self.emit(eng, lambda e: e.tensor_scalar(o, a, x1, x2, op0, **kw),
                         [in0, s1, s2], [out, accum_out])

    def stt(self, eng, out, in0, scalar, in1, op0, op1):
        o, a, s, b = self._a(out), self._a(in0), self._a(scalar), self._a(in1)
        return self.emit(eng, lambda e: e.scalar_tensor_tensor(o, a, s, b, op0, op1),
                         [in0, scalar, in1], [out])

    def copy(self, eng, out, in_):
        o, i = self._a(out), self._a(in_)
        if eng == "act":
            return self.emit("act", lambda e: e.copy(o, i), [in_], [out])
        return self.emit(eng, lambda e: e.tensor_copy(o, i), [in_], [out])

    def memset(self, eng, out, val):
        o = self._a(out)
        return self.emit(eng, lambda e: e.memset(o, val), [], [out])

    def scan(self, out, d0, d1, initial, op0, op1):
        o, a, b, i = self._a(out), self._a(d0), self._a(d1), self._a(initial)
        return self.emit("dve", lambda e: e.tensor_tensor_scan(o, a, b, i, op0, op1),
                         [d0, d1, initial], [out])

    def recip(self, out, in_):
        o, i = self._a(out), self._a(in_)
        return self.emit("dve", lambda e: e.reciprocal(o, i), [in_], [out])

    def generic(self, eng, fn, reads, writes):
        return self.emit(eng, fn, reads, writes)

    def finish(self):
        nc = self.nc
        esem = {(e, k): self.newsem(f"s_{e}{k}") for e in COMPUTE for k in range(NPH)}
        for e in COMPUTE:
            n = [0] * NPH
            for op in self.ops[e]:
                if op.signal and not op.dma:
                    n[op.ph] += 1
                    op.signum = n[op.ph]
        engobj = {"pe": "tensor", "act": "scalar", "dve": "vector", "pool": "gpsimd", "sp": "sync"}
        stats = {}
        with nc.Block() as block:
            for ename in ("sp", "pool", "act", "dve", "pe"):
                ops = self.ops[ename]
                finals = [t for (e, t) in self.final_waits if e == ename]
                if not ops and not finals:
                    continue

                def body(eng, ops=ops, ename=ename, finals=finals):
                    seen = {}
                    nw = 0
                    for op in ops:
                        need = {}
                        for A in op.deps:
                            k = esem[(A.eng, A.ph)]
                            if need.get(k, 0) < A.signum:
                                need[k] = A.signum
                        for sem, val in op.dwaits.values():
                            if need.get(sem, 0) < val:
                                need[sem] = val
                        for k, val in need.items():
                            if seen.get(k, 0) < val:
                                eng.wait_ge(k, val)
                                seen[k] = val
                                nw += 1
                        ins = op.fn(eng)
                        if op.signal and not op.dma:
                            ins.then_inc(esem[(ename, op.ph)], 1)
                    for sem, val in finals:
                        eng.wait_ge(sem, val)
                    stats[ename] = (len(ops), nw)

                getattr(block, engobj[ename])(body)
        self.stats = stats
        return stats

    def close(self):
        self.es.close()


D = 1024; KC = 8; DFF = 2816; FC = 22; T = 512; MEM = 256
DDT = BF16
EPS = 1e-6
L_CONV0, L_HG0, L_RW0 = 0, 6, 18
NHG = 6; NRW = 6
HORD = [0, 2, 4, 1, 3, 5]

PL = {}
_o = 0
for _n, _w in [("ffn1_norm", 8), ("mix_norm", 8), ("xattn_norm", 8), ("ffn2_norm", 8), ("mem_norm", 8),
               ("cw0", 2), ("cw1", 2), ("cw2", 2), ("cb", 2), ("lbl", 3), ("hgn", 3), ("mu", 11),
               ("w0", 3), ("a0", 3), ("k_k", 3), ("k_a", 3), ("r_k", 3), ("ln_w", 3), ("ln_b", 3),
               ("omu", 11), ("oka", 3), ("lb", 3), ("olb", 3)]:
    PL[_n] = (_o, _w); _o += _w
NPL = _o
CL = {}
_o = 0
for _n, _w in [("ident", 128), ("blk64", 128), ("ones", 128), ("m32i", 32), ("m64is", 128), ("m64s", 64),
               ("m64l", 64), ("r32", 512), ("r64", 512), ("id64", 64)]:
    CL[_n] = (_o, _w); _o += _w
NCL = _o


def make_consts():
    c = np.zeros((128, NCL), np.float32)
    def put(n, a):
        o, w = CL[n]; c[:a.shape[0], o:o + w] = a
    put("ident", np.eye(128))
    b = np.zeros((128, 128)); b[:64, :64] = 1; b[64:, 64:] = 1
    put("blk64", b)
    put("ones", np.ones((128, 128)))
    s32 = np.arange(32)
    put("m32i", (s32[:, None] <= s32[None, :]).astype(np.float32))
    s64 = np.arange(64)
    strict = (s64[:, None] < s64[None, :]).astype(np.float32)
    incl = (s64[:, None] <= s64[None, :]).astype(np.float32)
    put("m64is", np.concatenate([strict, incl], 1))
    put("m64s", strict)
    put("m64l", strict.T.copy())
    r = np.ones((128, 512)); r[:, ::32] = 0; put("r32", r)
    r = np.ones((128, 512)); r[:, ::64] = 0; put("r64", r)
    put("id64", np.eye(64))
    return c


def fm(v):
    v = np.asarray(v, np.float32).reshape(-1)
    return np.ascontiguousarray(v.reshape(-1, 128).T)


def build(S, L):
    NT = S // T
    nc = bass.Bass("TRN2", target_bir_lowering=False)
    P = Prog(nc)
    if SECLOG is not None:
        P.seclog = {e: [] for e in ("pe", "act", "dve", "pool", "sp")}
    ein = lambda n, sh: nc.dram_tensor(n, list(sh), F32, kind="ExternalInput").ap()
    x_d = ein("x", [S, D]); mem_d = ein("mem", [MEM, D])
    par_d = ein("params", [128, L * NPL + 8]); con_d = ein("consts", [128, NCL])
    wnames = [("ffn1_w_in", D, 2 * DFF), ("ffn1_w_out", DFF, D), ("w_mix_in", D, 3712), ("w_mix_out", D, D),
              ("xattn_wq", D, D), ("xattn_wkv", D, 2 * D), ("xattn_wo", D, D),
              ("ffn2_w_in", D, 2 * DFF), ("ffn2_w_out", DFF, D)]
    wf = {n: ein(n, [L, a, b]) for n, a, b in wnames}
    w2_d = ein("rwkv_w2", [L, 64, 384]); a2_d = ein("rwkv_a2", [L, 64, 384]); g2_d = ein("rwkv_g2", [L, 128, 384])
    out_d = P.dram("out", [S, D], F32, kind="ExternalOutput")
    wb = {}
    for n, a, b in wnames:
        t = nc.dram_tensor(n + "_b", [L, a, b], BF16, kind="Internal").ap()
        wb[n] = [V(t[l], [Buf(f"w_{n}{l}")]) for l in range(L)]
    mk_d = [P.dram(f"mk_d{l}", [128, 8 * 256], BF16) for l in range(L)]
    mv_d = [P.dram(f"mv_d{l}", [128, 2 * 1024], BF16) for l in range(L)]

    par = P.tile("par", [128, L * NPL + 8], F32)
    con = P.tile("con", [128, NCL], F32)
    cb = P.tile("cb", [128, 128 * 3], BF16)
    ident_b = cb[:, 0:128]; blk_b = cb[:, 128:256]; ones_b = cb[:, 256:384]
    ident_f = con[:, CL["ident"][0]:CL["ident"][0] + 128]
    def cst(n, rows=128):
        o, w = CL[n]; return con[0:rows, o:o + w]
    def pc(l, n, i=None):
        o, w = PL[n]; o += l * NPL
        return par[:, o:o + w] if i is None else par[:, o + i:o + i + 1]
    wa2 = P.tile("wa2", [128, L, 384], BF16)
    g2b = P.tile("g2b", [128, L, 384], BF16)
    def mtile(name, shape, dt, n):
        t, bufs = P.sbuf(name, shape, dt, nsub=n)
        return V(t[tuple(slice(None) for _ in shape)], bufs)
    def chv(v, c):
        return V(v.ap[:, c, :], [v.bufs[c]])
    hT = mtile("hT", [128, KC, T], F32, KC)
    xn = P.tile("xn", [128, KC, T], BF16)
    ARENA = 97 * 1024
    arena_t, _ = P.sbuf("arena", [128, ARENA // 2], BF16)
    gens = {}
    def carve(gen, name, shape, dt):
        g = gens.setdefault(gen, {"off": 0, "vs": []})
        esz = 4 if dt == F32 else 2
        n = 1
        for d in shape[1:]:
            n *= d
        nbytes = (n * esz + 63) // 64 * 64
        o = g["off"]; g["off"] += nbytes
        assert g["off"] <= ARENA, (gen, name, g["off"])
        ap = arena_t[0:shape[0], o // 2:(o + n * esz) // 2]
        if dt == F32:
            ap = ap.bitcast(F32)
        if len(shape) > 2:
            names = " ".join(f"d{i}" for i in range(1, len(shape)))
            ap = ap.rearrange(f"p ({names}) -> p {names}", **{f"d{i}": shape[i] for i in range(1, len(shape) - 1)})
        v = V(ap, [Buf(name)])
        g["vs"].append(v)
        return v
    def handoff(old, new):
        ops = []
        for v in gens[old]["vs"]:
            for b in v.bufs:
                if b.writer is not None:
                    ops.append(b.writer)
                ops.extend(b.readers)
        for v in gens[new]["vs"]:
            for b in v.bufs:
                b.readers.extend(ops)
    def enter(gen):
        for g_ in list(gens):
            if g_ != gen:
                handoff(g_, gen)
    hid = carve("ffn", "hid", [128, FC, T], BF16)
    mkT = carve("xa", "mkT", [128, 8, 256], BF16)
    mvt = carve("xa", "mvt", [128, 2, 1024], BF16)
    exs = [carve("xa", f"exs{i}", [128, 2, T], BF16) for i in range(4)]
    rden = [carve("xa", f"rden{i}", [128, T], F32) for i in range(4)]
    RING = 5
    ring = [P.tile(f"ring{i}", [128, 2048], BF16) for i in range(RING)]
    rstd = P.tile("rstd", [128, T], F32)
    xs = P.tile("xs", [128, 4, D], F32)
    sg = [P.tile(f"sg{i}", [128, T], F32) for i in range(2)]
    ymix = P.tile("ymix", [128, KC, T], BF16)
    qT = mtile("qT", [128, KC, T], BF16, KC)
    Sh = [P.tile(f"Sh{l}", [128, 3, 64], F32) for l in range(L)]
    Sr = [P.tile(f"Sr{l}", [128, 3, 64], F32) for l in range(L)]
    Shb = P.tile("Shb", [128, 3, 64], BF16)
    Srb = P.tile("Srb", [128, 3, 64], BF16)
    cz = [P.tile(f"cz{l}", [128, 2, 2], F32) for l in range(L)]
    crw = [P.tile(f"crw{l}", [128, 11], F32) for l in range(L)]
    PB = [P.psum(f"pb{i}", [128, 512]) for i in range(8)]
    st = {"ring": 0, "pj": 0, "ms": 0}
    def pj():
        st["pj"] = (st["pj"] + 1) % 3
        return PB[st["pj"]]
    def ms():
        st["ms"] = (st["ms"] + 1) % 4
        return PB[3 + st["ms"]]
    def bfview(bank):
        return V(bank.ap.bitcast(BF16), bank.bufs)

    P.dma("sp", par, par_d)
    P.dma("sp", con, con_d)
    for l in range(L):
        P.dma("pool", wb["xattn_wkv"][l], wf["xattn_wkv"][l])
    for l in range(L):
        for n in ("ffn1_w_in", "ffn1_w_out", "w_mix_in", "w_mix_out", "xattn_wq", "xattn_wo", "ffn2_w_in", "ffn2_w_out"):
            P.dma("pool", wb[n][l], wf[n][l])
    P.dma("pool", wa2[0:64], w2_d.rearrange("l k c -> k l c"))
    P.dma("pool", wa2[64:128], a2_d.rearrange("l k c -> k l c"))
    P.dma("pool", g2b, g2_d.rearrange("l k c -> k l c"))
    P.copy("act", cb[:, 0:128], ident_f)
    P.copy("act", cb[:, 128:256], cst("blk64"))
    P.copy("act", cb[:, 256:384], cst("ones"))
    for l in range(L):
        P.memset("pool", Sh[l], 0.0); P.memset("pool", Sr[l], 0.0)
        P.memset("pool", cz[l], 0.0); P.memset("pool", crw[l], 0.0)
    for l in range(L):
        P.ts("dve", pc(l, "omu"), pc(l, "mu"), -1.0, ALU.mult, 1.0, ALU.add)
        P.ts("dve", pc(l, "oka"), pc(l, "k_a"), -1.0, ALU.mult, 1.0, ALU.add)
    ex_l = P.tile("ex_l", [128, L, 3], F32)
    sm = P.tile("sm", [128, 3], F32)
    for l in range(L):
        P.act(ex_l[:, l, :], pc(l, "lbl"), AF.Exp)
    P.copy("dve", sm, ex_l[:, 0, :])
    for l in range(1, L):
        P.tt("dve", sm, sm, ex_l[:, l, :], ALU.add)
    P.recip(sm, sm)
    for l in range(L):
        P.tt("dve", ex_l[:, l, :], ex_l[:, l, :], sm, ALU.mult)
    P.memset("dve", pc(0, "lb"), 0.0)
    for l in range(1, L):
        P.tt("dve", pc(l, "lb"), pc(l - 1, "lb"), ex_l[:, l, :], ALU.add)
    for l in range(L):
        P.ts("dve", pc(l, "lb"), pc(l, "lb"), 0.0, ALU.max)
        P.ts("dve", pc(l, "olb"), pc(l, "lb"), -1.0, ALU.mult, 1.0, ALU.add)

    def ring_slot():
        st["ring"] = (st["ring"] + 1) % RING
        return ring[st["ring"]]

    def proj_cols(xin, W, groups, handler, ntok=T):
        for (c0, ncols) in groups:
            slot = ring_slot()
            sv = slot[:, 0:KC * ncols].re("p (k n) -> p k n", k=KC)
            P.dma("sp", sv, W[:, c0:c0 + ncols].re("(k p) n -> p k n", p=128))
            for j in range(ncols // 128):
                ps = pj()
                for k in range(KC):
                    P.matmul(ps[:, 0:ntok], sv[:, k, j * 128:(j + 1) * 128], xin[:, k, 0:ntok],
                             start=(k == 0), stop=(k == KC - 1))
                handler(c0 + j * 128, ps[:, 0:ntok])

    def proj_rows(rhs_list, W, handler):
        nK = len(rhs_list)
        for g in range(0, nK, 2):
            n = min(2, nK - g)
            slot = ring_slot()
            sv = slot[:, 0:n * 1024].re("p (k n) -> p k n", k=n)
            P.dma("sp", sv, W[g * 128:(g + n) * 128, :].re("(k p) n -> p k n", p=128))
            for kk in range(n):
                k = g + kk
                for dc in range(KC):
                    P.matmul(PB[dc], sv[:, kk, dc * 128:(dc + 1) * 128], rhs_list[k],
                             start=(k == 0), stop=(k == nK - 1))
        for dc in [KC - 1] + list(range(KC - 1)):
            handler(dc, PB[dc])

    def norm_partial(c):
        n = st.get("ncnt", 0)
        st["nps"] = PB[7]
        P.act(chv(qT, c), chv(hT, c), AF.Square)
        P.matmul(st["nps"], ones_b, chv(qT, c), start=(n == 0), stop=(n == KC - 1))
        st["ncnt"] = (n + 1) % KC
        if n == KC - 1:
            st["nready"] = True

    def rmsnorm(gcols, out):
        if not st.get("nready"):
            for c in range(KC):
                norm_partial(c)
        st["nready"] = False
        P.act(rstd, st["nps"], AF.Sqrt, scale=1.0 / D, bias=EPS)
        P.recip(rstd, rstd)
        for c in range(KC):
            P.stt("dve", out[:, c, :], chv(hT, c), gcols[:, c:c + 1], rstd, ALU.mult, ALU.mult)

    def groups(c0, n):
        g = []
        while n > 0:
            w = min(256, n); g.append((c0, w)); c0 += w; n -= w
        return g

    def ffn(l, which):
        P.section = which
        enter("ffn")
        rmsnorm(pc(l, which + "_norm"), xn)
        W = wb[which + "_w_in"][l]
        for g in range(FC // 2):
            def hg(c0, ps, g=g):
                j = (c0 - g * 256) // 128
                P.act(sg[j], ps, AF.Silu)
            proj_cols(xn, W, [(g * 256, 256)], hg)
            def hu(c0, ps, g=g):
                j = (c0 - DFF - g * 256) // 128
                P.tt("dve", hid[:, 2 * g + j, :], sg[j], ps, ALU.mult)
            proj_cols(xn, W, [(DFF + g * 256, 256)], hu)
        def ho(dc, ps):
            P.stt("dve", chv(hT, dc), ps, 0.5, chv(hT, dc), ALU.mult, ALU.add)
            norm_partial(dc)
        P.section = which + "_out"
        proj_rows([hid[:, k, :] for k in range(FC)], wb[which + "_w_out"][l], ho)

    def xattn(l):
        P.section = "xa"
        enter("xa")
        rmsnorm(pc(l, "xattn_norm"), xn)
        P.dma("sp", mkT, mk_d[l].re("p (k m) -> p k m", k=8))
        P.dma("sp", mvt, mv_d[l].re("p (k m) -> p k m", k=2))
        def hq(c0, ps):
            P.copy("act", qT[:, c0 // 128, :], ps)
        proj_cols(xn, wb["xattn_wq"][l], groups(0, D), hq)
        oT = xn
        for hh in range(4):
            for mb in range(2):
                ps = ms()
                for kk in range(2):
                    P.matmul(ps, mkT[:, 2 * hh + kk, mb * 128:(mb + 1) * 128], qT[:, 2 * hh + kk, :],
                             start=(kk == 0), stop=(kk == 1))
                P.act(exs[hh][:, mb, :], ps, AF.Exp, scale=1.0 / 16.0)
        for hh in range(4):
            ps = ms()
            for mb in range(2):
                P.matmul(ps, ones_b, exs[hh][:, mb, :], start=(mb == 0), stop=(mb == 1))
            P.recip(rden[hh], ps)
        for hh in range(4):
            for kk in range(2):
                ps = ms()
                for mb in range(2):
                    P.matmul(ps, mvt[:, mb, (2 * hh + kk) * 128:(2 * hh + kk + 1) * 128], exs[hh][:, mb, :],
                             start=(mb == 0), stop=(mb == 1))
                P.tt("dve", oT[:, 2 * hh + kk, :], ps, rden[hh], ALU.mult)
        def ho(dc, ps):
            P.tt("dve", chv(hT, dc), ps, chv(hT, dc), ALU.add)
            norm_partial(dc)
        proj_rows([oT[:, k, :] for k in range(KC)], wb["xattn_wo"][l], ho)

    cg = [carve("conv", f"cg{i}", [128, T], F32) for i in range(2)]
    zb = [carve("conv", f"zb{i}", [128, T + 2], F32) for i in range(2)]
    zc = [carve("conv", f"zc{i}", [128, T], F32) for i in range(2)]
    hq_ = carve("hg", "hq", [128, 3, T], BF16)
    hlf = carve("hg", "hlf", [128, 3, T], F32)
    hsg = carve("hg", "hsg", [128, 3, T], BF16)
    hgs = carve("hg", "hgs", [128, 3, T], BF16)
    hvT = carve("hg", "hvT", [128, 3, T], BF16)
    hqt = carve("hg", "hqt", [128, 3, T], BF16)
    hkt = carve("hg", "hkt", [128, 3, T], BF16)
    hgam = carve("hg", "hgam", [128, 3, 8], F32)
    hvtm = [carve("hg", f"hvtm{i}", [64, 384], BF16) for i in range(2)]
    hktm = [carve("hg", f"hktm{i}", [64, 384], BF16) for i in range(2)]
    hsc = carve("hg", "hsc", [64, 6, T], BF16)
    hAs = [carve("hg", f"hA{i}", [128, T], F32) for i in range(2)]; hBs = [carve("hg", f"hB{i}", [128, T], F32) for i in range(2)]
    hCs = [carve("hg", f"hC{i}", [128, T], F32) for i in range(2)]; hDs = [carve("hg", f"hD{i}", [128, T], F32) for i in range(2)]
    hA, hB, hC, hD = hAs[0], hBs[0], hCs[0], hDs[0]
    htS = carve("hg", "htS", [128, 3, 64], F32)
    of = carve("hg", "of", [128, T], F32)
    osq = carve("hg", "osq", [128, T], BF16)

    def conv_handlers(l):
        def h(ci, ps):
            i = ci % 2
            if ci in (2, 3):
                P.copy("act", cg[i], ps)
            elif ci in (4, 5):
                P.copy("pool", zb[i][:, 0:2], cz[l][:, i, :])
                P.tt("dve", zb[i][:, 2:T + 2], cg[i], ps, ALU.mult)
                P.copy("pool", cz[l][:, i, :], zb[i][:, T:T + 2])
                P.ts("dve", zc[i], zb[i][:, 2:T + 2], pc(l, "cw2", i), ALU.mult, pc(l, "cb", i), ALU.add)
                P.stt("dve", zc[i], zb[i][:, 1:T + 1], pc(l, "cw1", i), zc[i], ALU.mult, ALU.add)
                P.stt("dve", zc[i], zb[i][:, 0:T], pc(l, "cw0", i), zc[i], ALU.mult, ALU.add)
            else:
                P.tt("dve", ymix[:, i, :], ps, zc[i], ALU.mult)
        return h

    def hgrn_handler(l):
        def h(ci, ps):
            k = ci - L_HG0; i = k % 3; kind = k // 3
            if kind == 0:
                P.act(hq_[:, i, :], ps, AF.Silu)
            elif kind == 1:
                hA_, hB_, hC_, hD_ = hAs[i % 2], hBs[i % 2], hCs[i % 2], hDs[i % 2]
                P.ts("dve", hA_, ps, -80.0, ALU.max)
                P.act(hB_, hA_, AF.Exp, scale=-1.0)
                P.act(hC_, hB_, AF.Ln, bias=1.0)
                P.act(hD_, hB_, AF.Ln, scale=pc(l, "lb", i), bias=1.0)
                P.tt("dve", hlf[:, i, :], hD_, hC_, ALU.subtract)
                P.act(hsg[:, i, :], hA_, AF.Sigmoid, scale=-1.0)
            elif kind == 2:
                P.copy("act", hvT[:, i, :], ps)
            else:
                P.act(hgs[:, i, :], ps, AF.Silu)
        return h

    def tm_transposes(srcs, dsts, c, C):
        for (src, dst) in zip(srcs, dsts):
            bank = ms(); bv = bfview(bank)
            for i in range(3):
                P.transpose(bv[0:C, i * 128:(i + 1) * 128], src[:, i, c * C:(c + 1) * C], ident_b)
            P.copy("act", dst, bv[0:C, 0:384])

    def hgrn_core(l):
        P.section = "hg_core"
        C = 64; NCH = T // C
        for i in range(3):
            P.scan(hlf[:, i, :], cst("r64"), hlf[:, i, :], 0.0, ALU.mult, ALU.add)
            hA_, hB_ = hAs[i % 2], hBs[i % 2]
            P.act(hA_, hlf[:, i, :], AF.Exp)
            P.tt("dve", hqt[:, i, :], hq_[:, i, :], hA_, ALU.mult)
            P.copy("pool", hgam[:, i, :], hA_[:, C - 1::C])
            P.act(hB_, hlf[:, i, :], AF.Exp, scale=-1.0)
            P.stt("dve", hkt[:, i, :], hsg[:, i, :], pc(l, "olb", i), hB_, ALU.mult, ALU.mult)
        if FLAGS.get("hg_stage", 9) < 2:
            P.memset("pool", ymix[:, 2:5, :], 0.0); return
        m64 = cst("m64is", 64)[:, 64:128]
        mb = V(m64.ap.unsqueeze(1).to_broadcast([C, NCH, C]), m64.bufs)
        for h in HORD:
            i = h // 2; r0 = (h % 2) * 64
            ps = ms()
            for c in range(NCH):
                P.matmul(ps[0:C, c * C:(c + 1) * C], hkt[r0:r0 + 64, i, c * C:(c + 1) * C],
                         hqt[r0:r0 + 64, i, c * C:(c + 1) * C])
            P.tt("dve", hsc[:, h, :].re("p (c t) -> p c t", t=C), ps[0:C, :].re("p (c t) -> p c t", t=C), mb, ALU.mult)
        if FLAGS.get("hg_stage", 9) < 3:
            P.memset("pool", ymix[:, 2:5, :], 0.0); return
        P.copy("act", Shb, Sh[l])
        psO = [PB[0], PB[1], PB[2]]

        def hg_pre(c):
            tm_transposes((hvT, hkt), (hvtm[c % 2], hktm[c % 2]), c, C)
            yield

        def hg_chain(c):
            vt = hvtm[c % 2]; kt = hktm[c % 2]
            for part in ("even", "odd1", "odd2"):
                for h in ((0, 2, 4) if part == "even" else (1, 3, 5)):
                    i = h // 2; r0 = (h % 2) * 64
                    o = psO[i][r0:r0 + 64, c * C:(c + 1) * C]
                    if part != "odd2":
                        P.matmul(o, Shb[r0:r0 + 64, i, :], hqt[r0:r0 + 64, i, c * C:(c + 1) * C], start=True, stop=False)
                    if part != "odd1":
                        P.matmul(o, vt[:, h * 64:(h + 1) * 64], hsc[:, h, c * C:(c + 1) * C], start=False, stop=True)
            psD = ms()
            for h in range(NHG):
                i = h // 2; r0 = (h % 2) * 64
                P.matmul(psD[r0:r0 + 64, i * 64:(i + 1) * 64], kt[:, h * 64:(h + 1) * 64], vt[:, h * 64:(h + 1) * 64])
            P.tt("dve", htS, Sh[l], psD[:, 0:192].re("p (i e) -> p i e", i=3), ALU.add)
            g = hgam[:, :, c:c + 1]
            gb = V(g.ap.to_broadcast([128, 3, 64]), g.bufs)
            P.tt("dve", Shb, htS, gb, ALU.mult)
            P.tt("dve", Sh[l], htS, gb, ALU.mult)
            yield

        def interleave(gs):
            gs = list(gs)
            while gs:
                for g_ in list(gs):
                    try:
                        next(g_)
                    except StopIteration:
                        gs.remove(g_)

        interleave([hg_pre(0)])
        for c in range(NCH):
            interleave(([hg_pre(c + 1)] if c + 1 < NCH else []) + [hg_chain(c)])
        if FLAGS.get("hg_stage", 9) < 4:
            P.memset("pool", ymix[:, 2:5, :], 0.0); return
        for i in range(3):
            P.act(osq, psO[i], AF.Square)
            P.copy("dve", of, psO[i])
            ps = ms()
            P.matmul(ps, blk_b, osq)
            P.act(hA, ps, AF.Sqrt, scale=1.0 / 64.0, bias=EPS)
            P.recip(hA, hA)
            P.tt("dve", hB, of, hA, ALU.mult)
            P.stt("dve", ymix[:, 2 + i, :], hB, pc(l, "hgn", i), hgs[:, i, :], ALU.mult, ALU.mult)

    praw = [carve("rw", f"praw{i}", [128, T + 1], F32) for i in range(2)]
    rr = carve("rw", "rr", [128, 3, T], BF16); kr = carve("rw", "kr", [128, 3, T], F32); vr = carve("rw", "vr", [128, 3, T], BF16)
    wab = carve("rw", "wab", [128, T], BF16); gsb = carve("rw", "gsb", [128, T], BF16)
    lw = carve("rw", "lw", [128, T], F32); aicl = carve("rw", "aicl", [128, T], F32)
    gg = carve("rw", "gg", [128, 3, T], BF16); bon = carve("rw", "bon", [128, 3, T], BF16)
    kkn = carve("rw", "kkn", [128, T], F32); kmod = carve("rw", "kmod", [128, T], F32)
    bcs = carve("rw", "bcs", [128, T], F32)
    AR = carve("rw", "AR", [128, 3, 8, 2, 64], BF16)
    KT = carve("rw", "KT", [128, 3, T], BF16); BT = carve("rw", "BT", [128, 3, T], BF16); VT = carve("rw", "VT", [128, 3, T], BF16)
    rgam = carve("rw", "rgam", [128, 3, 8], F32)
    NS = 4
    ktm = [carve("rw", f"ktm{i}", [64, 384], BF16) for i in range(NS)]
    btm = [carve("rw", f"btm{i}", [64, 384], BF16) for i in range(NS)]
    vtm = [carve("rw", f"vtm{i}", [64, 384], BF16) for i in range(NS)]
    SKs = [carve("rw", f"SK{i}", [64, 6, 128], BF16) for i in range(NS)]
    SBs = [carve("rw", f"SBr{i}", [64, 6, 64], BF16) for i in range(NS)]
    Rs = [carve("rw", f"Rf{i}", [64, 6, 64], DDT) for i in range(NS)]
    Mfs = [[carve("rw", f"Mf{m}{i}", [64, 6, 64], DDT) for i in range(2)] for m in range(2)]
    Mtfs = [[carve("rw", f"Mtf{m}{i}", [64, 6, 64], DDT) for i in range(2)] for m in range(2)]
    P1f = carve("rw", "P1f", [64, 384], DDT); Ub = carve("rw", "Ub", [64, 384], BF16)
    sqb = carve("rw", "sqb", [128, T], BF16)
    tA = carve("rw", "tA", [128, T], F32); tB = carve("rw", "tB", [128, T], F32)
    tC = carve("rw", "tC", [128, T], F32); tD = carve("rw", "tD", [128, T], F32)
    tS = carve("rw", "tS", [128, 3, 64], F32)
    pT1 = carve("rw", "pT1", [128, T], F32); pT2 = carve("rw", "pT2", [128, T], F32)
    sqb2 = carve("rw", "sqb2", [128, T], BF16)

    def rwkv_handler(l):
        def h(ci, ps):
            j = ci - L_RW0
            pr = praw[j % 2]
            P.copy("act", pr[:, 1:T + 1], ps)
            P.copy("pool", pr[:, 0:1], crw[l][:, j:j + 1])
            P.copy("pool", crw[l][:, j:j + 1], pr[:, T:T + 1])
            P.ts("dve", tA, pr[:, 1:T + 1], pc(l, "omu", j), ALU.mult)
            if j < 9:
                dst = (rr, kr, vr)[j // 3][:, j % 3, :]
            else:
                dst = tB
            P.stt("dve", dst, pr[:, 0:T], pc(l, "mu", j), tA, ALU.mult, ALU.add)
            if j == 9:
                P.act(wab[0:64, :], tB[0:64, :], AF.Tanh)
                P.copy("act", wab[64:128, :], tB[64:128, :])
            elif j == 10:
                P.act(gsb, tB, AF.Sigmoid)
        return h

    def rwkv_core(l):
        P.section = "rw_pre"
        C = 64; NCH = T // C
        for i in range(3):
            ps = ms(); P.matmul(ps, wa2[0:64, l, i * 128:(i + 1) * 128], wab[0:64, :])
            P.act(tA, ps, AF.Sigmoid, bias=pc(l, "w0", i))
            P.ts("dve", lw, tA, -0.606531, ALU.mult)
            ps = ms(); P.matmul(ps, wa2[64:128, l, i * 128:(i + 1) * 128], wab[64:128, :])
            P.act(aicl, ps, AF.Sigmoid, bias=pc(l, "a0", i))
            ps = ms(); P.matmul(ps, g2b[:, l, i * 128:(i + 1) * 128], gsb)
            P.copy("act", gg[:, i, :], ps)
            P.ts("dve", tB, kr[:, i, :], pc(l, "k_k", i), ALU.mult)
            P.act(sqb, tB, AF.Square)
            ps = ms(); P.matmul(ps, blk_b, sqb)
            P.act(tC, ps, AF.Sqrt)
            P.ts("dve", tC, tC, 1e-12, ALU.max)
            P.recip(tC, tC)
            P.tt("dve", kkn, tB, tC, ALU.mult)
            P.ts("dve", pT1, aicl, pc(l, "k_a", i), ALU.mult, pc(l, "oka", i), ALU.add)
            P.tt("dve", kmod, kr[:, i, :], pT1, ALU.mult)
            P.tt("dve", pT1, rr[:, i, :], kmod, ALU.mult)
            P.ts("dve", sqb2, pT1, pc(l, "r_k", i), ALU.mult)
            ps = ms(); P.matmul(ps, blk_b, sqb2)
            P.tt("dve", bon[:, i, :], ps, vr[:, i, :], ALU.mult)
            P.scan(bcs, cst("r64"), lw, 0.0, ALU.mult, ALU.add)
            P.act(tA, bcs, AF.Exp)
            P.tt("dve", AR[:, i, :, 1, :], rr[:, i, :].re("p (c t) -> p c t", t=C), tA.re("p (c t) -> p c t", t=C), ALU.mult)
            P.copy("pool", rgam[:, i, :], tA[:, C - 1::C])
            P.tt("dve", tD, bcs, lw, ALU.subtract)
            P.act(tD, tD, AF.Exp)
            P.stt("dve", AR[:, i, :, 0, :], kkn.re("p (c t) -> p c t", t=C), -1.0, tD.re("p (c t) -> p c t", t=C), ALU.mult, ALU.mult)
            P.act(tC, bcs, AF.Exp, scale=-1.0)
            P.tt("dve", KT[:, i, :], kmod, tC, ALU.mult)
            P.tt("dve", pT2, kkn, aicl, ALU.mult)
            P.tt("dve", BT[:, i, :], pT2, tC, ALU.mult)
            P.copy("act", VT[:, i, :], vr[:, i, :])
        P.copy("act", Srb, Sr[l])
        psY = [PB[0], PB[1], PB[2]]
        mis = cst("m64is", 64); msk_s = cst("m64s", 64); msk_l = cst("m64l", 64); id64 = cst("id64", 64)
        bc3 = lambda m, n: V(m.ap.unsqueeze(1).to_broadcast([64, n, m.ap.shape[1]]), m.bufs)
        def rw_pre(c):
            P.section = "rw_chain"
            sx = c % NS; m = c % 2
            kt_ = ktm[sx]; bt_ = btm[sx]; vt_ = vtm[sx]
            SK = SKs[sx]; SBr = SBs[sx]; Rf = Rs[sx]; Mf = Mfs[m]; Mtf = Mtfs[m]
            tm_transposes((KT, BT, VT), (kt_, bt_, vt_), c, C)
            yield
            X1 = [ms(), ms()]
            for h in HORD:
                i = h // 2; r0 = (h % 2) * 64
                P.matmul(X1[h // 4][0:64, (h % 4) * 128:(h % 4 + 1) * 128], KT[r0:r0 + 64, i, c * C:(c + 1) * C],
                         AR[r0:r0 + 64, i, c, :, :].re("p a t -> p (a t)"))
            P.tt("dve", SK[:, 0:4, :], X1[0][0:64, :].re("p (h n) -> p h n", h=4), bc3(mis, 4), ALU.mult)
            P.tt("dve", SK[:, 4:6, :], X1[1][0:64, 0:256].re("p (h n) -> p h n", h=2), bc3(mis, 2), ALU.mult)
            yield
            X2 = [ms(), ms()]
            for h in HORD:
                i = h // 2; r0 = (h % 2) * 64
                P.matmul(X2[h // 4][0:64, (h % 4) * 128:(h % 4 + 1) * 128], BT[r0:r0 + 64, i, c * C:(c + 1) * C],
                         AR[r0:r0 + 64, i, c, :, :].re("p a t -> p (a t)"))
            for (bk, h0, nh) in ((X2[0], 0, 4), (X2[1], 4, 2)):
                v4 = bk[0:64, 0:nh * 128].re("p (h n) -> p h n", h=nh)
                P.tt("dve", Mf[0][:, h0:h0 + nh, :], v4[:, :, 0:64], bc3(msk_s, nh), ALU.mult)
                P.tt("dve", SBr[:, h0:h0 + nh, :], v4[:, :, 64:128], bc3(mis[:, 64:128], nh), ALU.mult)
            X3 = ms()
            for h in HORD:
                i = h // 2; r0 = (h % 2) * 64
                P.matmul(X3[0:64, h * 64:(h + 1) * 64], AR[r0:r0 + 64, i, c, 0, :], BT[r0:r0 + 64, i, c * C:(c + 1) * C])
            P.tt("dve", Mtf[0], X3[0:64, 0:384].re("p (h n) -> p h n", h=6), bc3(msk_l, 6), ALU.mult)
            P.tt("dve", Rf, Mf[0], bc3(id64, 6), ALU.add)
            yield
            idb64 = ident_b[0:64, 0:64]
            def sq_mm(cur, want_m):
                pMt = ms()
                for h in range(NRW):
                    P.matmul(pMt[0:64, h * 64:(h + 1) * 64], Mf[cur][:, h, :], Mtf[cur][:, h, :])
                pM = None
                if want_m:
                    pM = ms()
                    for h in range(NRW):
                        P.matmul(pM[0:64, h * 64:(h + 1) * 64], Mtf[cur][:, h, :], Mf[cur][:, h, :])
                return pMt, pM
            def sq_ev(pMt, pM, nxt):
                P.copy("dve", Mtf[nxt], pMt[0:64, 0:384].re("p (h n) -> p h n", h=6))
                if pM is not None:
                    P.copy("act", Mf[nxt], pM[0:64, 0:384].re("p (h n) -> p h n", h=6))
            def r_mm(nxt):
                pR = ms()
                for h in range(NRW):
                    P.matmul(pR[0:64, h * 64:(h + 1) * 64], Mtf[nxt][:, h, :], Rf[:, h, :])
                return pR
            def r_ev(pR):
                P.tt("dve", Rf, Rf, pR[0:64, 0:384].re("p (h n) -> p h n", h=6), ALU.add)
            cur = 0
            pMt, pM = sq_mm(cur, True)
            sq_ev(pMt, pM, 1 - cur)
            yield
            for lev in range(5):
                nxt = 1 - cur
                pR = r_mm(nxt)
                if lev < 4:
                    pMt, pM = sq_mm(nxt, lev < 3)
                r_ev(pR)
                if lev < 4:
                    sq_ev(pMt, pM, cur)
                yield
                cur = nxt

        def rw_chain(c):
            P.section = "rw_chain2"
            sx = c % NS
            kt_ = ktm[sx]; bt_ = btm[sx]; vt_ = vtm[sx]
            SK = SKs[sx]; SBr = SBs[sx]; Rf = Rs[sx]
            pP = ms()
            for h in HORD:
                i = h // 2; r0 = (h % 2) * 64
                o = pP[0:64, h * 64:(h + 1) * 64]
                P.matmul(o, AR[r0:r0 + 64, i, c, 0, :], Srb[r0:r0 + 64, i, :], start=True, stop=False)
                P.matmul(o, SK[:, h, 0:64], vt_[:, h * 64:(h + 1) * 64], start=False, stop=True)
            P.copy("act", P1f, pP[0:64, 0:384])
            yield
            pU = ms()
            for h in range(NRW):
                P.matmul(pU[0:64, h * 64:(h + 1) * 64], Rf[:, h, :], P1f[:, h * 64:(h + 1) * 64])
            P.copy("dve", Ub, pU[0:64, 0:384])
            yield
            for part in ("even", "odd1", "odd2"):
                for h in ((0, 2, 4) if part == "even" else (1, 3, 5)):
                    i = h // 2; r0 = (h % 2) * 64
                    o = psY[i][r0:r0 + 64, c * C:(c + 1) * C]
                    if part != "odd2":
                        P.matmul(o, Srb[r0:r0 + 64, i, :], AR[r0:r0 + 64, i, c, 1, :], start=True, stop=False)
                    if part != "odd1":
                        P.matmul(o, Ub[:, h * 64:(h + 1) * 64], SBr[:, h, :], start=False, stop=False)
                        P.matmul(o, vt_[:, h * 64:(h + 1) * 64], SK[:, h, 64:128], start=False, stop=True)
            pD = ms()
            for h in range(NRW):
                i = h // 2; r0 = (h % 2) * 64
                o = pD[r0:r0 + 64, i * 64:(i + 1) * 64]
                P.matmul(o, bt_[:, h * 64:(h + 1) * 64], Ub[:, h * 64:(h + 1) * 64], start=True, stop=False)
                P.matmul(o, kt_[:, h * 64:(h + 1) * 64], vt_[:, h * 64:(h + 1) * 64], start=False, stop=True)
            P.tt("dve", tS, Sr[l], pD[:, 0:192].re("p (i e) -> p i e", i=3), ALU.add)
            g = rgam[:, :, c:c + 1]
            gb = V(g.ap.to_broadcast([128, 3, 64]), g.bufs)
            P.tt("dve", Srb, tS, gb, ALU.mult)
            P.tt("dve", Sr[l], tS, gb, ALU.mult)
            yield

        def seq(*gs):
            for g_ in gs:
                yield from g_

        def interleave(gs):
            gs = list(gs)
            while gs:
                for g_ in list(gs):
                    try:
                        next(g_)
                    except StopIteration:
                        gs.remove(g_)

        if FLAGS.get("rw_pipe", True):
            interleave([rw_pre(0), rw_pre(1)])
            for k in range(1, NCH // 2):
                interleave([rw_pre(2 * k), rw_pre(2 * k + 1), seq(rw_chain(2 * k - 2), rw_chain(2 * k - 1))])
            interleave([seq(rw_chain(NCH - 2), rw_chain(NCH - 1))])
        else:
            for c in range(NCH):
                interleave([seq(rw_pre(c), rw_chain(c))])
        P.section = "rw_post"
        for i in range(3):
            P.copy("dve", tA, psY[i])
            P.copy("act", sqb, tA)
            ps = ms(); P.matmul(ps, blk_b, sqb)
            P.stt("dve", tB, ps, -1.0 / 64.0, tA, ALU.mult, ALU.add)
            P.act(sqb, tB, AF.Square)
            ps = ms(); P.matmul(ps, blk_b, sqb)
            P.act(tC, ps, AF.Sqrt, scale=1.0 / 64.0, bias=64e-5)
            P.recip(tC, tC)
            P.tt("dve", tB, tB, tC, ALU.mult)
            P.ts("pool", pT1, tB, pc(l, "ln_w", i), ALU.mult, pc(l, "ln_b", i), ALU.add)
            P.tt("pool", pT1, pT1, bon[:, i, :], ALU.add)
            P.tt("pool", ymix[:, 5 + i, :], pT1, gg[:, i, :], ALU.mult)

    def mixer(l):
        P.section = "mix_in"
        rmsnorm(pc(l, "mix_norm"), xn)
        W = wb["w_mix_in"][l]
        enter("conv")
        ch = conv_handlers(l)
        proj_cols(xn, W, [(256, 256), (512, 256), (0, 256)], lambda c0, ps: ch(c0 // 128, ps))
        if FLAGS["hgrn"]:
            enter("hg")
            P.section = "hg_in"
            hh = hgrn_handler(l)
            proj_cols(xn, W, groups(L_HG0 * 128, 12 * 128), lambda c0, ps: hh(c0 // 128, ps))
            hgrn_core(l)
        else:
            P.memset("pool", ymix[:, 2:5, :], 0.0)
        if FLAGS["rwkv"]:
            enter("rw")
            P.section = "rw_in"
            rh = rwkv_handler(l)
            proj_cols(xn, W, groups(L_RW0 * 128, 11 * 128), lambda c0, ps: rh(c0 // 128, ps))
            rwkv_core(l)
        else:
            P.memset("pool", ymix[:, 5:8, :], 0.0)
        def ho(dc, ps):
            P.tt("dve", chv(hT, dc), ps, chv(hT, dc), ALU.add)
            norm_partial(dc)
        P.section = "mix_out"
        proj_rows([ymix[:, k, :] for k in range(KC)], wb["w_mix_out"][l], ho)

    memt = xs[:, 0:2, :]
    P.dma("sp", memt, mem_d.rearrange("(b p) d -> p b d", p=128))
    mss = P.tile("mss", [128, 2], F32)
    gens["pro"] = {"off": gens["xa"]["off"], "vs": []}
    msq = carve("pro", "msq", [128, D], BF16)
    for b in range(2):
        P.act(msq, memt[:, b, :], AF.Square, accum_out=mss[:, b:b + 1])
    P.act(mss, mss, AF.Sqrt, scale=1.0 / D, bias=EPS)
    P.recip(mss, mss)
    memn = carve("pro", "memn", [128, 2, D], F32)
    for b in range(2):
        P.ts("dve", memn[:, b, :], memt[:, b, :], mss[:, b:b + 1], ALU.mult)
    memT = carve("pro", "memT", [128, KC, MEM], F32)
    for k in range(KC):
        ps = ms()
        for b in range(2):
            P.transpose(ps[:, b * 128:(b + 1) * 128], memn[:, b, k * 128:(k + 1) * 128], ident_f)
        P.copy("act", memT[:, k, :], ps[:, 0:256])
    memg = qT[:, :, 0:MEM]
    mks = ymix[:, :, 0:MEM]
    for l in range(L):
        for k in range(KC):
            P.ts("dve", memg[:, k, :], memT[:, k, :], pc(l, "mem_norm", k), ALU.mult)
        def hk(c0, ps):
            P.copy("act", mks[:, c0 // 128, :], ps)
        proj_cols(memg, wb["xattn_wkv"][l], groups(0, D), hk, ntok=MEM)
        P.dma("sp", mk_d[l].re("p (k m) -> p k m", k=8), mks)
        for (c0, ncols) in groups(D, D):
            slot = ring_slot()
            sv = slot[:, 0:KC * ncols].re("p (k n) -> p k n", k=KC)
            P.dma("sp", sv, wb["xattn_wkv"][l][:, c0:c0 + ncols].re("(k p) n -> p k n", p=128))
            for b in range(2):
                ps = pj()
                for k in range(KC):
                    P.matmul(ps[:, 0:ncols], memg[:, k, b * 128:(b + 1) * 128], sv[:, k, :], start=(k == 0), stop=(k == KC - 1))
                P.copy("act", mvt[:, b, c0 - D:c0 - D + ncols], ps[:, 0:ncols])
        P.dma("sp", mv_d[l].re("p (k m) -> p k m", k=2), mvt)

    for t in range(NT):
        P.section = "io"
        P.dma("sp", xs, x_d[t * T:(t + 1) * T, :].rearrange("(s p) d -> p s d", p=128))
        for c in range(KC):
            ps = pj()
            for s in range(4):
                P.transpose(ps[:, s * 128:(s + 1) * 128], xs[:, s, c * 128:(c + 1) * 128], ident_f)
            P.copy("act" if c % 2 else "dve", chv(hT, c), ps)
            norm_partial(c)
        for l in range(L):
            P.phase += 1
            if FLAGS["ffn"]:
                ffn(l, "ffn1")
            if FLAGS["mix"]:
                mixer(l)
            if FLAGS["xa"]:
                xattn(l)
            if FLAGS["ffn"]:
                ffn(l, "ffn2")
        P.phase += 1
        P.section = "io"
        fo = L * NPL
        if not st.get("nready"):
            for c in range(KC):
                norm_partial(c)
        st["nready"] = False
        P.act(rstd, st["nps"], AF.Sqrt, scale=1.0 / D, bias=EPS)
        P.recip(rstd, rstd)
        for c in range(KC):
            P.stt("dve", hT[:, c, :], hT[:, c, :], par[:, fo + c:fo + c + 1], rstd, ALU.mult, ALU.mult)
        for s in range(4):
            for half in range(2):
                ps = pj()
                for cc in range(4):
                    c = half * 4 + cc
                    P.transpose(ps[:, cc * 128:(cc + 1) * 128], hT[:, c, s * 128:(s + 1) * 128], ident_f)
                P.copy("act" if half else "dve", xs[:, s, half * 512:(half + 1) * 512], ps)
        P.dma("sp", out_d[t * T:(t + 1) * T, :].re("(s p) d -> p s d", p=128), xs)
    P.wait_dma_final("sp", out_d)
    stats = P.finish()
    P.close()
    if SECLOG is not None:
        SECLOG.update(P.seclog)
    return nc, stats


SECLOG = None
FLAGS = {"ffn": True, "mix": True, "xa": True, "hgrn": True, "rwkv": True}


def kernel(**inp):
    x = np.asarray(inp["x"], np.float32)
    B, S, _ = x.shape
    L = inp["ffn1_norm"].shape[0]
    nc, stats = build(S, L)
    par = np.zeros((128, L * NPL + 8), np.float32)
    for l in range(L):
        def put(n, a):
            o, w = PL[n]; par[:, l * NPL + o:l * NPL + o + w] = a
        for n in ("ffn1_norm", "mix_norm", "xattn_norm", "ffn2_norm", "mem_norm"):
            put(n, fm(inp[n][l]))
        for k in range(3):
            put(f"cw{k}", fm(inp["conv_w"][l][k]))
        put("cb", fm(inp["conv_b"][l])); put("lbl", fm(inp["hgrn_lb_logits"][l])); put("hgn", fm(inp["hgrn_norm"][l]))
        put("mu", fm(inp["rwkv_mu"][l])); put("w0", fm(inp["rwkv_w0"][l])); put("a0", fm(inp["rwkv_a0"][l]))
        put("k_k", fm(inp["rwkv_k_k"][l])); put("k_a", fm(inp["rwkv_k_a"][l])); put("r_k", fm(inp["rwkv_r_k"][l]))
        put("ln_w", fm(inp["rwkv_ln_w"][l])); put("ln_b", fm(inp["rwkv_ln_b"][l]))
    par[:, L * NPL:L * NPL + 8] = fm(inp["final_norm"])
    con = make_consts()
    shared = {"params": par, "consts": con}
    for n in ("ffn1_w_in", "ffn1_w_out", "w_mix_in", "w_mix_out", "xattn_wq", "xattn_wkv", "xattn_wo",
              "ffn2_w_in", "ffn2_w_out", "rwkv_w2", "rwkv_a2", "rwkv_g2"):
        shared[n] = np.ascontiguousarray(np.asarray(inp[n], np.float32))
    mem = np.asarray(inp["mem"], np.float32)
    in_maps = []
    for b in range(B):
        m = dict(shared)
        m["x"] = np.ascontiguousarray(x[b]); m["mem"] = np.ascontiguousarray(mem[b])
        in_maps.append(m)
    res = run_bass_kernel_spmd(nc, in_maps, core_ids=list(range(B)))
    return np.stack([np.asarray(r["out"], np.float32) for r in res.results], 0)
```
